# Optimizing a Trainium2 kernel written in Bass

```python
import jax, jax.numpy as jnp
from jax import lax
import numpy as np

D_MODEL = 1024
BATCH = 8
SEQ = 4096
DEPTH = 2

GRID_W = 64
D_MIX = D_MODEL
GROUP_W = D_MIX // 4
HEAD_DIM = 64
CONV_A_WIDTH = 31
GQA_HEADS = GROUP_W // HEAD_DIM
GQA_KV_HEADS = 2
CHUNK = 128
SGU_GROUPS = GROUP_W // HEAD_DIM
SGU_GROUP_DIM = GROUP_W // SGU_GROUPS
MLA_HEADS = GROUP_W // HEAD_DIM
MLA_Q_LORA = 3 * D_MODEL // 16
MLA_KV_LORA = D_MODEL // 8
MLA_NOPE = 64
MLA_ROPE = 32
MLA_V = GROUP_W // MLA_HEADS
Q_BLOCK = 128
ROPE_THETA = 10000.0
D_FF = 2816
FFN_CONV_WIDTH = 3
DEEPNORM_ALPHA = (2 * DEPTH) ** 0.25
DEEPNORM_BETA = (8 * DEPTH) ** -0.25
LN_EPS = 1e-5
RMS_EPS = 1e-6

SPLIT_SIZES = (2 * GROUP_W,
               GQA_HEADS * HEAD_DIM,
               GQA_KV_HEADS * HEAD_DIM,
               GQA_KV_HEADS * HEAD_DIM,
               2 * GROUP_W,
               MLA_Q_LORA,
               MLA_KV_LORA,
               MLA_ROPE)
D_IN_PROJ = sum(SPLIT_SIZES)
SPLIT_POINTS = [sum(SPLIT_SIZES[:i + 1]) for i in range(len(SPLIT_SIZES) - 1)]

kernel_name = 'hymba_style_hybrid_encoder'


def layer_norm(x, g, b):
    xf = x.astype(jnp.float32)
    mu = jnp.mean(xf, axis=-1, keepdims=True)
    xc = xf - mu
    var = jnp.mean(xc * xc, axis=-1, keepdims=True)
    return (xc * lax.rsqrt(var + LN_EPS) * g + b).astype(x.dtype)


def rms_norm(x, g):
    xf = x.astype(jnp.float32)
    ms = jnp.mean(xf * xf, axis=-1, keepdims=True)
    return (xf * lax.rsqrt(ms + RMS_EPS) * g).astype(x.dtype)


def depthwise_conv(x, w, b):
    k = w.shape[0]
    c = x.shape[-1]
    y = lax.conv_general_dilated(x, w[:, None, :], window_strides=(1,),
                                 padding=[(k // 2, k // 2)],
                                 dimension_numbers=('NWC', 'WIO', 'NWC'),
                                 feature_group_count=c)
    return y + b


def rope_1d(x, pos):
    d = x.shape[-1]
    half = d // 2
    inv_freq = ROPE_THETA ** (-jnp.arange(half, dtype=jnp.float32) / half)
    ang = pos[:, None] * inv_freq[None, :]
    cos = jnp.cos(ang)[:, None, :]
    sin = jnp.sin(ang)[:, None, :]
    xf = x.astype(jnp.float32)
    x1, x2 = xf[..., :half], xf[..., half:]
    return jnp.concatenate([x1 * cos - x2 * sin, x2 * cos + x1 * sin], axis=-1).astype(x.dtype)


def rope_2d(x, row, col):
    h = x.shape[-1] // 2
    return jnp.concatenate([rope_1d(x[..., :h], row), rope_1d(x[..., h:], col)], axis=-1)


def blocked_attention(q, k, v, scale):
    bsz, seq, hk, g, dk = q.shape
    nb = seq // Q_BLOCK
    qb = q.reshape(bsz, nb, Q_BLOCK, hk, g, dk).transpose(1, 0, 2, 3, 4, 5)

    def one_block(qi):
        s = jnp.einsum('bqhgd,bshd->bhgqs', qi, k, preferred_element_type=jnp.float32) * scale
        p = jax.nn.softmax(s, axis=-1).astype(v.dtype)
        return jnp.einsum('bhgqs,bshd->bqhgd', p, v)

    out = lax.map(one_block, qb)
    return out.transpose(1, 0, 2, 3, 4, 5).reshape(bsz, seq, hk, g, v.shape[-1])


def setup_inputs(seed: int = 0) -> dict:
    key = jax.random.key(seed)
    ks = iter(jax.random.split(key, 32))

    def nrm(shape, scale):
        return scale * jax.random.normal(next(ks), shape, dtype=jnp.float32)

    L = DEPTH
    return {
        'x': nrm((BATCH, SEQ, D_MODEL), 1.0),
        'ln_in_g': 1.0 + nrm((D_MODEL,), 0.02),
        'ln_in_b': nrm((D_MODEL,), 0.02),
        'w_in': nrm((L, D_MODEL, D_IN_PROJ), D_MODEL ** -0.5),
        'conv_a_w': nrm((L, CONV_A_WIDTH, GROUP_W), CONV_A_WIDTH ** -0.5),
        'conv_a_b': nrm((L, GROUP_W), 0.02),
        'ln_a_g': 1.0 + nrm((L, GROUP_W), 0.02),
        'ln_a_b': nrm((L, GROUP_W), 0.02),
        'qk_norm_q': 1.0 + nrm((L, HEAD_DIM), 0.02),
        'qk_norm_k': 1.0 + nrm((L, HEAD_DIM), 0.02),
        'sgu_ln_g': 1.0 + nrm((L, GROUP_W), 0.02),
        'sgu_ln_b': nrm((L, GROUP_W), 0.02),
        'sgu_w': nrm((L, SGU_GROUPS, CHUNK, CHUNK), CHUNK ** -0.5),
        'sgu_b': 1.0 + nrm((L, SGU_GROUPS, CHUNK), 0.02),
        'mla_q_norm': 1.0 + nrm((L, MLA_Q_LORA), 0.02),
        'mla_w_uq': nrm((L, MLA_Q_LORA, MLA_HEADS * (MLA_NOPE + MLA_ROPE)), MLA_Q_LORA ** -0.5),
        'mla_kv_norm': 1.0 + nrm((L, MLA_KV_LORA), 0.02),
        'mla_w_ukv': nrm((L, MLA_KV_LORA, MLA_HEADS * (MLA_NOPE + MLA_V)), MLA_KV_LORA ** -0.5),
        'w_out': nrm((L, D_MIX, D_MODEL), DEEPNORM_BETA * D_MIX ** -0.5),
        'ln_mix_g': 1.0 + nrm((L, D_MODEL), 0.02),
        'ln_mix_b': nrm((L, D_MODEL), 0.02),
        'ffn_w_up': nrm((L, D_MODEL, 2 * D_FF), D_MODEL ** -0.5),
        'ffn_conv_w': nrm((L, FFN_CONV_WIDTH, 2 * D_FF), FFN_CONV_WIDTH ** -0.5),
        'ffn_conv_b': nrm((L, 2 * D_FF), 0.02),
        'ffn_w_down': nrm((L, D_FF, D_MODEL), DEEPNORM_BETA * D_FF ** -0.5),
        'ln_ffn_g': 1.0 + nrm((L, D_MODEL), 0.02),
        'ln_ffn_b': nrm((L, D_MODEL), 0.02),
    }


def reference(x, ln_in_g, ln_in_b, w_in, conv_a_w, conv_a_b, ln_a_g, ln_a_b,
              qk_norm_q, qk_norm_k, sgu_ln_g, sgu_ln_b, sgu_w, sgu_b,
              mla_q_norm, mla_w_uq, mla_kv_norm, mla_w_ukv, w_out, ln_mix_g, ln_mix_b,
              ffn_w_up, ffn_conv_w, ffn_conv_b, ffn_w_down, ln_ffn_g, ln_ffn_b):
    bsz, seq, _ = x.shape
    rows = seq // GRID_W
    row = jnp.repeat(jnp.arange(rows, dtype=jnp.float32), GRID_W)
    col = jnp.tile(jnp.arange(GRID_W, dtype=jnp.float32), rows)

    h = layer_norm(x, ln_in_g, ln_in_b)
    for l in range(DEPTH):
        proj = h @ w_in[l]
        a_in, b_q, b_k, b_v, c_in, d_cq, d_ckv, d_kr = jnp.split(proj, SPLIT_POINTS, axis=-1)

        a = a_in[..., :GROUP_W] * jax.nn.sigmoid(a_in[..., GROUP_W:])
        a = depthwise_conv(a, conv_a_w[l], conv_a_b[l])
        o_a = jax.nn.silu(layer_norm(a, ln_a_g[l], ln_a_b[l]))

        q = rms_norm(b_q.reshape(bsz, seq, GQA_HEADS, HEAD_DIM), qk_norm_q[l])
        k = rms_norm(b_k.reshape(bsz, seq, GQA_KV_HEADS, HEAD_DIM), qk_norm_k[l])
        v = b_v.reshape(bsz, seq, GQA_KV_HEADS, HEAD_DIM)
        q = rope_2d(q, row, col).reshape(bsz, seq, GQA_KV_HEADS, GQA_HEADS // GQA_KV_HEADS, HEAD_DIM)
        k = rope_2d(k, row, col)
        o_b = blocked_attention(q, k, v, HEAD_DIM ** -0.5).reshape(bsz, seq, GROUP_W)

        c = jax.nn.gelu(c_in)
        u, sv = c[..., :GROUP_W], c[..., GROUP_W:]
        sv = layer_norm(sv, sgu_ln_g[l], sgu_ln_b[l])
        sv = sv.reshape(bsz, seq // CHUNK, CHUNK, SGU_GROUPS, SGU_GROUP_DIM)
        sv = jnp.einsum('gpq,bnqgc->bnpgc', sgu_w[l], sv) + sgu_b[l].T[:, :, None]
        o_c = u * sv.reshape(bsz, seq, GROUP_W)

        qd = (rms_norm(d_cq, mla_q_norm[l]) @ mla_w_uq[l]).reshape(bsz, seq, MLA_HEADS, MLA_NOPE + MLA_ROPE)
        kvd = (rms_norm(d_ckv, mla_kv_norm[l]) @ mla_w_ukv[l]).reshape(bsz, seq, MLA_HEADS, MLA_NOPE + MLA_V)
        q_nope, q_rope = qd[..., :MLA_NOPE], qd[..., MLA_NOPE:]
        k_nope, v_d = kvd[..., :MLA_NOPE], kvd[..., MLA_NOPE:]
        k_rope = rope_2d(d_kr[:, :, None, :], row, col)
        q_full = jnp.concatenate([q_nope, rope_2d(q_rope, row, col)], axis=-1)
        k_full = jnp.concatenate([k_nope, jnp.broadcast_to(k_rope, (bsz, seq, MLA_HEADS, MLA_ROPE))], axis=-1)
        o_d = blocked_attention(q_full[:, :, :, None, :], k_full, v_d,
                                (MLA_NOPE + MLA_ROPE) ** -0.5).reshape(bsz, seq, GROUP_W)

        mix = jnp.concatenate([o_a, o_b, o_c, o_d], axis=-1) @ w_out[l]
        h = layer_norm(DEEPNORM_ALPHA * h + mix, ln_mix_g[l], ln_mix_b[l])

        up = depthwise_conv(h @ ffn_w_up[l], ffn_conv_w[l], ffn_conv_b[l])
        f = (jax.nn.silu(up[..., :D_FF]) * up[..., D_FF:]) @ ffn_w_down[l]
        h = layer_norm(DEEPNORM_ALPHA * h + f, ln_ffn_g[l], ln_ffn_b[l])
    return h
```

```python
import contextlib
import numpy as np
import concourse.bass as bass
import concourse.mybir as mybir
from concourse.bass_utils import run_bass_kernel_spmd

F32 = mybir.dt.float32
BF16 = mybir.dt.bfloat16
AF = mybir.ActivationFunctionType
ALU = mybir.AluOpType
AX = mybir.AxisListType

D = 1024
DEPTH = 2
GW = 256
HD = 64
DIN = 1888
DFF = 2816
NJ = DFF // 128
ALPHA = float((2 * DEPTH) ** 0.25)
LN_EPS = 1e-5
RMS_EPS = 1e-6
ROPE_THETA = 10000.0
GRID_W = 64

ENGS = ("pe", "act", "dve", "pool", "sp")
EPOCH = 4000


class Res:
    __slots__ = ("name", "last_w", "readers")

    def __init__(self, name="r"):
        self.name = name
        self.last_w = None
        self.readers = []


class Op:
    __slots__ = ("eng", "fn", "deps", "is_dma", "lane", "sig", "cnt", "barrier", "idx")

    def __init__(self, eng, fn, is_dma=False, lane=None):
        self.eng = eng
        self.fn = fn
        self.deps = []
        self.is_dma = is_dma
        self.lane = lane
        self.sig = is_dma
        self.cnt = None
        self.barrier = False
        self.idx = -1


class Sched:
    def __init__(self, nc):
        self.nc = nc
        self.ops = []
        self.last_op = {e: None for e in ENGS}
        self.last_lane = {}

    def _dep(self, op, reads, writes):
        deps = set()
        for r in reads:
            if r.last_w is not None:
                deps.add(r.last_w)
        for w in writes:
            if w.last_w is not None:
                deps.add(w.last_w)
            latest = {}
            for rd in w.readers:
                if rd.is_dma:
                    deps.add(rd)
                else:
                    cur = latest.get(rd.eng)
                    if cur is None or rd.idx > cur.idx:
                        latest[rd.eng] = rd
            for rd in latest.values():
                deps.add(rd)
        deps.discard(op)
        for d in deps:
            if d.eng == "pe" and op.eng == "pe" and not d.is_dma and not op.is_dma:
                continue
            op.deps.append(d)
            d.sig = True
        for r in reads:
            r.readers.append(op)
        for w in writes:
            w.last_w = op
            w.readers = []

    def op(self, eng, fn, reads=(), writes=()):
        o = Op(eng, fn)
        o.idx = len(self.ops)
        self.ops.append(o)
        self._dep(o, reads, writes)
        self.last_op[eng] = o
        return o

    def dma(self, queue, fn, lane, reads=(), writes=()):
        o = Op(queue, fn, is_dma=True, lane=lane)
        o.idx = len(self.ops)
        self.ops.append(o)
        self._dep(o, reads, writes)
        prev = self.last_lane.get(lane)
        if prev is not None and prev not in o.deps:
            o.deps.append(prev)
        self.last_lane[lane] = o
        return o

    def barrier(self):
        pend = list(self.last_lane.values())
        for e in ENGS:
            if self.last_op[e] is not None:
                self.last_op[e].sig = True
                pend.append(self.last_op[e])
        for e in ENGS:
            o = Op(e, None)
            o.barrier = True
            o.deps = list(pend)
            self.ops.append(o)

    def emit(self):
        nc = self.nc
        eng_cnt = {e: 0 for e in ENGS}
        lane_cnt = {}
        for o in self.ops:
            if o.barrier:
                continue
            if o.is_dma:
                lane_cnt[o.lane] = lane_cnt.get(o.lane, 0) + 1
                o.cnt = lane_cnt[o.lane]
            elif o.sig:
                eng_cnt[o.eng] += 1
                o.cnt = eng_cnt[o.eng]
        sems = {}
        with contextlib.ExitStack() as stack:
            for e in ENGS:
                n_ep = max((eng_cnt[e] + EPOCH - 1) // EPOCH, 1)
                for k in range(n_ep):
                    sems[(e, k)] = stack.enter_context(nc.semaphore(f"s_{e}_{k}"))
            for i, ln in enumerate(lane_cnt):
                sems[("lane", ln)] = stack.enter_context(nc.semaphore(f"l_{i}"))

            def semof(o):
                if o.is_dma:
                    return sems[("lane", o.lane)], 16 * o.cnt, ("lane", o.lane)
                k = (o.cnt - 1) // EPOCH
                return sems[(o.eng, k)], o.cnt - k * EPOCH, (o.eng, k)

            per_eng = {e: [] for e in ENGS}
            for o in self.ops:
                per_eng[o.eng].append(o)
            final_waits = [(sems[("lane", ln)], 16 * c) for ln, c in lane_cnt.items()]

            def run(e, eng):
                seen = {}
                for o in per_eng[e]:
                    for d in o.deps:
                        s, v, key = semof(d)
                        if seen.get(key, 0) >= v:
                            continue
                        seen[key] = v
                        eng.wait_ge(s, v)
                    if o.barrier:
                        continue
                    ins = o.fn(eng)
                    if o.is_dma:
                        s, v, key = semof(o)
                        ins.then_inc(s, 16)
                    elif o.sig:
                        s, v, key = semof(o)
                        ins.then_inc(s, 1)
                if e == "sp":
                    for s, v in final_waits:
                        eng.wait_ge(s, v)

            with nc.Block() as block:
                @block.tensor
                def _(eng):
                    run("pe", eng)

                @block.scalar
                def _(eng):
                    run("act", eng)

                @block.vector
                def _(eng):
                    run("dve", eng)

                @block.gpsimd
                def _(eng):
                    run("pool", eng)

                @block.sync
                def _(eng):
                    run("sp", eng)
        return eng_cnt, len(lane_cnt)


class Buf:
    __slots__ = ("ap", "r")

    def __init__(self, ap, r):
        self.ap = ap
        self.r = r


DT_SIZE = {F32: 4, BF16: 2}


class Builder:
    def __init__(self, SEQ, debug=False, stop_after=None):
        self.SEQ = SEQ
        self.NT = SEQ // 128
        self.NQB = SEQ // 512
        self.debug = debug
        self.stop_after = stop_after
        nc = bass.Bass("TRN2", target_bir_lowering=False)
        self.nc = nc
        self.S = Sched(nc)
        self._uid = 0

    def mm(self, out, lhsT, rhs, start, stop, reads, writes):
        self.S.op("pe", lambda e: e.matmul(out, lhsT=lhsT, rhs=rhs, start=start, stop=stop), reads, writes)

    def tr(self, out, in_, ident, reads, writes):
        self.S.op("pe", lambda e: e.transpose(out=out, in_=in_, identity=ident), reads, writes)

    def actf(self, out, in_, func, reads, writes, bias=None, scale=None, accum_out=None):
        kw = {}
        if bias is not None:
            kw["bias"] = bias
        if scale is not None:
            kw["scale"] = scale
        if accum_out is not None:
            kw["accum_out"] = accum_out
        self.S.op("act", lambda e: e.activation(out=out, in_=in_, func=func, **kw), reads, writes)

    def tt(self, eng, out, in0, in1, op, reads, writes):
        self.S.op(eng, lambda e: e.tensor_tensor(out=out, in0=in0, in1=in1, op=op), reads, writes)

    def ts(self, eng, out, in0, s1, s2, op0, op1, reads, writes):
        if s2 is None:
            self.S.op(eng, lambda e: e.tensor_scalar(out=out, in0=in0, scalar1=s1, scalar2=None, op0=op0), reads, writes)
        else:
            self.S.op(eng, lambda e: e.tensor_scalar(out=out, in0=in0, scalar1=s1, scalar2=s2, op0=op0, op1=op1), reads, writes)

    def stt(self, eng, out, in0, scalar, in1, op0, op1, reads, writes):
        self.S.op(eng, lambda e: e.scalar_tensor_tensor(out=out, in0=in0, scalar=scalar, in1=in1, op0=op0, op1=op1), reads, writes)

    def cp(self, eng, out, in_, reads, writes):
        if eng == "act":
            self.S.op("act", lambda e: e.copy(out=out, in_=in_), reads, writes)
        else:
            self.S.op(eng, lambda e: e.tensor_copy(out=out, in_=in_), reads, writes)

    def memset(self, eng, ap, val, writes):
        self.S.op(eng, lambda e: e.memset(ap, val), (), writes)

    def dma(self, out, in_, lane, reads, writes, q="sp", nc_ok=False):
        if nc_ok:
            nc = self.nc

            def f(e):
                with nc.allow_non_contiguous_dma(reason="tiny strided parameter load"):
                    return e.dma_start(out=out, in_=in_)
            self.S.dma(q, f, lane, reads, writes)
        else:
            self.S.dma(q, lambda e: e.dma_start(out=out, in_=in_), lane, reads, writes)

    def arena_reset(self):
        self.S.barrier()
        self.aoff = 0

    def alloc(self, shape, dt, name="b"):
        n = 1
        for s in shape[1:]:
            n *= s
        nbytes = n * DT_SIZE[dt]
        nbytes = (nbytes + 63) // 64 * 64
        off = self.aoff
        self.aoff += nbytes
        assert self.aoff <= self.ARENA_BYTES, (name, self.aoff, self.ARENA_BYTES)
        if dt == F32:
            v = self.arena[:, off // 4: off // 4 + n]
        else:
            v = self.arena_bf[:, off // 2: off // 2 + n]
        if len(shape) == 3:
            v = v.rearrange("p (a b) -> p a b", a=shape[1])
        elif len(shape) == 4:
            v = v.rearrange("p (a b c) -> p a b c", a=shape[1], b=shape[2])
        if shape[0] < 128:
            v = v[0:shape[0]]
        self._uid += 1
        return Buf(v, Res(f"{name}{self._uid}"))

    def build(self):
        nc, SEQ, NT = self.nc, self.SEQ, self.NT
        L = DEPTH
        ext = lambda name, shape, dt=F32: nc.dram_tensor(name, shape, dt, kind="ExternalInput").ap()
        I = {}
        I["x"] = ext("x", [SEQ, D])
        I["ln_in_g"] = ext("ln_in_g", [D]); I["ln_in_b"] = ext("ln_in_b", [D])
        I["w_in"] = ext("w_in", [L, D, DIN])
        I["conv_a_w"] = ext("conv_a_w", [L, 31, GW]); I["conv_a_b"] = ext("conv_a_b", [L, GW])
        I["ln_a_g"] = ext("ln_a_g", [L, GW]); I["ln_a_b"] = ext("ln_a_b", [L, GW])
        I["qk_norm_q"] = ext("qk_norm_q", [L, HD]); I["qk_norm_k"] = ext("qk_norm_k", [L, HD])
        I["sgu_ln_g"] = ext("sgu_ln_g", [L, GW]); I["sgu_ln_b"] = ext("sgu_ln_b", [L, GW])
        I["sgu_w"] = ext("sgu_w", [L, 4, 128, 128]); I["sgu_b"] = ext("sgu_b", [L, 4, 128])
        I["mla_q_norm"] = ext("mla_q_norm", [L, 192]); I["mla_w_uq"] = ext("mla_w_uq", [L, 192, 384])
        I["mla_kv_norm"] = ext("mla_kv_norm", [L, 128]); I["mla_w_ukv"] = ext("mla_w_ukv", [L, 128, 512])
        I["w_out"] = ext("w_out", [L, D, D])
        I["ln_mix_g"] = ext("ln_mix_g", [L, D]); I["ln_mix_b"] = ext("ln_mix_b", [L, D])
        I["ffn_w_up"] = ext("ffn_w_up", [L, D, 2 * DFF])
        I["ffn_conv_w"] = ext("ffn_conv_w", [L, 3, 2 * DFF]); I["ffn_conv_b"] = ext("ffn_conv_b", [L, 2 * DFF])
        I["ffn_w_down"] = ext("ffn_w_down", [L, DFF, D])
        I["ln_ffn_g"] = ext("ln_ffn_g", [L, D]); I["ln_ffn_b"] = ext("ln_ffn_b", [L, D])
        I["rope_tbl"] = ext("rope_tbl", [SEQ, 96])
        self.I = I
        self.out = nc.dram_tensor("out", [SEQ, D], F32, kind="ExternalOutput").ap()

        skind = "ExternalOutput" if self.debug else "Internal"
        scr = lambda name, shape, dt=BF16: nc.dram_tensor(name, shape, dt, kind=skind).ap()
        self.aT_d = scr("aT_d", [2, 128, SEQ + 32]); self.r_aT = Res("aT_d")
        self.QT_d = scr("QT_d", [4, 64, SEQ]); self.r_QT = Res("QT_d")
        self.KT_d = scr("KT_d", [2, 64, SEQ]); self.r_KT = Res("KT_d")
        self.V_d = scr("V_d", [SEQ, 2, 64]); self.r_V = Res("V_d")
        self.QfT_d = scr("QfT_d", [4, 96, SEQ]); self.r_QfT = Res("QfT_d")
        self.KfT_d = scr("KfT_d", [4, 96, SEQ]); self.r_KfT = Res("KfT_d")
        self.Vd_d = scr("Vd_d", [SEQ, 4, 64]); self.r_Vd = Res("Vd_d")
        self.OT_d = scr("OT_d", [8, 128, SEQ]); self.r_OT = [Res(f"OT{c}") for c in range(8)]
        wk = "Internal"
        self.wup_d = [nc.dram_tensor(f"wup_d{l}", [NJ, 128, 8, 256], BF16, kind=wk).ap() for l in range(L)]
        self.wdn_d = [nc.dram_tensor(f"wdn_d{l}", [NJ, 128, D], BF16, kind=wk).ap() for l in range(L)]
        self.r_wup = [[Res(f"wup{l}_{j}") for j in range(NJ)] for l in range(L)]
        self.r_wdn = [[Res(f"wdn{l}_{j}") for j in range(NJ)] for l in range(L)]

        self.H = nc.alloc_sbuf_tensor("H", [128, NT, D], F32)
        self.Hr = [Res(f"H{i}") for i in range(NT)]
        self.ident = nc.alloc_sbuf_tensor("ident", [128, 128], BF16)
        self.identf = nc.alloc_sbuf_tensor("identf", [128, 128], F32)
        self.r_ident = Res("ident")
        self.cst = nc.alloc_sbuf_tensor("cst", [128, 16], F32)
        self.r_cst = Res("cst")
        rem = nc.sbuf_bytes_remaining
        self.ARENA_BYTES = (rem - 512) // 256 * 256
        self.arena = nc.alloc_sbuf_tensor("arena", [128, self.ARENA_BYTES // 4], F32)
        self.arena_bf = self.arena.bitcast(BF16)
        self.aoff = 0
        self.PS = nc.alloc_psum_tensor("PS", [128, 8, 512], F32)
        self.PSr = [Res(f"ps{b}") for b in range(8)]
        self.PSbf = self.PS.bitcast(BF16)
        self.wcast_queue = []

        self.consts()
        self.phase0()
        if self.stop_after != "p0":
            for l in range(L):
                self.wcast_prepare(l)
                self.phase1(l)
                if self.stop_after == f"p1_{l}":
                    break
                self.phase2_attn(l)
                if self.stop_after == f"p2a_{l}":
                    break
                self.phase2c(l)
                if self.stop_after == f"p2_{l}":
                    break
                self.phase3(l)
                if self.stop_after == f"p3_{l}":
                    break
        self.final_store()
        info = self.S.emit()
        return nc, info

    def psb(self, b):
        return self.PS[:, b, :]

    def psb_bf(self, b):
        return self.PSbf[:, b, :] if len(self.PSbf.shape) == 3 else self.PSbf[:, b * 1024:(b + 1) * 1024]

    def consts(self):
        S = self.S
        identf, ident = self.identf, self.ident
        S.op("pool", lambda e: e.memset(identf[:], 0.0), (), [self.r_ident])
        S.op("pool", lambda e: e.affine_select(out=identf[:], in_=identf[:], pattern=[[-1, 128]],
                                                compare_op=ALU.not_equal, fill=1.0, base=0, channel_multiplier=1),
             [self.r_ident], [self.r_ident])
        S.op("dve", lambda e: e.tensor_copy(out=ident[:], in_=identf[:]), [self.r_ident], [self.r_ident])
        cst = self.cst
        S.op("pool", lambda e: e.memset(cst[:, 0:8], -0.5), (), [self.r_cst])
        S.op("pool", lambda e: e.memset(cst[:, 8:9], 1.0 / 192.0), (), [self.r_cst])
        S.op("pool", lambda e: e.memset(cst[:, 9:10], 1.0 / 128.0), (), [self.r_cst])

    def rstd_pool(self, out, in_, n, eps, reads, writes, scale=None):
        if scale is None:
            self.ts("pool", out, in_, eps, None, ALU.add, None, reads, writes)
        else:
            self.ts("pool", out, in_, scale, eps, ALU.mult, ALU.add, reads, writes)
        self.tt("pool", out, out, self.cst[:, 0:n], ALU.pow, list(writes) + [self.r_cst], writes)

    def bcast_load(self, buf, src_1d, n, lane):
        self.dma(buf.ap, src_1d.partition_broadcast(128), lane, (), [buf.r])

    def layernorm_tile(self, x_ap, x_res, out_ap, out_res, g, b, scr, eng_b="pool"):
        st = scr.ap
        r = scr.r
        S = self.S
        S.op("dve", lambda e: e.bn_stats(out=st[:, 0:6], in_=x_ap[:, 0:512]), [x_res], [r])
        S.op("dve", lambda e: e.bn_stats(out=st[:, 6:12], in_=x_ap[:, 512:1024]), [x_res], [r])
        S.op("dve", lambda e: e.bn_aggr(out=st[:, 12:14], in_=st[:, 0:12]), [r], [r])
        self.rstd_pool(st[:, 14:15], st[:, 13:14], 1, LN_EPS, [r], [r])
        self.stt("dve", st[:, 15:16], st[:, 12:13], -1.0, st[:, 14:15], ALU.mult, ALU.mult, [r], [r])
        self.actf(out_ap, x_ap, AF.Identity, [x_res, r], [out_res], bias=st[:, 15:16], scale=st[:, 14:15])
        self.tt("dve", out_ap, out_ap, g.ap, ALU.mult, [out_res, g.r], [out_res])
        self.tt(eng_b, out_ap, out_ap, b.ap, ALU.add, [out_res, b.r], [out_res])

    def layernorm_tiles(self, items, g, b, eng_b="dve"):
        S = self.S
        for x_ap, xr, scr in items:
            st = scr.ap; r = scr.r
            S.op("dve", (lambda o, i_: (lambda e: e.bn_stats(out=o, in_=i_)))(st[:, 0:6], x_ap[:, 0:512]), [xr], [r])
            S.op("dve", (lambda o, i_: (lambda e: e.bn_stats(out=o, in_=i_)))(st[:, 6:12], x_ap[:, 512:1024]), [xr], [r])
            S.op("dve", (lambda o, i_: (lambda e: e.bn_aggr(out=o, in_=i_)))(st[:, 12:14], st[:, 0:12]), [r], [r])
        for x_ap, xr, scr in items:
            st = scr.ap; r = scr.r
            self.rstd_pool(st[:, 14:15], st[:, 13:14], 1, LN_EPS, [r], [r])
        for x_ap, xr, scr in items:
            st = scr.ap; r = scr.r
            self.stt("dve", st[:, 15:16], st[:, 12:13], -1.0, st[:, 14:15], ALU.mult, ALU.mult, [r], [r])
        for x_ap, xr, scr in items:
            st = scr.ap; r = scr.r
            self.actf(x_ap, x_ap, AF.Identity, [xr, r], [xr], bias=st[:, 15:16], scale=st[:, 14:15])
        for x_ap, xr, scr in items:
            self.tt("dve", x_ap, x_ap, g.ap, ALU.mult, [xr, g.r], [xr])
        for k, (x_ap, xr, scr) in enumerate(items):
            self.tt(eng_b if k % 2 == 0 else "pool", x_ap, x_ap, b.ap, ALU.add, [xr, b.r], [xr])

    def phase0(self):
        self.arena_reset()
        g = self.alloc([128, D], F32, "g0"); b = self.alloc([128, D], F32, "b0")
        self.bcast_load(g, self.I["ln_in_g"], D, "p0g")
        self.bcast_load(b, self.I["ln_in_b"], D, "p0b")
        scrs = [self.alloc([128, 32], F32, "st") for _ in range(8)]
        for i0 in range(0, self.NT, 4):
            items = []
            for i in range(i0, i0 + 4):
                hi = self.H[:, i, :]
                self.dma(hi, self.I["x"][i * 128:(i + 1) * 128, :], f"Hld{i % 4}", (), [self.Hr[i]])
                items.append((hi, self.Hr[i], scrs[i % 8]))
            self.layernorm_tiles(items, g, b)

    def wcast_prepare(self, l):
        q = []
        wu = self.I["ffn_w_up"][l].rearrange("(k p) n -> p k n", p=128)
        for j in range(NJ):
            q.append((wu[:, :, j * 128:(j + 1) * 128], self.wup_d[l][j, :, :, 0:128], self.r_wup[l][j], True))
            q.append((wu[:, :, DFF + j * 128: DFF + (j + 1) * 128], self.wup_d[l][j, :, :, 128:256], self.r_wup[l][j], True))
        for j in range(NJ):
            q.append((self.I["ffn_w_down"][l, j * 128:(j + 1) * 128, :], self.wdn_d[l][j], self.r_wdn[l][j], False))
        self.wcast_queue = q

    def wcast_step(self, bufs, n=1):
        for _ in range(n):
            if not self.wcast_queue:
                return
            src, dst, res, is3 = self.wcast_queue.pop(0)
            k = self._wc_i = getattr(self, "_wc_i", 0) + 1
            fb, bb = bufs[k % len(bufs)]
            if is3:
                self.dma(fb.ap.rearrange("p (k n) -> p k n", k=8), src, f"wc_in{k % len(bufs)}", (), [fb.r])
            else:
                self.dma(fb.ap, src, f"wc_in{k % len(bufs)}", (), [fb.r])
            self.cp("pool", bb.ap, fb.ap, [fb.r], [bb.r])
            if is3:
                self.dma(dst, bb.ap.rearrange("p (k n) -> p k n", k=8), f"wc_out{k % len(bufs)}", [bb.r], [res])
            else:
                self.dma(dst, bb.ap, f"wc_out{k % len(bufs)}", [bb.r], [res])

    def phase1(self, l):
        self._p1(l, "AB")
        self._p1(l, "CD")

    def _p1(self, l, part):
        I, S, NT = self.I, self.S, self.NT
        AB = part == "AB"
        self.arena_reset()
        ident = self.ident[:]
        rid = self.r_ident
        PSr = self.PSr
        wc0, wc1 = (0, 1024) if AB else (1024, DIN)
        winb = self.alloc([128, 8, wc1 - wc0], BF16, "winb")
        for k in range(8):
            self.dma(winb.ap[:, k, :], I["w_in"][l, k * 128:(k + 1) * 128, wc0:wc1], f"winb{k % 2}", (), [winb.r], q="pool")
        if AB:
            gqk = self.alloc([128, 6, 64], F32, "gqk")
            for h in range(4):
                self.dma(gqk.ap[:, h, :], I["qk_norm_q"][l].partition_broadcast(128), "gq", (), [gqk.r])
            for h in range(2):
                self.dma(gqk.ap[:, 4 + h, :], I["qk_norm_k"][l].partition_broadcast(128), "gk", (), [gqk.r])
            zb = self.alloc([128, 2, 16], BF16, "zb")
            self.memset("dve", zb.ap, 0.0, [zb.r])
            self.dma(self.aT_d[:, :, 0:16].rearrange("c p t -> p c t"), zb.ap, "zpad0", [zb.r], [self.r_aT])
            self.dma(self.aT_d[:, :, 16 + self.SEQ:32 + self.SEQ].rearrange("c p t -> p c t"), zb.ap, "zpad1", [zb.r], [self.r_aT])
        else:
            wuq = self.alloc([128, 2, 384], BF16, "wuq")
            self.dma(wuq.ap[:, 0, :], I["mla_w_uq"][l, 0:128, :], "wuq0", (), [wuq.r], q="pool")
            self.dma(wuq.ap[0:64, 1, :], I["mla_w_uq"][l, 128:192, :], "wuq1", (), [wuq.r], q="pool")
            wukv = self.alloc([128, 512], BF16, "wukv")
            self.dma(wukv.ap, I["mla_w_ukv"][l], "wukv", (), [wukv.r], q="pool")
            wsn = self.alloc([128, 4, 128], BF16, "wsn")
            self.dma(wsn.ap, I["sgu_w"][l].rearrange("g p q -> p g q"), "wsn", (), [wsn.r], q="pool")
            wsT = self.alloc([128, 4, 128], BF16, "wsT")
            pb = self.psb_bf(7)
            for g in range(4):
                self.tr(pb[:, g * 128:(g + 1) * 128], wsn.ap[:, g, :], ident, [wsn.r, rid], [self.PSr[7]])
            self.cp("dve", wsT.ap, pb[:, 0:512].rearrange("p (g q) -> p g q", g=4), [self.PSr[7]], [wsT.r])
            sgub = self.alloc([128, 4], F32, "sgub")
            self.dma(sgub.ap, I["sgu_b"][l].rearrange("g p -> p g"), "sgub", (), [sgub.r], nc_ok=True)
            gsg = self.alloc([128, 256], F32, "gsg"); bsg = self.alloc([128, 256], F32, "bsg")
            self.bcast_load(gsg, I["sgu_ln_g"][l], 256, "gsg"); self.bcast_load(bsg, I["sgu_ln_b"][l], 256, "bsg")
            gcq = self.alloc([128, 192], F32, "gcq"); gckv = self.alloc([128, 128], F32, "gckv")
            self.bcast_load(gcq, I["mla_q_norm"][l], 192, "gcq"); self.bcast_load(gckv, I["mla_kv_norm"][l], 128, "gckv")

        NB = 2
        mk = lambda shape, dt, nm: [self.alloc(shape, dt, nm) for _ in range(NB)]
        hb = mk([128, D], BF16, "hb"); hT = mk([128, 8, 128], BF16, "hT")
        tbl = mk([128, 96], F32, "tbl"); st = mk([128, 32], F32, "st")
        if AB:
            sgm = mk([128, 256], F32, "sgm"); a_bf = mk([128, 256], BF16, "a_bf"); aT = mk([128, 2, 128], BF16, "aT")
            qk_sb = mk([128, 6, 64], F32, "qk_sb"); sq = mk([128, 384], F32, "sq")
            ra = mk([128, 192], F32, "ra"); rb = mk([128, 192], F32, "rb")
            qk_bf = mk([128, 6, 64], BF16, "qk_bf"); qkT = mk([64, 6, 128], BF16, "qkT"); v_bf = mk([128, 2, 64], BF16, "v_bf")
        else:
            sqd = mk([128, 384], F32, "sqd"); rad = mk([128, 64], F32, "rad"); rbd = mk([128, 64], F32, "rbd")
            c_sb = mk([128, 512], F32, "c_sb"); svn = mk([128, 256], F32, "svn"); svn_bf = mk([128, 256], BF16, "svn_bf")
            oc_bf = mk([128, 256], BF16, "oc_bf"); ocT = mk([128, 2, 128], BF16, "ocT")
            d_sb = mk([128, 352], F32, "d_sb"); cqn = mk([128, 320], BF16, "cqn"); krr = mk([128, 32], BF16, "krr")
            cT = mk([128, 3, 128], BF16, "cT")
            qf = mk([128, 4, 96], BF16, "qf"); kf = mk([128, 4, 96], BF16, "kf"); vd = mk([128, 4, 64], BF16, "vd")
            qkfT = mk([96, 8, 128], BF16, "qkfT")

        def rope(x1, x2, cos, sin, o1, o2, ta, tb, rin, rtmp_a, rtmp_b, rout):
            self.tt("dve", ta, x1, cos, ALU.mult, rin, [rtmp_a])
            self.tt("dve", tb, x2, sin, ALU.mult, rin, [rtmp_b])
            self.tt("dve", o1, ta, tb, ALU.subtract, [rtmp_a, rtmp_b], [rout])
            self.tt("dve", ta, x2, cos, ALU.mult, rin, [rtmp_a])
            self.tt("dve", tb, x1, sin, ALU.mult, rin, [rtmp_b])
            self.tt("dve", o2, ta, tb, ALU.add, [rtmp_a, rtmp_b], [rout])

        def make_tile(i):
            s = i % NB
            Hi = self.H[:, i, :]
            tsl = slice(i * 128, (i + 1) * 128)
            bT = 4 + s if AB else 4
            b0, b1 = 2 * s, 2 * s + 1
            cols = [(0, 512), (512, wc1 - wc0)]
            def pro():
                yield
                self.dma(tbl[s].ap, I["rope_tbl"][tsl, :], f"tbl{s}", (), [tbl[s].r])
                bT = 4 + s if AB else 4
                yield
                self.cp("act", hb[s].ap, Hi, [self.Hr[i]], [hb[s].r])
                p4 = self.psb_bf(bT)
                yield
                for k in range(8):
                    self.tr(p4[:, k * 128:(k + 1) * 128], hb[s].ap[:, k * 128:(k + 1) * 128], ident, [hb[s].r, rid], [PSr[bT]])
                yield
                self.cp("dve" if AB else "act", hT[s].ap, p4.rearrange("p (k t) -> p k t", k=8), [PSr[bT]], [hT[s].r])
                b0, b1 = 2 * s, 2 * s + 1
                cols = [(0, 512), (512, wc1 - wc0)]
                yield
                for bnk, (c0, c1) in zip((b0, b1), cols):
                    for k in range(8):
                        self.mm(self.PS[:, bnk, 0:c1 - c0], hT[s].ap[:, k, :], winb.ap[:, k, c0:c1], k == 0, k == 7,
                                [hT[s].r, winb.r], [PSr[bnk]])
                yield
            stq = st[s].ap
            if AB:
                bX = 6 + s
                pA = self.psb(b0); pB = self.psb(b1); pX = self.psb_bf(bX)
            else:
                pC = self.psb(b0); pD = self.psb(b1)
                p5f = self.psb(5); p5 = self.psb_bf(5); p6f = self.psb(6); p6 = self.psb_bf(6)
                p7f = self.psb(7); p7 = self.psb_bf(7)

            def chainA():
                p0 = pA
                yield
                self.actf(sgm[s].ap, p0[:, 256:512], AF.Tanh, [PSr[b0]], [sgm[s].r], scale=0.5)
                yield
                self.ts("dve", sgm[s].ap, sgm[s].ap, 0.5, 0.5, ALU.mult, ALU.add, [sgm[s].r], [sgm[s].r])
                yield
                self.tt("dve", a_bf[s].ap, p0[:, 0:256], sgm[s].ap, ALU.mult, [PSr[b0], sgm[s].r], [a_bf[s].r])
                p5 = pX
                yield
                for c in range(2):
                    self.tr(p5[:, c * 128:(c + 1) * 128], a_bf[s].ap[:, c * 128:(c + 1) * 128], ident, [a_bf[s].r, rid], [PSr[bX]])
                yield
                self.cp("act", aT[s].ap, p5[:, 0:256].rearrange("p (c t) -> p c t", c=2), [PSr[bX]], [aT[s].r])
                yield
                self.dma(self.aT_d[:, :, 16 + i * 128: 16 + (i + 1) * 128].rearrange("c p t -> p c t"), aT[s].ap,
                         f"aT{s}", [aT[s].r], [self.r_aT])
                yield
            def chainB():
                p1 = pB
                qkf = qk_sb[s].ap.rearrange("p h d -> p (h d)")
                yield
                self.cp("act", qkf, p1[:, 0:384], [PSr[b1]], [qk_sb[s].r])
                yield
                self.cp("act", v_bf[s].ap.rearrange("p h d -> p (h d)"), p1[:, 384:512], [PSr[b1]], [v_bf[s].r])
                yield
                self.dma(self.V_d[tsl, :, :], v_bf[s].ap, f"v{s}", [v_bf[s].r], [self.r_V])
                yield
                self.tt("pool", sq[s].ap, qkf, qkf, ALU.mult, [qk_sb[s].r], [sq[s].r])
                stq = st[s].ap
                yield
                S.op("dve", (lambda o, i_: (lambda e: e.tensor_reduce(out=o, in_=i_, axis=AX.X, op=ALU.add)))(
                    stq[:, 0:6], sq[s].ap.rearrange("p (h d) -> p h d", h=6)), [sq[s].r], [st[s].r])
                yield
                self.rstd_pool(stq[:, 0:6], stq[:, 0:6], 6, RMS_EPS, [st[s].r], [st[s].r], scale=1.0 / 64.0)
                yield
                self.tt("dve", qk_sb[s].ap, qk_sb[s].ap, stq[:, 0:6].unsqueeze(2).broadcast_to([128, 6, 64]), ALU.mult,
                        [qk_sb[s].r, st[s].r], [qk_sb[s].r])
                yield
                self.tt("dve", qk_sb[s].ap, qk_sb[s].ap, gqk.ap, ALU.mult, [qk_sb[s].r, gqk.r], [qk_sb[s].r])
                xv = qk_sb[s].ap.rearrange("p h (r f e) -> p h r f e", r=2, f=2)
                ov = qk_bf[s].ap.rearrange("p h (r f e) -> p h r f e", r=2, f=2)
                cosB = tbl[s].ap[:, 0:32].rearrange("p (r e) -> p r e", r=2).unsqueeze(1).broadcast_to([128, 6, 2, 16])
                sinB = tbl[s].ap[:, 32:64].rearrange("p (r e) -> p r e", r=2).unsqueeze(1).broadcast_to([128, 6, 2, 16])
                ta = ra[s].ap.rearrange("p (h r e) -> p h r e", h=6, r=2)
                tb = rb[s].ap.rearrange("p (h r e) -> p h r e", h=6, r=2)
                yield
                rope(xv[:, :, :, 0, :], xv[:, :, :, 1, :], cosB, sinB, ov[:, :, :, 0, :], ov[:, :, :, 1, :], ta, tb,
                     [qk_sb[s].r, tbl[s].r], ra[s].r, rb[s].r, qk_bf[s].r)
                yield
                for h in range(6):
                    self.tr(pX[0:64, 256 + h * 128:256 + (h + 1) * 128], qk_bf[s].ap[:, h, :], ident, [qk_bf[s].r, rid], [PSr[bX]])
                yield
                self.cp("act", qkT[s].ap, pX[0:64, 256:1024].rearrange("p (h t) -> p h t", h=6), [PSr[bX]], [qkT[s].r])
                yield
                self.dma(self.QT_d[:, :, tsl].rearrange("h d t -> d h t"), qkT[s].ap[:, 0:4, :], f"qT{s}", [qkT[s].r], [self.r_QT])
                yield
                self.dma(self.KT_d[:, :, tsl].rearrange("h d t -> d h t"), qkT[s].ap[:, 4:6, :], f"kT{s}", [qkT[s].r], [self.r_KT])
                yield
            def chainC():
                yield
                self.actf(c_sb[s].ap, pC, AF.Gelu, [PSr[b0]], [c_sb[s].r])
                sv = c_sb[s].ap[:, 256:512]
                yield
                S.op("dve", (lambda o, i_: (lambda e: e.bn_stats(out=o, in_=i_)))(stq[:, 8:14], sv), [c_sb[s].r], [st[s].r])
                yield
                S.op("dve", (lambda o, i_: (lambda e: e.bn_aggr(out=o, in_=i_)))(stq[:, 14:16], stq[:, 8:14]), [st[s].r], [st[s].r])
                yield
                self.rstd_pool(stq[:, 16:17], stq[:, 15:16], 1, LN_EPS, [st[s].r], [st[s].r])
                yield
                self.stt("dve", stq[:, 17:18], stq[:, 14:15], -1.0, stq[:, 16:17], ALU.mult, ALU.mult, [st[s].r], [st[s].r])
                yield
                self.actf(svn[s].ap, sv, AF.Identity, [c_sb[s].r, st[s].r], [svn[s].r], bias=stq[:, 17:18], scale=stq[:, 16:17])
                yield
                self.tt("dve", svn[s].ap, svn[s].ap, gsg.ap, ALU.mult, [svn[s].r, gsg.r], [svn[s].r])
                yield
                self.tt("pool", svn_bf[s].ap, svn[s].ap, bsg.ap, ALU.add, [svn[s].r, bsg.r], [svn_bf[s].r])
                yield
                for g in range(4):
                    self.mm(p5f[:, g * 64:(g + 1) * 64], wsT.ap[:, g, :], svn_bf[s].ap[:, g * 64:(g + 1) * 64], True, True,
                            [wsT.r, svn_bf[s].r], [PSr[5]])
                yield
                for g in range(4):
                    self.stt("dve", oc_bf[s].ap[:, g * 64:(g + 1) * 64], p5f[:, g * 64:(g + 1) * 64], sgub.ap[:, g:g + 1],
                             c_sb[s].ap[:, g * 64:(g + 1) * 64], ALU.add, ALU.mult, [PSr[5], sgub.r, c_sb[s].r], [oc_bf[s].r])
                yield
                for c in range(2):
                    self.tr(p5[:, 512 + c * 128: 512 + (c + 1) * 128], oc_bf[s].ap[:, c * 128:(c + 1) * 128], ident,
                            [oc_bf[s].r, rid], [PSr[5]])
                yield
                self.cp("act", ocT[s].ap, p5[:, 512:768].rearrange("p (c t) -> p c t", c=2), [PSr[5]], [ocT[s].r])
                yield
                self.dma(self.OT_d[4:6, :, tsl].rearrange("c p t -> p c t"), ocT[s].ap, f"ocT{s}", [ocT[s].r],
                         [self.r_OT[4], self.r_OT[5]])
                yield
            def chainD():
                yield
                self.cp("act", d_sb[s].ap, pD[:, 0:352], [PSr[b1]], [d_sb[s].r])
                yield
                self.tt("pool", sqd[s].ap[:, 0:320], d_sb[s].ap[:, 0:320], d_sb[s].ap[:, 0:320], ALU.mult, [d_sb[s].r], [sqd[s].r])
                yield
                S.op("dve", (lambda o, i_: (lambda e: e.tensor_reduce(out=o, in_=i_, axis=AX.X, op=ALU.add)))(
                    stq[:, 20:21], sqd[s].ap[:, 0:192]), [sqd[s].r], [st[s].r])
                yield
                S.op("dve", (lambda o, i_: (lambda e: e.tensor_reduce(out=o, in_=i_, axis=AX.X, op=ALU.add)))(
                    stq[:, 21:22], sqd[s].ap[:, 192:320]), [sqd[s].r], [st[s].r])
                yield
                self.tt("dve", stq[:, 20:22], stq[:, 20:22], self.cst[:, 8:10], ALU.mult, [st[s].r, self.r_cst], [st[s].r])
                yield
                self.rstd_pool(stq[:, 20:22], stq[:, 20:22], 2, RMS_EPS, [st[s].r], [st[s].r])
                yield
                self.stt("dve", cqn[s].ap[:, 0:192], d_sb[s].ap[:, 0:192], stq[:, 20:21], gcq.ap, ALU.mult, ALU.mult,
                         [d_sb[s].r, st[s].r, gcq.r], [cqn[s].r])
                yield
                self.stt("dve", cqn[s].ap[:, 192:320], d_sb[s].ap[:, 192:320], stq[:, 21:22], gckv.ap, ALU.mult, ALU.mult,
                         [d_sb[s].r, st[s].r, gckv.r], [cqn[s].r])
                kx = d_sb[s].ap[:, 320:352].rearrange("p (r f e) -> p r f e", r=2, f=2)
                ko = krr[s].ap.rearrange("p (r f e) -> p r f e", r=2, f=2)
                cosD = tbl[s].ap[:, 64:80].rearrange("p (r e) -> p r e", r=2)
                sinD = tbl[s].ap[:, 80:96].rearrange("p (r e) -> p r e", r=2)
                ta2 = rad[s].ap[:, 0:16].rearrange("p (r e) -> p r e", r=2)
                tb2 = rbd[s].ap[:, 0:16].rearrange("p (r e) -> p r e", r=2)
                yield
                rope(kx[:, :, 0, :], kx[:, :, 1, :], cosD, sinD, ko[:, :, 0, :], ko[:, :, 1, :], ta2, tb2,
                     [d_sb[s].r, tbl[s].r], rad[s].r, rbd[s].r, krr[s].r)
                yield
                self.tr(p7[:, 768:896], cqn[s].ap[:, 0:128], ident, [cqn[s].r, rid], [PSr[7]])
                yield
                self.tr(p7[0:64, 896:1024], cqn[s].ap[:, 128:192], ident, [cqn[s].r, rid], [PSr[7]])
                yield
                self.tr(p5[:, 768:896], cqn[s].ap[:, 192:320], ident, [cqn[s].r, rid], [PSr[5]])
                yield
                self.cp("act", cT[s].ap[:, 0, :], p7[:, 768:896], [PSr[7]], [cT[s].r])
                yield
                self.cp("act", cT[s].ap[0:64, 1, :], p7[0:64, 896:1024], [PSr[7]], [cT[s].r])
                yield
                self.cp("act", cT[s].ap[:, 2, :], p5[:, 768:896], [PSr[5]], [cT[s].r])
                yield
                self.mm(p7f[:, 0:384], cT[s].ap[:, 0, :], wuq.ap[:, 0, :], True, False, [cT[s].r, wuq.r], [PSr[7]])
                yield
                self.mm(p7f[:, 0:384], cT[s].ap[0:64, 1, :], wuq.ap[0:64, 1, :], False, True, [cT[s].r, wuq.r], [PSr[7]])
                yield
                self.mm(p6f, cT[s].ap[:, 2, :], wukv.ap, True, True, [cT[s].r, wukv.r], [PSr[6]])
                qd = p7f[:, 0:384].rearrange("p (h d) -> p h d", h=4)
                kvd = p6f.rearrange("p (h d) -> p h d", h=4)
                yield
                self.cp("act", qf[s].ap[:, :, 0:64], qd[:, :, 0:64], [PSr[7]], [qf[s].r])
                qd_sb = sqd[s].ap[:, 0:384].rearrange("p (h d) -> p h d", h=4)
                yield
                self.cp("act", qd_sb[:, :, 64:96], qd[:, :, 64:96], [PSr[7]], [sqd[s].r])
                qx = qd_sb[:, :, 64:96].rearrange("p h (r f e) -> p h r f e", r=2, f=2)
                qo = qf[s].ap[:, :, 64:96].rearrange("p h (r f e) -> p h r f e", r=2, f=2)
                cosD4 = cosD.unsqueeze(1).broadcast_to([128, 4, 2, 8])
                sinD4 = sinD.unsqueeze(1).broadcast_to([128, 4, 2, 8])
                ta3 = rad[s].ap[:, 0:64].rearrange("p (h r e) -> p h r e", h=4, r=2)
                tb3 = rbd[s].ap[:, 0:64].rearrange("p (h r e) -> p h r e", h=4, r=2)
                yield
                rope(qx[:, :, :, 0, :], qx[:, :, :, 1, :], cosD4, sinD4, qo[:, :, :, 0, :], qo[:, :, :, 1, :], ta3, tb3,
                     [sqd[s].r, tbl[s].r], rad[s].r, rbd[s].r, qf[s].r)
                yield
                self.cp("act", kf[s].ap[:, :, 0:64], kvd[:, :, 0:64], [PSr[6]], [kf[s].r])
                yield
                self.cp("pool", kf[s].ap[:, :, 64:96], krr[s].ap.unsqueeze(1).broadcast_to([128, 4, 32]), [krr[s].r], [kf[s].r])
                yield
                self.cp("act", vd[s].ap, kvd[:, :, 64:128], [PSr[6]], [vd[s].r])
                yield
                self.dma(self.Vd_d[tsl, :, :], vd[s].ap, f"vd{s}", [vd[s].r], [self.r_Vd])
                yield
                for h in range(4):
                    self.tr(p6[0:96, h * 128:(h + 1) * 128], qf[s].ap[:, h, :], ident, [qf[s].r, rid], [PSr[6]])
                yield
                for h in range(4):
                    self.tr(p6[0:96, (4 + h) * 128:(5 + h) * 128], kf[s].ap[:, h, :], ident, [kf[s].r, rid], [PSr[6]])
                yield
                self.cp("act", qkfT[s].ap, p6[0:96, :].rearrange("p (h t) -> p h t", h=8), [PSr[6]], [qkfT[s].r])
                yield
                self.dma(self.QfT_d[:, :, tsl].rearrange("h d t -> d h t"), qkfT[s].ap[:, 0:4, :], f"qfT{s}", [qkfT[s].r], [self.r_QfT])
                yield
                self.dma(self.KfT_d[:, :, tsl].rearrange("h d t -> d h t"), qkfT[s].ap[:, 4:8, :], f"kfT{s}", [qkfT[s].r], [self.r_KfT])
                yield
            return pro(), ([chainA(), chainB()] if AB else [chainC(), chainD()])

        tiles = [make_tile(i) for i in range(NT)]
        for _ in tiles[0][0]:
            pass
        for i in range(NT):
            gens = list(tiles[i][1])
            if i + 1 < NT:
                gens.append(tiles[i + 1][0])
            while gens:
                for gch in list(gens):
                    try:
                        next(gch)
                    except StopIteration:
                        gens.remove(gch)

    def phase2_attn(self, l):
        SEQ, NT, NQB = self.SEQ, self.NT, self.NQB
        self.arena_reset()
        PSr = self.PSr
        wbufs = [(self.alloc([128, 1024], F32, "wcf"), self.alloc([128, 1024], BF16, "wcb")) for _ in range(2)]
        kt_sb = [self.alloc([128, SEQ], BF16, "kt") for _ in range(2)]
        for kb_ in kt_sb:
            self.memset("pool", kb_.ap[64:128, :], 0.0, [kb_.r])
        v_sb = [self.alloc([128, NT, 128], BF16, "v") for _ in range(2)]
        for vb in v_sb:
            self.memset("pool", vb.ap[:, :, 64:128], 1.0, [vb.r])
        q_sb = [self.alloc([128, 512], BF16, "q") for _ in range(2)]
        for qb_ in q_sb:
            self.memset("pool", qb_.ap[64:128, :], 0.0, [qb_.r])
        pT = [self.alloc([128, 2, 512], BF16, "pT") for _ in range(3)]
        rc = [self.alloc([128, 512], F32, "rc") for _ in range(2)]
        o_sb = [self.alloc([128, 512], BF16, "o") for _ in range(2)]
        passes = []
        for kvh in range(2):
            heads = []
            for g in range(2):
                hq = kvh * 2 + g
                heads.append((self.QT_d[hq], self.r_QT, 2 + hq // 2, (hq % 2) * 64))
            passes.append((self.KT_d[kvh], self.r_KT, self.V_d[:, kvh, :], self.r_V, 64, 64 ** -0.5, heads))
        for h in range(4):
            passes.append((self.KfT_d[h], self.r_KfT, self.Vd_d[:, h, :], self.r_Vd, 96, 96 ** -0.5,
                           [(self.QfT_d[h], self.r_QfT, 6 + h // 2, (h % 2) * 64)]))
        n_iter = sum(len(p[6]) for p in passes) * NQB
        per_it = (len(self.wcast_queue) + n_iter - 1) // n_iter
        groups = []
        it = 0
        for pi, (Ksrc, rK, Vsrc, rV, dk, scale, heads) in enumerate(passes):
            for qb in range(NQB):
                for hi, (Qsrc, rQ, chunk, base) in enumerate(heads):
                    for kg in range(NT // 2):
                        groups.append(dict(pi=pi, qb=qb, hi=hi, kg=kg, it=it, first=(kg == 0), last=(kg == NT // 2 - 1),
                                           pass_first=(qb == 0 and hi == 0 and kg == 0),
                                           prefetch=(qb == 0 and hi == 0 and kg == min(6, NT // 2 - 1))))
                    it += 1
        LOOK = 2
        NG = len(groups)

        def load_kv(pi):
            Ksrc, rK, Vsrc, rV, dk, scale, heads = passes[pi]
            kb = kt_sb[pi % 2]; vb = v_sb[pi % 2]
            self.dma(kb.ap[0:dk, :], Ksrc, f"ktld{pi % 2}", [rK], [kb.r])
            Vv = Vsrc.rearrange("(t p) d -> p t d", p=128)
            for t0 in range(0, NT, 8):
                self.dma(vb.ap[:, t0:t0 + 8, 0:64], Vv[:, t0:t0 + 8, :], f"vld{pi % 2}", [rV], [vb.r])

        def emit_qk(gi):
            g = groups[gi]
            Ksrc, rK, Vsrc, rV, dk, scale, heads = passes[g["pi"]]
            Qsrc, rQ, chunk, base = heads[g["hi"]]
            kb = kt_sb[g["pi"] % 2]; vb = v_sb[g["pi"] % 2]
            if g["pass_first"] and g["pi"] == 0:
                load_kv(0)
            if g["prefetch"] and g["pi"] + 1 < len(passes):
                load_kv(g["pi"] + 1)
            qs = q_sb[g["it"] % 2]
            if g["first"]:
                qsl = slice(g["qb"] * 512, (g["qb"] + 1) * 512)
                self.dma(qs.ap[0:dk, :], Qsrc[:, qsl], f"qld{g['it'] % 2}", [rQ], [qs.r])
                self.wcast_step(wbufs, per_it)
            r3 = gi % 3
            for u in range(2):
                ktile = g["kg"] * 2 + u
                self.mm(self.PS[:, 2 * r3 + u, :], kb.ap[:, ktile * 128:(ktile + 1) * 128], qs.ap[:, :],
                        True, True, [kb.r, qs.r], [PSr[2 * r3 + u]])
            self.actf(pT[r3].ap, self.PS[:, 2 * r3:2 * r3 + 2, :], AF.Exp, [PSr[2 * r3], PSr[2 * r3 + 1]],
                      [pT[r3].r], scale=float(scale))

        def emit_pv(gi):
            g = groups[gi]
            Ksrc, rK, Vsrc, rV, dk, scale, heads = passes[g["pi"]]
            Qsrc, rQ, chunk, base = heads[g["hi"]]
            vb = v_sb[g["pi"] % 2]
            r3 = gi % 3
            acc_b = 6 + g["it"] % 2
            acc = self.psb(acc_b)
            for u in range(2):
                ktile = g["kg"] * 2 + u
                self.mm(acc, vb.ap[:, ktile, :], pT[r3].ap[:, u, :], ktile == 0, ktile == NT - 1,
                        [vb.r, pT[r3].r], [PSr[acc_b]])
            if g["last"]:
                qsl = slice(g["qb"] * 512, (g["qb"] + 1) * 512)
                rcb = rc[g["it"] % 2]; ob = o_sb[g["it"] % 2]
                self.S.op("dve", (lambda o, i_: (lambda e: e.reciprocal(out=o, in_=i_)))(rcb.ap[64:128, :], acc[64:128, :]),
                          [PSr[acc_b]], [rcb.r])
                self.tt("dve", ob.ap[base:base + 64, :], acc[0:64, :], rcb.ap[64:128, :], ALU.mult,
                        [PSr[acc_b], rcb.r], [ob.r])
                self.dma(self.OT_d[chunk, base:base + 64, qsl], ob.ap[base:base + 64, :], f"ost{g['it'] % 2}", [ob.r],
                         [self.r_OT[chunk]])

        for gi in range(NG + LOOK):
            if gi < NG:
                emit_qk(gi)
            if gi >= LOOK:
                emit_pv(gi - LOOK)
        while self.wcast_queue:
            self.wcast_step(wbufs, 1)

    def phase2c(self, l):
        I, S, NT, NQB = self.I, self.S, self.NT, self.NQB
        self.arena_reset()
        PSr = self.PSr
        ident = self.ident[:]; rid = self.r_ident
        woutb = self.alloc([128, 8, D], BF16, "woutb")
        for k in range(8):
            self.dma(woutb.ap[:, k, :], I["w_out"][l, k * 128:(k + 1) * 128, :], f"woutb{k % 2}", (), [woutb.r], q="pool")
        g1 = self.alloc([128, D], F32, "g1"); b1 = self.alloc([128, D], F32, "b1")
        self.bcast_load(g1, I["ln_mix_g"][l], D, "g1"); self.bcast_load(b1, I["ln_mix_b"][l], D, "b1")
        cab = self.alloc([128, 256], F32, "cab")
        self.bcast_load(cab, I["conv_a_b"][l], 256, "cab")
        cw31 = self.alloc([31, 256], F32, "cw31")
        self.dma(cw31.ap, I["conv_a_w"][l], "cw31", (), [cw31.r])
        cw = self.alloc([128, 2, 32], F32, "cw")
        lgb = self.alloc([4, 256], F32, "lgb")
        self.dma(lgb.ap[0:2, 0:128], I["ln_a_g"][l].rearrange("(c p) -> c p", p=128), "lga", (), [lgb.r])
        self.dma(lgb.ap[0:2, 128:256], I["ln_a_b"][l].rearrange("(c p) -> c p", p=128), "lgbb", (), [lgb.r])
        lgT = self.alloc([128, 4], F32, "lgT")
        pf = self.psb(7)
        identf = self.identf[:]
        for c in range(2):
            S.op("pe", (lambda o, i_: (lambda e: e.transpose(out=o, in_=i_, identity=identf[0:31, 0:31])))(
                pf[:, c * 32:c * 32 + 31], cw31.ap[:, c * 128:(c + 1) * 128]), [cw31.r, rid], [PSr[7]])
        S.op("pe", (lambda o, i_: (lambda e: e.transpose(out=o, in_=i_, identity=identf[0:2, 0:2])))(
            pf[:, 64:66], lgb.ap[0:2, 0:128]), [lgb.r, rid], [PSr[7]])
        S.op("pe", (lambda o, i_: (lambda e: e.transpose(out=o, in_=i_, identity=identf[0:2, 0:2])))(
            pf[:, 66:68], lgb.ap[0:2, 128:256]), [lgb.r, rid], [PSr[7]])
        self.cp("dve", cw.ap, pf[:, 0:64].rearrange("p (c j) -> p c j", c=2), [PSr[7]], [cw.r])
        self.cp("dve", lgT.ap, pf[:, 64:68], [PSr[7]], [lgT.r])
        dg = self.alloc([128, 2 * 31 * 128], BF16, "dg")
        dgv = dg.ap.rearrange("p (c j q) -> p c j q", c=2, j=31)
        for c in range(2):
            self.tt("dve", dgv[:, c, :, :], self.identf[:].unsqueeze(1).broadcast_to([128, 31, 128]),
                    cw.ap[:, c, 0:31].unsqueeze(2).broadcast_to([128, 31, 128]), ALU.mult, [cw.r, rid], [dg.r])
        NB = 2
        aw = [self.alloc([128, 2, 542], BF16, "aw") for _ in range(NB)]
        ob = [self.alloc([128, 6, 512], BF16, "ob") for _ in range(NB)]
        xa = [self.alloc([128, 256], F32, "xa") for _ in range(NB)]
        xab = [self.alloc([128, 256], BF16, "xab") for _ in range(NB)]
        oaT = [self.alloc([128, 2, 128], BF16, "oaT") for _ in range(NB)]
        st = [self.alloc([128, 32], F32, "st") for _ in range(NB)]
        for qb in range(NQB):
            s = qb % NB
            qsl = slice(qb * 512, (qb + 1) * 512)
            self.dma(aw[s].ap, self.aT_d[:, :, 1 + qb * 512: 1 + qb * 512 + 542].rearrange("c p t -> p c t"), f"aw{s}",
                     [self.r_aT], [aw[s].r])
            self.dma(ob[s].ap, self.OT_d[2:8, :, qsl].rearrange("c p t -> p c t"), f"ob{s}",
                     self.r_OT[2:8], [ob[s].r])
            for m in range(4):
                i = qb * 4 + m
                u = i % NB
                cb = 4 + 2 * (i % 2)
                p4 = self.psb(cb)
                for c in range(2):
                    for j in range(31):
                        self.mm(p4[:, c * 128:(c + 1) * 128], aw[s].ap[:, c, m * 128 + j: m * 128 + j + 128], dgv[:, c, j, :],
                                j == 0, j == 30, [aw[s].r, dg.r], [PSr[cb]])
                self.tt("dve", xa[u].ap, p4[:, 0:256], cab.ap, ALU.add, [PSr[cb], cab.r], [xa[u].r])
                stq = st[u].ap
                S.op("dve", (lambda o, i_: (lambda e: e.bn_stats(out=o, in_=i_)))(stq[:, 0:6], xa[u].ap), [xa[u].r], [st[u].r])
                S.op("dve", (lambda o, i_: (lambda e: e.bn_aggr(out=o, in_=i_)))(stq[:, 6:8], stq[:, 0:6]), [st[u].r], [st[u].r])
                self.rstd_pool(stq[:, 8:9], stq[:, 7:8], 1, LN_EPS, [st[u].r], [st[u].r])
                self.stt("dve", stq[:, 9:10], stq[:, 6:7], -1.0, stq[:, 8:9], ALU.mult, ALU.mult, [st[u].r], [st[u].r])
                self.actf(xab[u].ap, xa[u].ap, AF.Identity, [xa[u].r, st[u].r], [xab[u].r], bias=stq[:, 9:10], scale=stq[:, 8:9])
                tbk = 5 + 2 * (i % 2)
                p5 = self.psb_bf(tbk)
                for c in range(2):
                    self.tr(p5[:, c * 128:(c + 1) * 128], xab[u].ap[:, c * 128:(c + 1) * 128], ident, [xab[u].r, rid], [PSr[tbk]])
                for c in range(2):
                    self.actf(oaT[u].ap[:, c, :], p5[:, c * 128:(c + 1) * 128], AF.Silu, [PSr[tbk], lgT.r], [oaT[u].r],
                              bias=lgT.ap[:, 2 + c:3 + c], scale=lgT.ap[:, c:c + 1])
                pb0 = 2 * (i % 2)
                for half in range(2):
                    for k in range(8):
                        lhsT = oaT[u].ap[:, k, :] if k < 2 else ob[s].ap[:, k - 2, m * 128:(m + 1) * 128]
                        rr = [oaT[u].r] if k < 2 else [ob[s].r]
                        self.mm(self.PS[:, pb0 + half, :], lhsT, woutb.ap[:, k, half * 512:(half + 1) * 512], k == 0, k == 7,
                                rr + [woutb.r], [PSr[pb0 + half]])
                Hi = self.H[:, i, :]
                for half in range(2):
                    hs = slice(half * 512, (half + 1) * 512)
                    self.stt("dve", Hi[:, hs], Hi[:, hs], ALPHA, self.PS[:, pb0 + half, :], ALU.mult, ALU.add,
                             [self.Hr[i], PSr[pb0 + half]], [self.Hr[i]])
                self.layernorm_tile(Hi, self.Hr[i], Hi, self.Hr[i], g1, b1, st[u], eng_b="dve")

    def phase3(self, l):
        I, S, NT, NQB, SEQ = self.I, self.S, self.NT, self.NQB, self.SEQ
        self.arena_reset()
        PSr = self.PSr
        ident = self.ident[:]; rid = self.r_ident
        identf = self.identf[:]
        g2 = self.alloc([128, D], F32, "g2"); b2 = self.alloc([128, D], F32, "b2")
        self.bcast_load(g2, I["ln_ffn_g"][l], D, "g2"); self.bcast_load(b2, I["ln_ffn_b"][l], D, "b2")
        cin = self.alloc([44, 4, 128], F32, "cin")
        self.dma(cin.ap[:, 0:3, :], I["ffn_conv_w"][l].rearrange("j (c p) -> c j p", p=128), "cin_w", (), [cin.r])
        self.dma(cin.ap[:, 3, :], I["ffn_conv_b"][l].rearrange("(c p) -> c p", p=128), "cin_b", (), [cin.r])
        cwf = self.alloc([128, 4, 44], F32, "cwf")
        pf = self.psb(7)
        for t in range(4):
            S.op("pe", (lambda o, i_: (lambda e: e.transpose(out=o, in_=i_, identity=identf[0:44, 0:44])))(
                pf[:, t * 44:(t + 1) * 44], cin.ap[:, t, :]), [cin.r, rid], [PSr[7]])
        self.cp("dve", cwf.ap, pf[:, 0:176].rearrange("p (t c) -> p t c", t=4), [PSr[7]], [cwf.r])
        hT = self.alloc([128, 8, 513], BF16, "hT")
        one1 = self.alloc([1, 1], BF16, "one1")
        self.memset("dve", one1.ap, 1.0, [one1.r])
        G = self.alloc([128, NJ, 512], BF16, "G")
        G_r = [Res(f"G{j}") for j in range(NJ)]
        wup = [self.alloc([128, 8, 256], BF16, "wup") for _ in range(3)]
        wdn = [self.alloc([128, D], BF16, "wdn") for _ in range(3)]
        U = [[self.alloc([128, 514], F32, "U") for _ in range(2)] for _ in range(2)]
        Y = [[self.alloc([128, 512], F32, "Y") for _ in range(2)] for _ in range(2)]
        hb = Buf(Y[1][1].ap.bitcast(BF16), Y[1][1].r)
        hrow = Buf(Y[1][0].ap.bitcast(BF16)[0:1, 0:D], Y[1][0].r)
        carry = self.alloc([128, 44, 2], F32, "carry")
        self.memset("dve", carry.ap, 0.0, [carry.r])
        st = [self.alloc([128, 32], F32, "st") for _ in range(4)]
        cnt = {"wi": 0, "di": 0}

        def htgen(c):
            for m in range(4):
                i = c * 4 + m
                self.cp("act", hb.ap, self.H[:, i, :], [self.Hr[i]], [hb.r])
                bk = 2 + (m % 2)
                pb = self.psb_bf(bk)
                for k in range(8):
                    self.tr(pb[:, k * 128:(k + 1) * 128], hb.ap[:, k * 128:(k + 1) * 128], ident, [hb.r, rid], [PSr[bk]])
                self.cp("dve", hT.ap[:, :, m * 128:(m + 1) * 128], pb.rearrange("p (k t) -> p k t", k=8), [PSr[bk]], [hT.r])
            if c < NQB - 1:
                i = c * 4 + 4
                self.cp("act", hrow.ap, self.H[0:1, i, :], [self.Hr[i]], [hrow.r])
                p2 = self.psb(2)
                for k in range(8):
                    self.mm(p2[:, k:k + 1], hrow.ap[0:1, k * 128:(k + 1) * 128], one1.ap, True, True, [hrow.r, one1.r], [PSr[2]])
                self.cp("act", hT.ap[:, :, 512:513], p2[:, 0:8].unsqueeze(2), [PSr[2]], [hT.r])
            else:
                self.memset("dve", hT.ap[:, :, 512:513], 0.0, [hT.r])

        def up(c):
            for j in range(NJ):
                wi = cnt["wi"]; cnt["wi"] += 1
                wb = wup[wi % 3]
                self.dma(wb.ap, self.wup_d[l][j], f"wupld{wi % 3}", [self.r_wup[l][j]], [wb.r])
                par = j % 2
                for gv in range(2):
                    jj = gv * NJ + j
                    bank = 2 * par + gv
                    pbk = self.psb(bank)
                    for k in range(8):
                        self.mm(pbk, wb.ap[:, k, gv * 128:(gv + 1) * 128], hT.ap[:, k, 1:513], k == 0, k == 7,
                                [wb.r, hT.r], [PSr[bank]])
                    Ub = U[par][gv]; Yb = Y[par][gv]
                    if c == 0:
                        p7 = self.psb(7)
                        for k in range(8):
                            self.mm(p7[:, jj:jj + 1], wb.ap[:, k, gv * 128:(gv + 1) * 128], hT.ap[:, k, 0:1], k == 0, k == 7,
                                    [wb.r, hT.r], [PSr[7]])
                        self.memset("pool", Ub.ap[:, 0:1], 0.0, [Ub.r])
                        self.cp("act", Ub.ap[:, 1:2], p7[:, jj:jj + 1], [PSr[7]], [Ub.r])
                    else:
                        self.cp("pool", Ub.ap[:, 0:2], carry.ap[:, jj, :], [carry.r], [Ub.r])
                    self.cp("act", Ub.ap[:, 2:514], pbk, [PSr[bank]], [Ub.r])
                    self.cp("act", carry.ap[:, jj, :], pbk[:, 510:512], [PSr[bank]], [carry.r])
                    self.ts("pool", Yb.ap, Ub.ap[:, 1:513], cwf.ap[:, 1, jj:jj + 1], cwf.ap[:, 3, jj:jj + 1], ALU.mult, ALU.add,
                            [Ub.r, cwf.r], [Yb.r])
                    self.stt("dve", Yb.ap, Ub.ap[:, 0:512], cwf.ap[:, 0, jj:jj + 1], Yb.ap, ALU.mult, ALU.add,
                             [Ub.r, cwf.r, Yb.r], [Yb.r])
                    self.stt("dve", Yb.ap, Ub.ap[:, 2:514], cwf.ap[:, 2, jj:jj + 1], Yb.ap, ALU.mult, ALU.add,
                             [Ub.r, cwf.r, Yb.r], [Yb.r])
                Yg = Y[par][0]; Yv = Y[par][1]
                self.actf(Yg.ap, Yg.ap, AF.Silu, [Yg.r], [Yg.r])
                self.tt("dve", G.ap[:, j, :], Yg.ap, Yv.ap, ALU.mult, [Yg.r, Yv.r], [G_r[j]])

        def down(c):
            for k in range(NJ):
                di = cnt["di"]; cnt["di"] += 1
                wd = wdn[di % 3]
                self.dma(wd.ap, self.wdn_d[l][k], f"wdnld{di % 3}", [self.r_wdn[l][k]], [wd.r])
                for m in range(4):
                    for half in range(2):
                        bank = (4 + 2 * m + half) % 8
                        self.mm(self.psb(bank), G.ap[:, k, m * 128:(m + 1) * 128], wd.ap[:, half * 512:(half + 1) * 512],
                                k == 0, k == NJ - 1, [G_r[k], wd.r], [PSr[bank]])

        def epi(c, tiles):
            ln_items = []
            for m in tiles:
                i = c * 4 + m
                Hi = self.H[:, i, :]
                for half in range(2):
                    bank = (4 + 2 * m + half) % 8
                    hs = slice(half * 512, (half + 1) * 512)
                    self.stt("dve", Hi[:, hs], Hi[:, hs], ALPHA, self.psb(bank), ALU.mult, ALU.add,
                             [self.Hr[i], PSr[bank]], [self.Hr[i]])
                ln_items.append((Hi, self.Hr[i], st[m]))
            self.layernorm_tiles(ln_items, g2, b2)

        htgen(0)
        for c in range(NQB):
            up(c)
            down(c)
            epi(c, [2, 3])
            if c + 1 < NQB:
                htgen(c + 1)
            epi(c, [0, 1])

    def final_store(self):
        for i in range(self.NT):
            self.dma(self.out[i * 128:(i + 1) * 128, :], self.H[:, i, :], f"ost{i % 3}", [self.Hr[i]], ())


@contextlib.contextmanager
def nc_allow(nc):
    with nc.allow_non_contiguous_dma(reason="tiny strided parameter load"):
        yield


def rope_table(SEQ):
    t = np.arange(SEQ)
    row = (t // GRID_W).astype(np.float64)
    col = (t % GRID_W).astype(np.float64)
    invB = ROPE_THETA ** (-np.arange(16, dtype=np.float64) / 16)
    invD = ROPE_THETA ** (-np.arange(8, dtype=np.float64) / 8)
    f32 = np.float32
    angB_r = (row.astype(f32)[:, None] * invB.astype(f32)[None, :]).astype(f32)
    angB_c = (col.astype(f32)[:, None] * invB.astype(f32)[None, :]).astype(f32)
    angD_r = (row.astype(f32)[:, None] * invD.astype(f32)[None, :]).astype(f32)
    angD_c = (col.astype(f32)[:, None] * invD.astype(f32)[None, :]).astype(f32)
    tbl = np.concatenate([
        np.cos(angB_r.astype(np.float64)), np.cos(angB_c.astype(np.float64)),
        np.sin(angB_r.astype(np.float64)), np.sin(angB_c.astype(np.float64)),
        np.cos(angD_r.astype(np.float64)), np.cos(angD_c.astype(np.float64)),
        np.sin(angD_r.astype(np.float64)), np.sin(angD_c.astype(np.float64)),
    ], axis=1).astype(np.float32)
    return np.ascontiguousarray(tbl)


_CACHE = {}


def get_program(SEQ, debug=False, stop_after=None):
    key = (SEQ, debug, stop_after)
    if key not in _CACHE:
        b = Builder(SEQ, debug=debug, stop_after=stop_after)
        nc, info = b.build()
        _CACHE[key] = (nc, info)
    return _CACHE[key]


def kernel(**inputs):
    x = np.asarray(inputs["x"], dtype=np.float32)
    B, SEQ, _ = x.shape
    nc, info = get_program(SEQ)
    tbl = rope_table(SEQ)
    shared = {k: np.ascontiguousarray(np.asarray(v, dtype=np.float32)) for k, v in inputs.items() if k != "x"}
    shared["rope_tbl"] = tbl
    in_maps = []
    for b in range(B):
        m = dict(shared)
        m["x"] = np.ascontiguousarray(x[b])
        in_maps.append(m)
    res = run_bass_kernel_spmd(nc, in_maps, core_ids=list(range(B)))
    out = np.stack([np.asarray(r["out"], dtype=np.float32) for r in res.results], axis=0)
    return out
```

```python
import contextlib
import numpy as np
import concourse.bass as bass
import concourse.mybir as mybir
from concourse.bass_utils import run_bass_kernel_spmd

F32 = mybir.dt.float32
BF16 = mybir.dt.bfloat16
AF = mybir.ActivationFunctionType
ALU = mybir.AluOpType
AX = mybir.AxisListType

D = 1024
DEPTH = 2
GW = 256
HD = 64
DIN = 1888
DFF = 2816
NJ = DFF // 128
ALPHA = float((2 * DEPTH) ** 0.25)
LN_EPS = 1e-5
RMS_EPS = 1e-6
ROPE_THETA = 10000.0
GRID_W = 64

ENGS = ("pe", "act", "dve", "pool", "sp")
EPOCH = 4000


class Res:
    __slots__ = ("name", "last_w", "readers")

    def __init__(self, name="r"):
        self.name = name
        self.last_w = None
        self.readers = []


class Op:
    __slots__ = ("eng", "fn", "deps", "is_dma", "lane", "sig", "cnt", "barrier", "idx")

    def __init__(self, eng, fn, is_dma=False, lane=None):
        self.eng = eng
        self.fn = fn
        self.deps = []
        self.is_dma = is_dma
        self.lane = lane
        self.sig = is_dma
        self.cnt = None
        self.barrier = False
        self.idx = -1


class Sched:
    def __init__(self, nc):
        self.nc = nc
        self.ops = []
        self.last_op = {e: None for e in ENGS}
        self.last_lane = {}

    def _dep(self, op, reads, writes):
        deps = set()
        for r in reads:
            if r.last_w is not None:
                deps.add(r.last_w)
        for w in writes:
            if w.last_w is not None:
                deps.add(w.last_w)
            latest = {}
            for rd in w.readers:
                if rd.is_dma:
                    deps.add(rd)
                else:
                    cur = latest.get(rd.eng)
                    if cur is None or rd.idx > cur.idx:
                        latest[rd.eng] = rd
            for rd in latest.values():
                deps.add(rd)
        deps.discard(op)
        for d in deps:
            if d.eng == "pe" and op.eng == "pe" and not d.is_dma and not op.is_dma:
                continue
            op.deps.append(d)
            d.sig = True
        for r in reads:
            r.readers.append(op)
        for w in writes:
            w.last_w = op
            w.readers = []

    def op(self, eng, fn, reads=(), writes=()):
        o = Op(eng, fn)
        o.idx = len(self.ops)
        self.ops.append(o)
        self._dep(o, reads, writes)
        self.last_op[eng] = o
        return o

    def dma(self, queue, fn, lane, reads=(), writes=()):
        o = Op(queue, fn, is_dma=True, lane=lane)
        o.idx = len(self.ops)
        self.ops.append(o)
        self._dep(o, reads, writes)
        prev = self.last_lane.get(lane)
        if prev is not None and prev not in o.deps:
            o.deps.append(prev)
        self.last_lane[lane] = o
        return o

    def barrier(self):
        pend = list(self.last_lane.values())
        for e in ENGS:
            if self.last_op[e] is not None:
                self.last_op[e].sig = True
                pend.append(self.last_op[e])
        for e in ENGS:
            o = Op(e, None)
            o.barrier = True
            o.deps = list(pend)
            self.ops.append(o)

    def emit(self):
        nc = self.nc
        eng_cnt = {e: 0 for e in ENGS}
        lane_cnt = {}
        for o in self.ops:
            if o.barrier:
                continue
            if o.is_dma:
                lane_cnt[o.lane] = lane_cnt.get(o.lane, 0) + 1
                o.cnt = lane_cnt[o.lane]
            elif o.sig:
                eng_cnt[o.eng] += 1
                o.cnt = eng_cnt[o.eng]
        sems = {}
        with contextlib.ExitStack() as stack:
            for e in ENGS:
                n_ep = max((eng_cnt[e] + EPOCH - 1) // EPOCH, 1)
                for k in range(n_ep):
                    sems[(e, k)] = stack.enter_context(nc.semaphore(f"s_{e}_{k}"))
            for i, ln in enumerate(lane_cnt):
                sems[("lane", ln)] = stack.enter_context(nc.semaphore(f"l_{i}"))

            def semof(o):
                if o.is_dma:
                    return sems[("lane", o.lane)], 16 * o.cnt, ("lane", o.lane)
                k = (o.cnt - 1) // EPOCH
                return sems[(o.eng, k)], o.cnt - k * EPOCH, (o.eng, k)

            per_eng = {e: [] for e in ENGS}
            for o in self.ops:
                per_eng[o.eng].append(o)
            final_waits = [(sems[("lane", ln)], 16 * c) for ln, c in lane_cnt.items()]

            def run(e, eng):
                seen = {}
                for o in per_eng[e]:
                    for d in o.deps:
                        s, v, key = semof(d)
                        if seen.get(key, 0) >= v:
                            continue
                        seen[key] = v
                        eng.wait_ge(s, v)
                    if o.barrier:
                        continue
                    ins = o.fn(eng)
                    if o.is_dma:
                        s, v, key = semof(o)
                        ins.then_inc(s, 16)
                    elif o.sig:
                        s, v, key = semof(o)
                        ins.then_inc(s, 1)
                if e == "sp":
                    for s, v in final_waits:
                        eng.wait_ge(s, v)

            with nc.Block() as block:
                @block.tensor
                def _(eng):
                    run("pe", eng)

                @block.scalar
                def _(eng):
                    run("act", eng)

                @block.vector
                def _(eng):
                    run("dve", eng)

                @block.gpsimd
                def _(eng):
                    run("pool", eng)

                @block.sync
                def _(eng):
                    run("sp", eng)
        return eng_cnt, len(lane_cnt)


class Buf:
    __slots__ = ("ap", "r")

    def __init__(self, ap, r):
        self.ap = ap
        self.r = r


DT_SIZE = {F32: 4, BF16: 2}


class Builder:
    def __init__(self, SEQ, debug=False, stop_after=None):
        self.SEQ = SEQ
        self.NT = SEQ // 128
        self.NQB = SEQ // 512
        self.debug = debug
        self.stop_after = stop_after
        nc = bass.Bass("TRN2", target_bir_lowering=False)
        self.nc = nc
        self.S = Sched(nc)
        self._uid = 0

    def mm(self, out, lhsT, rhs, start, stop, reads, writes):
        self.S.op("pe", lambda e: e.matmul(out, lhsT=lhsT, rhs=rhs, start=start, stop=stop), reads, writes)

    def tr(self, out, in_, ident, reads, writes):
        self.S.op("pe", lambda e: e.transpose(out=out, in_=in_, identity=ident), reads, writes)

    def actf(self, out, in_, func, reads, writes, bias=None, scale=None, accum_out=None):
        kw = {}
        if bias is not None:
            kw["bias"] = bias
        if scale is not None:
            kw["scale"] = scale
        if accum_out is not None:
            kw["accum_out"] = accum_out
        self.S.op("act", lambda e: e.activation(out=out, in_=in_, func=func, **kw), reads, writes)

    def tt(self, eng, out, in0, in1, op, reads, writes):
        self.S.op(eng, lambda e: e.tensor_tensor(out=out, in0=in0, in1=in1, op=op), reads, writes)

    def ts(self, eng, out, in0, s1, s2, op0, op1, reads, writes):
        if s2 is None:
            self.S.op(eng, lambda e: e.tensor_scalar(out=out, in0=in0, scalar1=s1, scalar2=None, op0=op0), reads, writes)
        else:
            self.S.op(eng, lambda e: e.tensor_scalar(out=out, in0=in0, scalar1=s1, scalar2=s2, op0=op0, op1=op1), reads, writes)

    def stt(self, eng, out, in0, scalar, in1, op0, op1, reads, writes):
        self.S.op(eng, lambda e: e.scalar_tensor_tensor(out=out, in0=in0, scalar=scalar, in1=in1, op0=op0, op1=op1), reads, writes)

    def cp(self, eng, out, in_, reads, writes):
        if eng == "act":
            self.S.op("act", lambda e: e.copy(out=out, in_=in_), reads, writes)
        else:
            self.S.op(eng, lambda e: e.tensor_copy(out=out, in_=in_), reads, writes)

    def memset(self, eng, ap, val, writes):
        self.S.op(eng, lambda e: e.memset(ap, val), (), writes)

    def dma(self, out, in_, lane, reads, writes, q="sp", nc_ok=False):
        if nc_ok:
            nc = self.nc

            def f(e):
                with nc.allow_non_contiguous_dma(reason="tiny strided parameter load"):
                    return e.dma_start(out=out, in_=in_)
            self.S.dma(q, f, lane, reads, writes)
        else:
            self.S.dma(q, lambda e: e.dma_start(out=out, in_=in_), lane, reads, writes)

    def arena_reset(self):
        self.S.barrier()
        self.aoff = 0

    def alloc(self, shape, dt, name="b"):
        n = 1
        for s in shape[1:]:
            n *= s
        nbytes = n * DT_SIZE[dt]
        nbytes = (nbytes + 63) // 64 * 64
        off = self.aoff
        self.aoff += nbytes
        assert self.aoff <= self.ARENA_BYTES, (name, self.aoff, self.ARENA_BYTES)
        if dt == F32:
            v = self.arena[:, off // 4: off // 4 + n]
        else:
            v = self.arena_bf[:, off // 2: off // 2 + n]
        if len(shape) == 3:
            v = v.rearrange("p (a b) -> p a b", a=shape[1])
        elif len(shape) == 4:
            v = v.rearrange("p (a b c) -> p a b c", a=shape[1], b=shape[2])
        if shape[0] < 128:
            v = v[0:shape[0]]
        self._uid += 1
        return Buf(v, Res(f"{name}{self._uid}"))

    def build(self):
        nc, SEQ, NT = self.nc, self.SEQ, self.NT
        L = DEPTH
        ext = lambda name, shape, dt=F32: nc.dram_tensor(name, shape, dt, kind="ExternalInput").ap()
        I = {}
        I["x"] = ext("x", [SEQ, D])
        I["ln_in_g"] = ext("ln_in_g", [D]); I["ln_in_b"] = ext("ln_in_b", [D])
        I["w_in"] = ext("w_in", [L, D, DIN])
        I["conv_a_w"] = ext("conv_a_w", [L, 31, GW]); I["conv_a_b"] = ext("conv_a_b", [L, GW])
        I["ln_a_g"] = ext("ln_a_g", [L, GW]); I["ln_a_b"] = ext("ln_a_b", [L, GW])
        I["qk_norm_q"] = ext("qk_norm_q", [L, HD]); I["qk_norm_k"] = ext("qk_norm_k", [L, HD])
        I["sgu_ln_g"] = ext("sgu_ln_g", [L, GW]); I["sgu_ln_b"] = ext("sgu_ln_b", [L, GW])
        I["sgu_w"] = ext("sgu_w", [L, 4, 128, 128]); I["sgu_b"] = ext("sgu_b", [L, 4, 128])
        I["mla_q_norm"] = ext("mla_q_norm", [L, 192]); I["mla_w_uq"] = ext("mla_w_uq", [L, 192, 384])
        I["mla_kv_norm"] = ext("mla_kv_norm", [L, 128]); I["mla_w_ukv"] = ext("mla_w_ukv", [L, 128, 512])
        I["w_out"] = ext("w_out", [L, D, D])
        I["ln_mix_g"] = ext("ln_mix_g", [L, D]); I["ln_mix_b"] = ext("ln_mix_b", [L, D])
        I["ffn_w_up"] = ext("ffn_w_up", [L, D, 2 * DFF])
        I["ffn_conv_w"] = ext("ffn_conv_w", [L, 3, 2 * DFF]); I["ffn_conv_b"] = ext("ffn_conv_b", [L, 2 * DFF])
        I["ffn_w_down"] = ext("ffn_w_down", [L, DFF, D])
        I["ln_ffn_g"] = ext("ln_ffn_g", [L, D]); I["ln_ffn_b"] = ext("ln_ffn_b", [L, D])
        I["rope_tbl"] = ext("rope_tbl", [SEQ, 96])
        self.I = I
        self.out = nc.dram_tensor("out", [SEQ, D], F32, kind="ExternalOutput").ap()

        skind = "ExternalOutput" if self.debug else "Internal"
        scr = lambda name, shape, dt=BF16: nc.dram_tensor(name, shape, dt, kind=skind).ap()
        self.aT_d = scr("aT_d", [2, 128, SEQ + 32]); self.r_aT = Res("aT_d")
        self.QT_d = scr("QT_d", [4, 64, SEQ]); self.r_QT = Res("QT_d")
        self.KT_d = scr("KT_d", [2, 64, SEQ]); self.r_KT = Res("KT_d")
        self.V_d = scr("V_d", [SEQ, 2, 64]); self.r_V = Res("V_d")
        self.QfT_d = scr("QfT_d", [4, 96, SEQ]); self.r_QfT = Res("QfT_d")
        self.KfT_d = scr("KfT_d", [4, 96, SEQ]); self.r_KfT = Res("KfT_d")
        self.Vd_d = scr("Vd_d", [SEQ, 4, 64]); self.r_Vd = Res("Vd_d")
        self.OT_d = scr("OT_d", [8, 128, SEQ]); self.r_OT = [Res(f"OT{c}") for c in range(8)]
        wk = "Internal"
        self.wup_d = [nc.dram_tensor(f"wup_d{l}", [NJ, 128, 8, 256], BF16, kind=wk).ap() for l in range(L)]
        self.wdn_d = [nc.dram_tensor(f"wdn_d{l}", [NJ, 128, D], BF16, kind=wk).ap() for l in range(L)]
        self.r_wup = [[Res(f"wup{l}_{j}") for j in range(NJ)] for l in range(L)]
        self.r_wdn = [[Res(f"wdn{l}_{j}") for j in range(NJ)] for l in range(L)]

        self.H = nc.alloc_sbuf_tensor("H", [128, NT, D], F32)
        self.Hr = [Res(f"H{i}") for i in range(NT)]
        self.ident = nc.alloc_sbuf_tensor("ident", [128, 128], BF16)
        self.identf = nc.alloc_sbuf_tensor("identf", [128, 128], F32)
        self.r_ident = Res("ident")
        self.cst = nc.alloc_sbuf_tensor("cst", [128, 16], F32)
        self.r_cst = Res("cst")
        rem = nc.sbuf_bytes_remaining
        self.ARENA_BYTES = (rem - 512) // 256 * 256
        self.arena = nc.alloc_sbuf_tensor("arena", [128, self.ARENA_BYTES // 4], F32)
        self.arena_bf = self.arena.bitcast(BF16)
        self.aoff = 0
        self.PS = nc.alloc_psum_tensor("PS", [128, 8, 512], F32)
        self.PSr = [Res(f"ps{b}") for b in range(8)]
        self.PSbf = self.PS.bitcast(BF16)
        self.wcast_queue = []

        self.consts()
        self.phase0()
        if self.stop_after != "p0":
            for l in range(L):
                self.wcast_prepare(l)
                self.phase1(l)
                if self.stop_after == f"p1_{l}":
                    break
                self.phase2_attn(l)
                if self.stop_after == f"p2a_{l}":
                    break
                self.phase2c(l)
                if self.stop_after == f"p2_{l}":
                    break
                self.phase3(l)
                if self.stop_after == f"p3_{l}":
                    break
        self.final_store()
        info = self.S.emit()
        return nc, info

    def psb(self, b):
        return self.PS[:, b, :]

    def psb_bf(self, b):
        return self.PSbf[:, b, :] if len(self.PSbf.shape) == 3 else self.PSbf[:, b * 1024:(b + 1) * 1024]

    def consts(self):
        S = self.S
        identf, ident = self.identf, self.ident
        S.op("pool", lambda e: e.memset(identf[:], 0.0), (), [self.r_ident])
        S.op("pool", lambda e: e.affine_select(out=identf[:], in_=identf[:], pattern=[[-1, 128]],
                                                compare_op=ALU.not_equal, fill=1.0, base=0, channel_multiplier=1),
             [self.r_ident], [self.r_ident])
        S.op("dve", lambda e: e.tensor_copy(out=ident[:], in_=identf[:]), [self.r_ident], [self.r_ident])
        cst = self.cst
        S.op("pool", lambda e: e.memset(cst[:, 0:8], -0.5), (), [self.r_cst])
        S.op("pool", lambda e: e.memset(cst[:, 8:9], 1.0 / 192.0), (), [self.r_cst])
        S.op("pool", lambda e: e.memset(cst[:, 9:10], 1.0 / 128.0), (), [self.r_cst])

    def rstd_pool(self, out, in_, n, eps, reads, writes, scale=None):
        if scale is None:
            self.ts("pool", out, in_, eps, None, ALU.add, None, reads, writes)
        else:
            self.ts("pool", out, in_, scale, eps, ALU.mult, ALU.add, reads, writes)
        self.tt("pool", out, out, self.cst[:, 0:n], ALU.pow, list(writes) + [self.r_cst], writes)

    def bcast_load(self, buf, src_1d, n, lane):
        self.dma(buf.ap, src_1d.partition_broadcast(128), lane, (), [buf.r])

    def layernorm_tile(self, x_ap, x_res, out_ap, out_res, g, b, scr, eng_b="pool"):
        st = scr.ap
        r = scr.r
        S = self.S
        S.op("dve", lambda e: e.bn_stats(out=st[:, 0:6], in_=x_ap[:, 0:512]), [x_res], [r])
        S.op("dve", lambda e: e.bn_stats(out=st[:, 6:12], in_=x_ap[:, 512:1024]), [x_res], [r])
        S.op("dve", lambda e: e.bn_aggr(out=st[:, 12:14], in_=st[:, 0:12]), [r], [r])
        self.rstd_pool(st[:, 14:15], st[:, 13:14], 1, LN_EPS, [r], [r])
        self.stt("dve", st[:, 15:16], st[:, 12:13], -1.0, st[:, 14:15], ALU.mult, ALU.mult, [r], [r])
        self.actf(out_ap, x_ap, AF.Identity, [x_res, r], [out_res], bias=st[:, 15:16], scale=st[:, 14:15])
        self.tt("dve", out_ap, out_ap, g.ap, ALU.mult, [out_res, g.r], [out_res])
        self.tt(eng_b, out_ap, out_ap, b.ap, ALU.add, [out_res, b.r], [out_res])

    def layernorm_tiles(self, items, g, b, eng_b="dve"):
        S = self.S
        for x_ap, xr, scr in items:
            st = scr.ap; r = scr.r
            S.op("dve", (lambda o, i_: (lambda e: e.bn_stats(out=o, in_=i_)))(st[:, 0:6], x_ap[:, 0:512]), [xr], [r])
            S.op("dve", (lambda o, i_: (lambda e: e.bn_stats(out=o, in_=i_)))(st[:, 6:12], x_ap[:, 512:1024]), [xr], [r])
            S.op("dve", (lambda o, i_: (lambda e: e.bn_aggr(out=o, in_=i_)))(st[:, 12:14], st[:, 0:12]), [r], [r])
        for x_ap, xr, scr in items:
            st = scr.ap; r = scr.r
            self.rstd_pool(st[:, 14:15], st[:, 13:14], 1, LN_EPS, [r], [r])
        for x_ap, xr, scr in items:
            st = scr.ap; r = scr.r
            self.stt("dve", st[:, 15:16], st[:, 12:13], -1.0, st[:, 14:15], ALU.mult, ALU.mult, [r], [r])
        for x_ap, xr, scr in items:
            st = scr.ap; r = scr.r
            self.actf(x_ap, x_ap, AF.Identity, [xr, r], [xr], bias=st[:, 15:16], scale=st[:, 14:15])
        for x_ap, xr, scr in items:
            self.tt("dve", x_ap, x_ap, g.ap, ALU.mult, [xr, g.r], [xr])
        for k, (x_ap, xr, scr) in enumerate(items):
            self.tt(eng_b if k % 2 == 0 else "pool", x_ap, x_ap, b.ap, ALU.add, [xr, b.r], [xr])

    def phase0(self):
        self.arena_reset()
        g = self.alloc([128, D], F32, "g0"); b = self.alloc([128, D], F32, "b0")
        self.bcast_load(g, self.I["ln_in_g"], D, "p0g")
        self.bcast_load(b, self.I["ln_in_b"], D, "p0b")
        scrs = [self.alloc([128, 32], F32, "st") for _ in range(8)]
        for i0 in range(0, self.NT, 4):
            items = []
            for i in range(i0, i0 + 4):
                hi = self.H[:, i, :]
                self.dma(hi, self.I["x"][i * 128:(i + 1) * 128, :], f"Hld{i % 4}", (), [self.Hr[i]])
                items.append((hi, self.Hr[i], scrs[i % 8]))
            self.layernorm_tiles(items, g, b)

    def wcast_prepare(self, l):
        q = []
        wu = self.I["ffn_w_up"][l].rearrange("(k p) n -> p k n", p=128)
        for j in range(NJ):
            q.append((wu[:, :, j * 128:(j + 1) * 128], self.wup_d[l][j, :, :, 0:128], self.r_wup[l][j], True))
            q.append((wu[:, :, DFF + j * 128: DFF + (j + 1) * 128], self.wup_d[l][j, :, :, 128:256], self.r_wup[l][j], True))
        for j in range(NJ):
            q.append((self.I["ffn_w_down"][l, j * 128:(j + 1) * 128, :], self.wdn_d[l][j], self.r_wdn[l][j], False))
        self.wcast_queue = q

    def wcast_step(self, bufs, n=1):
        for _ in range(n):
            if not self.wcast_queue:
                return
            src, dst, res, is3 = self.wcast_queue.pop(0)
            k = self._wc_i = getattr(self, "_wc_i", 0) + 1
            fb, bb = bufs[k % len(bufs)]
            if is3:
                self.dma(fb.ap.rearrange("p (k n) -> p k n", k=8), src, f"wc_in{k % len(bufs)}", (), [fb.r])
            else:
                self.dma(fb.ap, src, f"wc_in{k % len(bufs)}", (), [fb.r])
            self.cp("pool", bb.ap, fb.ap, [fb.r], [bb.r])
            if is3:
                self.dma(dst, bb.ap.rearrange("p (k n) -> p k n", k=8), f"wc_out{k % len(bufs)}", [bb.r], [res])
            else:
                self.dma(dst, bb.ap, f"wc_out{k % len(bufs)}", [bb.r], [res])

    def phase1(self, l):
        self._p1(l, "AB")
        self._p1(l, "CD")

    def _p1(self, l, part):
        I, S, NT = self.I, self.S, self.NT
        AB = part == "AB"
        self.arena_reset()
        ident = self.ident[:]
        rid = self.r_ident
        PSr = self.PSr
        wc0, wc1 = (0, 1024) if AB else (1024, DIN)
        winb = self.alloc([128, 8, wc1 - wc0], BF16, "winb")
        for k in range(8):
            self.dma(winb.ap[:, k, :], I["w_in"][l, k * 128:(k + 1) * 128, wc0:wc1], f"winb{k % 2}", (), [winb.r], q="pool")
        if AB:
            gqk = self.alloc([128, 6, 64], F32, "gqk")
            for h in range(4):
                self.dma(gqk.ap[:, h, :], I["qk_norm_q"][l].partition_broadcast(128), "gq", (), [gqk.r])
            for h in range(2):
                self.dma(gqk.ap[:, 4 + h, :], I["qk_norm_k"][l].partition_broadcast(128), "gk", (), [gqk.r])
            zb = self.alloc([128, 2, 16], BF16, "zb")
            self.memset("dve", zb.ap, 0.0, [zb.r])
            self.dma(self.aT_d[:, :, 0:16].rearrange("c p t -> p c t"), zb.ap, "zpad0", [zb.r], [self.r_aT])
            self.dma(self.aT_d[:, :, 16 + self.SEQ:32 + self.SEQ].rearrange("c p t -> p c t"), zb.ap, "zpad1", [zb.r], [self.r_aT])
        else:
            wuq = self.alloc([128, 2, 384], BF16, "wuq")
            self.dma(wuq.ap[:, 0, :], I["mla_w_uq"][l, 0:128, :], "wuq0", (), [wuq.r], q="pool")
            self.dma(wuq.ap[0:64, 1, :], I["mla_w_uq"][l, 128:192, :], "wuq1", (), [wuq.r], q="pool")
            wukv = self.alloc([128, 512], BF16, "wukv")
            self.dma(wukv.ap, I["mla_w_ukv"][l], "wukv", (), [wukv.r], q="pool")
            wsn = self.alloc([128, 4, 128], BF16, "wsn")
            self.dma(wsn.ap, I["sgu_w"][l].rearrange("g p q -> p g q"), "wsn", (), [wsn.r], q="pool")
            wsT = self.alloc([128, 4, 128], BF16, "wsT")
            pb = self.psb_bf(7)
            for g in range(4):
                self.tr(pb[:, g * 128:(g + 1) * 128], wsn.ap[:, g, :], ident, [wsn.r, rid], [self.PSr[7]])
            self.cp("dve", wsT.ap, pb[:, 0:512].rearrange("p (g q) -> p g q", g=4), [self.PSr[7]], [wsT.r])
            sgub = self.alloc([128, 4], F32, "sgub")
            self.dma(sgub.ap, I["sgu_b"][l].rearrange("g p -> p g"), "sgub", (), [sgub.r], nc_ok=True)
            gsg = self.alloc([128, 256], F32, "gsg"); bsg = self.alloc([128, 256], F32, "bsg")
            self.bcast_load(gsg, I["sgu_ln_g"][l], 256, "gsg"); self.bcast_load(bsg, I["sgu_ln_b"][l], 256, "bsg")
            gcq = self.alloc([128, 192], F32, "gcq"); gckv = self.alloc([128, 128], F32, "gckv")
            self.bcast_load(gcq, I["mla_q_norm"][l], 192, "gcq"); self.bcast_load(gckv, I["mla_kv_norm"][l], 128, "gckv")

        NB = 2
        mk = lambda shape, dt, nm: [self.alloc(shape, dt, nm) for _ in range(NB)]
        hb = mk([128, D], BF16, "hb"); hT = mk([128, 8, 128], BF16, "hT")
        tbl = mk([128, 96], F32, "tbl"); st = mk([128, 32], F32, "st")
        if AB:
            sgm = mk([128, 256], F32, "sgm"); a_bf = mk([128, 256], BF16, "a_bf"); aT = mk([128, 2, 128], BF16, "aT")
            qk_sb = mk([128, 6, 64], F32, "qk_sb"); sq = mk([128, 384], F32, "sq")
            ra = mk([128, 192], F32, "ra"); rb = mk([128, 192], F32, "rb")
            qk_bf = mk([128, 6, 64], BF16, "qk_bf"); qkT = mk([64, 6, 128], BF16, "qkT"); v_bf = mk([128, 2, 64], BF16, "v_bf")
        else:
            sqd = mk([128, 384], F32, "sqd"); rad = mk([128, 64], F32, "rad"); rbd = mk([128, 64], F32, "rbd")
            c_sb = mk([128, 512], F32, "c_sb"); svn = mk([128, 256], F32, "svn"); svn_bf = mk([128, 256], BF16, "svn_bf")
            oc_bf = mk([128, 256], BF16, "oc_bf"); ocT = mk([128, 2, 128], BF16, "ocT")
            d_sb = mk([128, 352], F32, "d_sb"); cqn = mk([128, 320], BF16, "cqn"); krr = mk([128, 32], BF16, "krr")
            cT = mk([128, 3, 128], BF16, "cT")
            qf = mk([128, 4, 96], BF16, "qf"); kf = mk([128, 4, 96], BF16, "kf"); vd = mk([128, 4, 64], BF16, "vd")
            qkfT = mk([96, 8, 128], BF16, "qkfT")

        def rope(x1, x2, cos, sin, o1, o2, ta, tb, rin, rtmp_a, rtmp_b, rout):
            self.tt("dve", ta, x1, cos, ALU.mult, rin, [rtmp_a])
            self.tt("dve", tb, x2, sin, ALU.mult, rin, [rtmp_b])
            self.tt("dve", o1, ta, tb, ALU.subtract, [rtmp_a, rtmp_b], [rout])
            self.tt("dve", ta, x2, cos, ALU.mult, rin, [rtmp_a])
            self.tt("dve", tb, x1, sin, ALU.mult, rin, [rtmp_b])
            self.tt("dve", o2, ta, tb, ALU.add, [rtmp_a, rtmp_b], [rout])

        def make_tile(i):
            s = i % NB
            Hi = self.H[:, i, :]
            tsl = slice(i * 128, (i + 1) * 128)
            bT = 4 + s if AB else 4
            b0, b1 = 2 * s, 2 * s + 1
            cols = [(0, 512), (512, wc1 - wc0)]
            def pro():
                yield
                self.dma(tbl[s].ap, I["rope_tbl"][tsl, :], f"tbl{s}", (), [tbl[s].r])
                bT = 4 + s if AB else 4
                yield
                self.cp("act", hb[s].ap, Hi, [self.Hr[i]], [hb[s].r])
                p4 = self.psb_bf(bT)
                yield
                for k in range(8):
                    self.tr(p4[:, k * 128:(k + 1) * 128], hb[s].ap[:, k * 128:(k + 1) * 128], ident, [hb[s].r, rid], [PSr[bT]])
                yield
                self.cp("dve" if AB else "act", hT[s].ap, p4.rearrange("p (k t) -> p k t", k=8), [PSr[bT]], [hT[s].r])
                b0, b1 = 2 * s, 2 * s + 1
                cols = [(0, 512), (512, wc1 - wc0)]
                yield
                for bnk, (c0, c1) in zip((b0, b1), cols):
                    for k in range(8):
                        self.mm(self.PS[:, bnk, 0:c1 - c0], hT[s].ap[:, k, :], winb.ap[:, k, c0:c1], k == 0, k == 7,
                                [hT[s].r, winb.r], [PSr[bnk]])
                yield
            stq = st[s].ap
            if AB:
                bX = 6 + s
                pA = self.psb(b0); pB = self.psb(b1); pX = self.psb_bf(bX)
            else:
                pC = self.psb(b0); pD = self.psb(b1)
                p5f = self.psb(5); p5 = self.psb_bf(5); p6f = self.psb(6); p6 = self.psb_bf(6)
                p7f = self.psb(7); p7 = self.psb_bf(7)

            def chainA():
                p0 = pA
                yield
                self.actf(sgm[s].ap, p0[:, 256:512], AF.Tanh, [PSr[b0]], [sgm[s].r], scale=0.5)
                yield
                self.ts("dve", sgm[s].ap, sgm[s].ap, 0.5, 0.5, ALU.mult, ALU.add, [sgm[s].r], [sgm[s].r])
                yield
                self.tt("dve", a_bf[s].ap, p0[:, 0:256], sgm[s].ap, ALU.mult, [PSr[b0], sgm[s].r], [a_bf[s].r])
                p5 = pX
                yield
                for c in range(2):
                    self.tr(p5[:, c * 128:(c + 1) * 128], a_bf[s].ap[:, c * 128:(c + 1) * 128], ident, [a_bf[s].r, rid], [PSr[bX]])
                yield
                self.cp("act", aT[s].ap, p5[:, 0:256].rearrange("p (c t) -> p c t", c=2), [PSr[bX]], [aT[s].r])
                yield
                self.dma(self.aT_d[:, :, 16 + i * 128: 16 + (i + 1) * 128].rearrange("c p t -> p c t"), aT[s].ap,
                         f"aT{s}", [aT[s].r], [self.r_aT])
                yield
            def chainB():
                p1 = pB
                qkf = qk_sb[s].ap.rearrange("p h d -> p (h d)")
                yield
                self.cp("act", qkf, p1[:, 0:384], [PSr[b1]], [qk_sb[s].r])
                yield
                self.cp("act", v_bf[s].ap.rearrange("p h d -> p (h d)"), p1[:, 384:512], [PSr[b1]], [v_bf[s].r])
                yield
                self.dma(self.V_d[tsl, :, :], v_bf[s].ap, f"v{s}", [v_bf[s].r], [self.r_V])
                yield
                self.tt("pool", sq[s].ap, qkf, qkf, ALU.mult, [qk_sb[s].r], [sq[s].r])
                stq = st[s].ap
                yield
                S.op("dve", (lambda o, i_: (lambda e: e.tensor_reduce(out=o, in_=i_, axis=AX.X, op=ALU.add)))(
                    stq[:, 0:6], sq[s].ap.rearrange("p (h d) -> p h d", h=6)), [sq[s].r], [st[s].r])
                yield
                self.rstd_pool(stq[:, 0:6], stq[:, 0:6], 6, RMS_EPS, [st[s].r], [st[s].r], scale=1.0 / 64.0)
                yield
                self.tt("dve", qk_sb[s].ap, qk_sb[s].ap, stq[:, 0:6].unsqueeze(2).broadcast_to([128, 6, 64]), ALU.mult,
                        [qk_sb[s].r, st[s].r], [qk_sb[s].r])
                yield
                self.tt("dve", qk_sb[s].ap, qk_sb[s].ap, gqk.ap, ALU.mult, [qk_sb[s].r, gqk.r], [qk_sb[s].r])
                xv = qk_sb[s].ap.rearrange("p h (r f e) -> p h r f e", r=2, f=2)
                ov = qk_bf[s].ap.rearrange("p h (r f e) -> p h r f e", r=2, f=2)
                cosB = tbl[s].ap[:, 0:32].rearrange("p (r e) -> p r e", r=2).unsqueeze(1).broadcast_to([128, 6, 2, 16])
                sinB = tbl[s].ap[:, 32:64].rearrange("p (r e) -> p r e", r=2).unsqueeze(1).broadcast_to([128, 6, 2, 16])
                ta = ra[s].ap.rearrange("p (h r e) -> p h r e", h=6, r=2)
                tb = rb[s].ap.rearrange("p (h r e) -> p h r e", h=6, r=2)
                yield
                rope(xv[:, :, :, 0, :], xv[:, :, :, 1, :], cosB, sinB, ov[:, :, :, 0, :], ov[:, :, :, 1, :], ta, tb,
                     [qk_sb[s].r, tbl[s].r], ra[s].r, rb[s].r, qk_bf[s].r)
                yield
                for h in range(6):
                    self.tr(pX[0:64, 256 + h * 128:256 + (h + 1) * 128], qk_bf[s].ap[:, h, :], ident, [qk_bf[s].r, rid], [PSr[bX]])
                yield
                self.cp("act", qkT[s].ap, pX[0:64, 256:1024].rearrange("p (h t) -> p h t", h=6), [PSr[bX]], [qkT[s].r])
                yield
                self.dma(self.QT_d[:, :, tsl].rearrange("h d t -> d h t"), qkT[s].ap[:, 0:4, :], f"qT{s}", [qkT[s].r], [self.r_QT])
                yield
                self.dma(self.KT_d[:, :, tsl].rearrange("h d t -> d h t"), qkT[s].ap[:, 4:6, :], f"kT{s}", [qkT[s].r], [self.r_KT])
                yield
            def chainC():
                yield
                self.actf(c_sb[s].ap, pC, AF.Gelu, [PSr[b0]], [c_sb[s].r])
                sv = c_sb[s].ap[:, 256:512]
                yield
                S.op("dve", (lambda o, i_: (lambda e: e.bn_stats(out=o, in_=i_)))(stq[:, 8:14], sv), [c_sb[s].r], [st[s].r])
                yield
                S.op("dve", (lambda o, i_: (lambda e: e.bn_aggr(out=o, in_=i_)))(stq[:, 14:16], stq[:, 8:14]), [st[s].r], [st[s].r])
                yield
                self.rstd_pool(stq[:, 16:17], stq[:, 15:16], 1, LN_EPS, [st[s].r], [st[s].r])
                yield
                self.stt("dve", stq[:, 17:18], stq[:, 14:15], -1.0, stq[:, 16:17], ALU.mult, ALU.mult, [st[s].r], [st[s].r])
                yield
                self.actf(svn[s].ap, sv, AF.Identity, [c_sb[s].r, st[s].r], [svn[s].r], bias=stq[:, 17:18], scale=stq[:, 16:17])
                yield
                self.tt("dve", svn[s].ap, svn[s].ap, gsg.ap, ALU.mult, [svn[s].r, gsg.r], [svn[s].r])
                yield
                self.tt("pool", svn_bf[s].ap, svn[s].ap, bsg.ap, ALU.add, [svn[s].r, bsg.r], [svn_bf[s].r])
                yield
                for g in range(4):
                    self.mm(p5f[:, g * 64:(g + 1) * 64], wsT.ap[:, g, :], svn_bf[s].ap[:, g * 64:(g + 1) * 64], True, True,
                            [wsT.r, svn_bf[s].r], [PSr[5]])
                yield
                for g in range(4):
                    self.stt("dve", oc_bf[s].ap[:, g * 64:(g + 1) * 64], p5f[:, g * 64:(g + 1) * 64], sgub.ap[:, g:g + 1],
                             c_sb[s].ap[:, g * 64:(g + 1) * 64], ALU.add, ALU.mult, [PSr[5], sgub.r, c_sb[s].r], [oc_bf[s].r])
                yield
                for c in range(2):
                    self.tr(p5[:, 512 + c * 128: 512 + (c + 1) * 128], oc_bf[s].ap[:, c * 128:(c + 1) * 128], ident,
                            [oc_bf[s].r, rid], [PSr[5]])
                yield
                self.cp("act", ocT[s].ap, p5[:, 512:768].rearrange("p (c t) -> p c t", c=2), [PSr[5]], [ocT[s].r])
                yield
                self.dma(self.OT_d[4:6, :, tsl].rearrange("c p t -> p c t"), ocT[s].ap, f"ocT{s}", [ocT[s].r],
                         [self.r_OT[4], self.r_OT[5]])
                yield
            def chainD():
                yield
                self.cp("act", d_sb[s].ap, pD[:, 0:352], [PSr[b1]], [d_sb[s].r])
                yield
                self.tt("pool", sqd[s].ap[:, 0:320], d_sb[s].ap[:, 0:320], d_sb[s].ap[:, 0:320], ALU.mult, [d_sb[s].r], [sqd[s].r])
                yield
                S.op("dve", (lambda o, i_: (lambda e: e.tensor_reduce(out=o, in_=i_, axis=AX.X, op=ALU.add)))(
                    stq[:, 20:21], sqd[s].ap[:, 0:192]), [sqd[s].r], [st[s].r])
                yield
                S.op("dve", (lambda o, i_: (lambda e: e.tensor_reduce(out=o, in_=i_, axis=AX.X, op=ALU.add)))(
                    stq[:, 21:22], sqd[s].ap[:, 192:320]), [sqd[s].r], [st[s].r])
                yield
                self.tt("dve", stq[:, 20:22], stq[:, 20:22], self.cst[:, 8:10], ALU.mult, [st[s].r, self.r_cst], [st[s].r])
                yield
                self.rstd_pool(stq[:, 20:22], stq[:, 20:22], 2, RMS_EPS, [st[s].r], [st[s].r])
                yield
                self.stt("dve", cqn[s].ap[:, 0:192], d_sb[s].ap[:, 0:192], stq[:, 20:21], gcq.ap, ALU.mult, ALU.mult,
                         [d_sb[s].r, st[s].r, gcq.r], [cqn[s].r])
                yield
                self.stt("dve", cqn[s].ap[:, 192:320], d_sb[s].ap[:, 192:320], stq[:, 21:22], gckv.ap, ALU.mult, ALU.mult,
                         [d_sb[s].r, st[s].r, gckv.r], [cqn[s].r])
                kx = d_sb[s].ap[:, 320:352].rearrange("p (r f e) -> p r f e", r=2, f=2)
                ko = krr[s].ap.rearrange("p (r f e) -> p r f e", r=2, f=2)
                cosD = tbl[s].ap[:, 64:80].rearrange("p (r e) -> p r e", r=2)
                sinD = tbl[s].ap[:, 80:96].rearrange("p (r e) -> p r e", r=2)
                ta2 = rad[s].ap[:, 0:16].rearrange("p (r e) -> p r e", r=2)
                tb2 = rbd[s].ap[:, 0:16].rearrange("p (r e) -> p r e", r=2)
                yield
                rope(kx[:, :, 0, :], kx[:, :, 1, :], cosD, sinD, ko[:, :, 0, :], ko[:, :, 1, :], ta2, tb2,
                     [d_sb[s].r, tbl[s].r], rad[s].r, rbd[s].r, krr[s].r)
                yield
                self.tr(p7[:, 768:896], cqn[s].ap[:, 0:128], ident, [cqn[s].r, rid], [PSr[7]])
                yield
                self.tr(p7[0:64, 896:1024], cqn[s].ap[:, 128:192], ident, [cqn[s].r, rid], [PSr[7]])
                yield
                self.tr(p5[:, 768:896], cqn[s].ap[:, 192:320], ident, [cqn[s].r, rid], [PSr[5]])
                yield
                self.cp("act", cT[s].ap[:, 0, :], p7[:, 768:896], [PSr[7]], [cT[s].r])
                yield
                self.cp("act", cT[s].ap[0:64, 1, :], p7[0:64, 896:1024], [PSr[7]], [cT[s].r])
                yield
                self.cp("act", cT[s].ap[:, 2, :], p5[:, 768:896], [PSr[5]], [cT[s].r])
                yield
                self.mm(p7f[:, 0:384], cT[s].ap[:, 0, :], wuq.ap[:, 0, :], True, False, [cT[s].r, wuq.r], [PSr[7]])
                yield
                self.mm(p7f[:, 0:384], cT[s].ap[0:64, 1, :], wuq.ap[0:64, 1, :], False, True, [cT[s].r, wuq.r], [PSr[7]])
                yield
                self.mm(p6f, cT[s].ap[:, 2, :], wukv.ap, True, True, [cT[s].r, wukv.r], [PSr[6]])
                qd = p7f[:, 0:384].rearrange("p (h d) -> p h d", h=4)
                kvd = p6f.rearrange("p (h d) -> p h d", h=4)
                yield
                self.cp("act", qf[s].ap[:, :, 0:64], qd[:, :, 0:64], [PSr[7]], [qf[s].r])
                qd_sb = sqd[s].ap[:, 0:384].rearrange("p (h d) -> p h d", h=4)
                yield
                self.cp("act", qd_sb[:, :, 64:96], qd[:, :, 64:96], [PSr[7]], [sqd[s].r])
                qx = qd_sb[:, :, 64:96].rearrange("p h (r f e) -> p h r f e", r=2, f=2)
                qo = qf[s].ap[:, :, 64:96].rearrange("p h (r f e) -> p h r f e", r=2, f=2)
                cosD4 = cosD.unsqueeze(1).broadcast_to([128, 4, 2, 8])
                sinD4 = sinD.unsqueeze(1).broadcast_to([128, 4, 2, 8])
                ta3 = rad[s].ap[:, 0:64].rearrange("p (h r e) -> p h r e", h=4, r=2)
                tb3 = rbd[s].ap[:, 0:64].rearrange("p (h r e) -> p h r e", h=4, r=2)
                yield
                rope(qx[:, :, :, 0, :], qx[:, :, :, 1, :], cosD4, sinD4, qo[:, :, :, 0, :], qo[:, :, :, 1, :], ta3, tb3,
                     [sqd[s].r, tbl[s].r], rad[s].r, rbd[s].r, qf[s].r)
                yield
                self.cp("act", kf[s].ap[:, :, 0:64], kvd[:, :, 0:64], [PSr[6]], [kf[s].r])
                yield
                self.cp("pool", kf[s].ap[:, :, 64:96], krr[s].ap.unsqueeze(1).broadcast_to([128, 4, 32]), [krr[s].r], [kf[s].r])
                yield
                self.cp("act", vd[s].ap, kvd[:, :, 64:128], [PSr[6]], [vd[s].r])
                yield
                self.dma(self.Vd_d[tsl, :, :], vd[s].ap, f"vd{s}", [vd[s].r], [self.r_Vd])
                yield
                for h in range(4):
                    self.tr(p6[0:96, h * 128:(h + 1) * 128], qf[s].ap[:, h, :], ident, [qf[s].r, rid], [PSr[6]])
                yield
                for h in range(4):
                    self.tr(p6[0:96, (4 + h) * 128:(5 + h) * 128], kf[s].ap[:, h, :], ident, [kf[s].r, rid], [PSr[6]])
                yield
                self.cp("act", qkfT[s].ap, p6[0:96, :].rearrange("p (h t) -> p h t", h=8), [PSr[6]], [qkfT[s].r])
                yield
                self.dma(self.QfT_d[:, :, tsl].rearrange("h d t -> d h t"), qkfT[s].ap[:, 0:4, :], f"qfT{s}", [qkfT[s].r], [self.r_QfT])
                yield
                self.dma(self.KfT_d[:, :, tsl].rearrange("h d t -> d h t"), qkfT[s].ap[:, 4:8, :], f"kfT{s}", [qkfT[s].r], [self.r_KfT])
                yield
            return pro(), ([chainA(), chainB()] if AB else [chainC(), chainD()])

        tiles = [make_tile(i) for i in range(NT)]
        for _ in tiles[0][0]:
            pass
        for i in range(NT):
            gens = list(tiles[i][1])
            if i + 1 < NT:
                gens.append(tiles[i + 1][0])
            while gens:
                for gch in list(gens):
                    try:
                        next(gch)
                    except StopIteration:
                        gens.remove(gch)

    def phase2_attn(self, l):
        SEQ, NT, NQB = self.SEQ, self.NT, self.NQB
        self.arena_reset()
        PSr = self.PSr
        wbufs = [(self.alloc([128, 1024], F32, "wcf"), self.alloc([128, 1024], BF16, "wcb")) for _ in range(2)]
        kt_sb = [self.alloc([128, SEQ], BF16, "kt") for _ in range(2)]
        for kb_ in kt_sb:
            self.memset("pool", kb_.ap[64:128, :], 0.0, [kb_.r])
        v_sb = [self.alloc([128, NT, 128], BF16, "v") for _ in range(2)]
        for vb in v_sb:
            self.memset("pool", vb.ap[:, :, 64:128], 1.0, [vb.r])
        q_sb = [self.alloc([128, 512], BF16, "q") for _ in range(2)]
        for qb_ in q_sb:
            self.memset("pool", qb_.ap[64:128, :], 0.0, [qb_.r])
        pT = [self.alloc([128, 2, 512], BF16, "pT") for _ in range(3)]
        rc = [self.alloc([128, 512], F32, "rc") for _ in range(2)]
        o_sb = [self.alloc([128, 512], BF16, "o") for _ in range(2)]
        passes = []
        for kvh in range(2):
            heads = []
            for g in range(2):
                hq = kvh * 2 + g
                heads.append((self.QT_d[hq], self.r_QT, 2 + hq // 2, (hq % 2) * 64))
            passes.append((self.KT_d[kvh], self.r_KT, self.V_d[:, kvh, :], self.r_V, 64, 64 ** -0.5, heads))
        for h in range(4):
            passes.append((self.KfT_d[h], self.r_KfT, self.Vd_d[:, h, :], self.r_Vd, 96, 96 ** -0.5,
                           [(self.QfT_d[h], self.r_QfT, 6 + h // 2, (h % 2) * 64)]))
        n_iter = sum(len(p[6]) for p in passes) * NQB
        per_it = (len(self.wcast_queue) + n_iter - 1) // n_iter
        groups = []
        it = 0
        for pi, (Ksrc, rK, Vsrc, rV, dk, scale, heads) in enumerate(passes):
            for qb in range(NQB):
                for hi, (Qsrc, rQ, chunk, base) in enumerate(heads):
                    for kg in range(NT // 2):
                        groups.append(dict(pi=pi, qb=qb, hi=hi, kg=kg, it=it, first=(kg == 0), last=(kg == NT // 2 - 1),
                                           pass_first=(qb == 0 and hi == 0 and kg == 0),
                                           prefetch=(qb == 0 and hi == 0 and kg == min(6, NT // 2 - 1))))
                    it += 1
        LOOK = 2
        NG = len(groups)

        def load_kv(pi):
            Ksrc, rK, Vsrc, rV, dk, scale, heads = passes[pi]
            kb = kt_sb[pi % 2]; vb = v_sb[pi % 2]
            self.dma(kb.ap[0:dk, :], Ksrc, f"ktld{pi % 2}", [rK], [kb.r])
            Vv = Vsrc.rearrange("(t p) d -> p t d", p=128)
            for t0 in range(0, NT, 8):
                self.dma(vb.ap[:, t0:t0 + 8, 0:64], Vv[:, t0:t0 + 8, :], f"vld{pi % 2}", [rV], [vb.r])

        def emit_qk(gi):
            g = groups[gi]
            Ksrc, rK, Vsrc, rV, dk, scale, heads = passes[g["pi"]]
            Qsrc, rQ, chunk, base = heads[g["hi"]]
            kb = kt_sb[g["pi"] % 2]; vb = v_sb[g["pi"] % 2]
            if g["pass_first"] and g["pi"] == 0:
                load_kv(0)
            if g["prefetch"] and g["pi"] + 1 < len(passes):
                load_kv(g["pi"] + 1)
            qs = q_sb[g["it"] % 2]
            if g["first"]:
                qsl = slice(g["qb"] * 512, (g["qb"] + 1) * 512)
                self.dma(qs.ap[0:dk, :], Qsrc[:, qsl], f"qld{g['it'] % 2}", [rQ], [qs.r])
                self.wcast_step(wbufs, per_it)
            r3 = gi % 3
            for u in range(2):
                ktile = g["kg"] * 2 + u
                self.mm(self.PS[:, 2 * r3 + u, :], kb.ap[:, ktile * 128:(ktile + 1) * 128], qs.ap[:, :],
                        True, True, [kb.r, qs.r], [PSr[2 * r3 + u]])
            self.actf(pT[r3].ap, self.PS[:, 2 * r3:2 * r3 + 2, :], AF.Exp, [PSr[2 * r3], PSr[2 * r3 + 1]],
                      [pT[r3].r], scale=float(scale))

        def emit_pv(gi):
            g = groups[gi]
            Ksrc, rK, Vsrc, rV, dk, scale, heads = passes[g["pi"]]
            Qsrc, rQ, chunk, base = heads[g["hi"]]
            vb = v_sb[g["pi"] % 2]
            r3 = gi % 3
            acc_b = 6 + g["it"] % 2
            acc = self.psb(acc_b)
            for u in range(2):
                ktile = g["kg"] * 2 + u
                self.mm(acc, vb.ap[:, ktile, :], pT[r3].ap[:, u, :], ktile == 0, ktile == NT - 1,
                        [vb.r, pT[r3].r], [PSr[acc_b]])
            if g["last"]:
                qsl = slice(g["qb"] * 512, (g["qb"] + 1) * 512)
                rcb = rc[g["it"] % 2]; ob = o_sb[g["it"] % 2]
                self.S.op("dve", (lambda o, i_: (lambda e: e.reciprocal(out=o, in_=i_)))(rcb.ap[64:128, :], acc[64:128, :]),
                          [PSr[acc_b]], [rcb.r])
                self.tt("dve", ob.ap[base:base + 64, :], acc[0:64, :], rcb.ap[64:128, :], ALU.mult,
                        [PSr[acc_b], rcb.r], [ob.r])
                self.dma(self.OT_d[chunk, base:base + 64, qsl], ob.ap[base:base + 64, :], f"ost{g['it'] % 2}", [ob.r],
                         [self.r_OT[chunk]])

        for gi in range(NG + LOOK):
            if gi < NG:
                emit_qk(gi)
            if gi >= LOOK:
                emit_pv(gi - LOOK)
        while self.wcast_queue:
            self.wcast_step(wbufs, 1)

    def phase2c(self, l):
        I, S, NT, NQB = self.I, self.S, self.NT, self.NQB
        self.arena_reset()
        PSr = self.PSr
        ident = self.ident[:]; rid = self.r_ident
        woutb = self.alloc([128, 8, D], BF16, "woutb")
        for k in range(8):
            self.dma(woutb.ap[:, k, :], I["w_out"][l, k * 128:(k + 1) * 128, :], f"woutb{k % 2}", (), [woutb.r], q="pool")
        g1 = self.alloc([128, D], F32, "g1"); b1 = self.alloc([128, D], F32, "b1")
        self.bcast_load(g1, I["ln_mix_g"][l], D, "g1"); self.bcast_load(b1, I["ln_mix_b"][l], D, "b1")
        cab = self.alloc([128, 256], F32, "cab")
        self.bcast_load(cab, I["conv_a_b"][l], 256, "cab")
        cw31 = self.alloc([31, 256], F32, "cw31")
        self.dma(cw31.ap, I["conv_a_w"][l], "cw31", (), [cw31.r])
        cw = self.alloc([128, 2, 32], F32, "cw")
        lgb = self.alloc([4, 256], F32, "lgb")
        self.dma(lgb.ap[0:2, 0:128], I["ln_a_g"][l].rearrange("(c p) -> c p", p=128), "lga", (), [lgb.r])
        self.dma(lgb.ap[0:2, 128:256], I["ln_a_b"][l].rearrange("(c p) -> c p", p=128), "lgbb", (), [lgb.r])
        lgT = self.alloc([128, 4], F32, "lgT")
        pf = self.psb(7)
        identf = self.identf[:]
        for c in range(2):
            S.op("pe", (lambda o, i_: (lambda e: e.transpose(out=o, in_=i_, identity=identf[0:31, 0:31])))(
                pf[:, c * 32:c * 32 + 31], cw31.ap[:, c * 128:(c + 1) * 128]), [cw31.r, rid], [PSr[7]])
        S.op("pe", (lambda o, i_: (lambda e: e.transpose(out=o, in_=i_, identity=identf[0:2, 0:2])))(
            pf[:, 64:66], lgb.ap[0:2, 0:128]), [lgb.r, rid], [PSr[7]])
        S.op("pe", (lambda o, i_: (lambda e: e.transpose(out=o, in_=i_, identity=identf[0:2, 0:2])))(
            pf[:, 66:68], lgb.ap[0:2, 128:256]), [lgb.r, rid], [PSr[7]])
        self.cp("dve", cw.ap, pf[:, 0:64].rearrange("p (c j) -> p c j", c=2), [PSr[7]], [cw.r])
        self.cp("dve", lgT.ap, pf[:, 64:68], [PSr[7]], [lgT.r])
        dg = self.alloc([128, 2 * 31 * 128], BF16, "dg")
        dgv = dg.ap.rearrange("p (c j q) -> p c j q", c=2, j=31)
        for c in range(2):
            self.tt("dve", dgv[:, c, :, :], self.identf[:].unsqueeze(1).broadcast_to([128, 31, 128]),
                    cw.ap[:, c, 0:31].unsqueeze(2).broadcast_to([128, 31, 128]), ALU.mult, [cw.r, rid], [dg.r])
        NB = 2
        aw = [self.alloc([128, 2, 542], BF16, "aw") for _ in range(NB)]
        ob = [self.alloc([128, 6, 512], BF16, "ob") for _ in range(NB)]
        xa = [self.alloc([128, 256], F32, "xa") for _ in range(NB)]
        xab = [self.alloc([128, 256], BF16, "xab") for _ in range(NB)]
        oaT = [self.alloc([128, 2, 128], BF16, "oaT") for _ in range(NB)]
        st = [self.alloc([128, 32], F32, "st") for _ in range(NB)]
        def make_tile(i):
            qb, m = divmod(i, 4)
            s = qb % NB
            u = i % NB
            qsl = slice(qb * 512, (qb + 1) * 512)
            cb = 4 + 2 * (i % 2)
            tbk = 5 + 2 * (i % 2)
            pb0 = 2 * (i % 2)
            Hi = self.H[:, i, :]
            stq = st[u].ap

            def front():
                if m == 0:
                    self.dma(aw[s].ap, self.aT_d[:, :, 1 + qb * 512: 1 + qb * 512 + 542].rearrange("c p t -> p c t"), f"aw{s}",
                             [self.r_aT], [aw[s].r])
                    self.dma(ob[s].ap, self.OT_d[2:8, :, qsl].rearrange("c p t -> p c t"), f"ob{s}",
                             self.r_OT[2:8], [ob[s].r])
                    yield
                p4 = self.psb(cb)
                for c in range(2):
                    for j in range(31):
                        self.mm(p4[:, c * 128:(c + 1) * 128], aw[s].ap[:, c, m * 128 + j: m * 128 + j + 128], dgv[:, c, j, :],
                                j == 0, j == 30, [aw[s].r, dg.r], [PSr[cb]])
                        if j % 8 == 7:
                            yield
                yield
                self.tt("dve", xa[u].ap, p4[:, 0:256], cab.ap, ALU.add, [PSr[cb], cab.r], [xa[u].r])
                yield
                S.op("dve", (lambda o, i_: (lambda e: e.bn_stats(out=o, in_=i_)))(stq[:, 16:22], xa[u].ap), [xa[u].r], [st[u].r])
                S.op("dve", (lambda o, i_: (lambda e: e.bn_aggr(out=o, in_=i_)))(stq[:, 22:24], stq[:, 16:22]), [st[u].r], [st[u].r])
                yield
                self.rstd_pool(stq[:, 24:25], stq[:, 23:24], 1, LN_EPS, [st[u].r], [st[u].r])
                yield
                self.stt("dve", stq[:, 25:26], stq[:, 22:23], -1.0, stq[:, 24:25], ALU.mult, ALU.mult, [st[u].r], [st[u].r])
                yield
                self.actf(xab[u].ap, xa[u].ap, AF.Identity, [xa[u].r, st[u].r], [xab[u].r], bias=stq[:, 25:26], scale=stq[:, 24:25])
                yield
                p5 = self.psb_bf(tbk)
                for c in range(2):
                    self.tr(p5[:, c * 128:(c + 1) * 128], xab[u].ap[:, c * 128:(c + 1) * 128], ident, [xab[u].r, rid], [PSr[tbk]])
                yield
                for c in range(2):
                    self.actf(oaT[u].ap[:, c, :], p5[:, c * 128:(c + 1) * 128], AF.Silu, [PSr[tbk], lgT.r], [oaT[u].r],
                              bias=lgT.ap[:, 2 + c:3 + c], scale=lgT.ap[:, c:c + 1])
                yield

            def back():
                for half in range(2):
                    for k in range(8):
                        lhsT = oaT[u].ap[:, k, :] if k < 2 else ob[s].ap[:, k - 2, m * 128:(m + 1) * 128]
                        rr = [oaT[u].r] if k < 2 else [ob[s].r]
                        self.mm(self.PS[:, pb0 + half, :], lhsT, woutb.ap[:, k, half * 512:(half + 1) * 512], k == 0, k == 7,
                                rr + [woutb.r], [PSr[pb0 + half]])
                    yield
                for half in range(2):
                    hs = slice(half * 512, (half + 1) * 512)
                    self.stt("dve", Hi[:, hs], Hi[:, hs], ALPHA, self.PS[:, pb0 + half, :], ALU.mult, ALU.add,
                             [self.Hr[i], PSr[pb0 + half]], [self.Hr[i]])
                    yield
                r = st[u].r
                S.op("dve", (lambda o, i_: (lambda e: e.bn_stats(out=o, in_=i_)))(stq[:, 0:6], Hi[:, 0:512]), [self.Hr[i]], [r])
                S.op("dve", (lambda o, i_: (lambda e: e.bn_stats(out=o, in_=i_)))(stq[:, 6:12], Hi[:, 512:1024]), [self.Hr[i]], [r])
                S.op("dve", (lambda o, i_: (lambda e: e.bn_aggr(out=o, in_=i_)))(stq[:, 12:14], stq[:, 0:12]), [r], [r])
                yield
                self.rstd_pool(stq[:, 14:15], stq[:, 13:14], 1, LN_EPS, [r], [r])
                yield
                self.stt("dve", stq[:, 15:16], stq[:, 12:13], -1.0, stq[:, 14:15], ALU.mult, ALU.mult, [r], [r])
                yield
                self.actf(Hi, Hi, AF.Identity, [self.Hr[i], r], [self.Hr[i]], bias=stq[:, 15:16], scale=stq[:, 14:15])
                yield
                self.tt("dve", Hi, Hi, g1.ap, ALU.mult, [self.Hr[i], g1.r], [self.Hr[i]])
                yield
                self.tt("pool" if i % 2 else "dve", Hi, Hi, b1.ap, ALU.add, [self.Hr[i], b1.r], [self.Hr[i]])
                yield
            return front(), back()

        tiles = [make_tile(i) for i in range(NT)]
        for _ in tiles[0][0]:
            pass
        for i in range(NT):
            gens = [tiles[i][1]]
            if i + 1 < NT:
                gens.append(tiles[i + 1][0])
            while gens:
                for gch in list(gens):
                    try:
                        next(gch)
                    except StopIteration:
                        gens.remove(gch)

    def phase3(self, l):
        I, S, NT, NQB, SEQ = self.I, self.S, self.NT, self.NQB, self.SEQ
        self.arena_reset()
        PSr = self.PSr
        ident = self.ident[:]; rid = self.r_ident
        identf = self.identf[:]
        g2 = self.alloc([128, D], F32, "g2"); b2 = self.alloc([128, D], F32, "b2")
        self.bcast_load(g2, I["ln_ffn_g"][l], D, "g2"); self.bcast_load(b2, I["ln_ffn_b"][l], D, "b2")
        cin = self.alloc([44, 4, 128], F32, "cin")
        self.dma(cin.ap[:, 0:3, :], I["ffn_conv_w"][l].rearrange("j (c p) -> c j p", p=128), "cin_w", (), [cin.r])
        self.dma(cin.ap[:, 3, :], I["ffn_conv_b"][l].rearrange("(c p) -> c p", p=128), "cin_b", (), [cin.r])
        cwf = self.alloc([128, 4, 44], F32, "cwf")
        pf = self.psb(7)
        for t in range(4):
            S.op("pe", (lambda o, i_: (lambda e: e.transpose(out=o, in_=i_, identity=identf[0:44, 0:44])))(
                pf[:, t * 44:(t + 1) * 44], cin.ap[:, t, :]), [cin.r, rid], [PSr[7]])
        self.cp("dve", cwf.ap, pf[:, 0:176].rearrange("p (t c) -> p t c", t=4), [PSr[7]], [cwf.r])
        hT = self.alloc([128, 8, 513], BF16, "hT")
        one1 = self.alloc([1, 1], BF16, "one1")
        self.memset("dve", one1.ap, 1.0, [one1.r])
        G = self.alloc([128, NJ, 512], BF16, "G")
        G_r = [Res(f"G{j}") for j in range(NJ)]
        wup = [self.alloc([128, 8, 256], BF16, "wup") for _ in range(3)]
        wdn = [self.alloc([128, D], BF16, "wdn") for _ in range(3)]
        U = [[self.alloc([128, 514], F32, "U") for _ in range(2)] for _ in range(2)]
        Y = [[self.alloc([128, 512], F32, "Y") for _ in range(2)] for _ in range(2)]
        hb = Buf(Y[1][1].ap.bitcast(BF16), Y[1][1].r)
        hrow = Buf(Y[1][0].ap.bitcast(BF16)[0:1, 0:D], Y[1][0].r)
        carry = self.alloc([128, 44, 2], F32, "carry")
        self.memset("dve", carry.ap, 0.0, [carry.r])
        st = [self.alloc([128, 32], F32, "st") for _ in range(4)]
        cnt = {"wi": 0, "di": 0}

        def htgen(c):
            for m in range(4):
                i = c * 4 + m
                self.cp("act", hb.ap, self.H[:, i, :], [self.Hr[i]], [hb.r])
                bk = 2 + (m % 2)
                pb = self.psb_bf(bk)
                for k in range(8):
                    self.tr(pb[:, k * 128:(k + 1) * 128], hb.ap[:, k * 128:(k + 1) * 128], ident, [hb.r, rid], [PSr[bk]])
                self.cp("dve", hT.ap[:, :, m * 128:(m + 1) * 128], pb.rearrange("p (k t) -> p k t", k=8), [PSr[bk]], [hT.r])
            if c < NQB - 1:
                i = c * 4 + 4
                self.cp("act", hrow.ap, self.H[0:1, i, :], [self.Hr[i]], [hrow.r])
                p2 = self.psb(2)
                for k in range(8):
                    self.mm(p2[:, k:k + 1], hrow.ap[0:1, k * 128:(k + 1) * 128], one1.ap, True, True, [hrow.r, one1.r], [PSr[2]])
                self.cp("act", hT.ap[:, :, 512:513], p2[:, 0:8].unsqueeze(2), [PSr[2]], [hT.r])
            else:
                self.memset("dve", hT.ap[:, :, 512:513], 0.0, [hT.r])

        def up(c):
            for j in range(NJ):
                wi = cnt["wi"]; cnt["wi"] += 1
                wb = wup[wi % 3]
                self.dma(wb.ap, self.wup_d[l][j], f"wupld{wi % 3}", [self.r_wup[l][j]], [wb.r])
                par = j % 2
                for gv in range(2):
                    jj = gv * NJ + j
                    bank = 2 * par + gv
                    pbk = self.psb(bank)
                    for k in range(8):
                        self.mm(pbk, wb.ap[:, k, gv * 128:(gv + 1) * 128], hT.ap[:, k, 1:513], k == 0, k == 7,
                                [wb.r, hT.r], [PSr[bank]])
                    Ub = U[par][gv]; Yb = Y[par][gv]
                    if c == 0:
                        p7 = self.psb(7)
                        for k in range(8):
                            self.mm(p7[:, jj:jj + 1], wb.ap[:, k, gv * 128:(gv + 1) * 128], hT.ap[:, k, 0:1], k == 0, k == 7,
                                    [wb.r, hT.r], [PSr[7]])
                        self.memset("pool", Ub.ap[:, 0:1], 0.0, [Ub.r])
                        self.cp("act", Ub.ap[:, 1:2], p7[:, jj:jj + 1], [PSr[7]], [Ub.r])
                    else:
                        self.cp("pool", Ub.ap[:, 0:2], carry.ap[:, jj, :], [carry.r], [Ub.r])
                    self.cp("act", Ub.ap[:, 2:514], pbk, [PSr[bank]], [Ub.r])
                    self.cp("act", carry.ap[:, jj, :], pbk[:, 510:512], [PSr[bank]], [carry.r])
                    self.ts("pool", Yb.ap, Ub.ap[:, 1:513], cwf.ap[:, 1, jj:jj + 1], cwf.ap[:, 3, jj:jj + 1], ALU.mult, ALU.add,
                            [Ub.r, cwf.r], [Yb.r])
                    self.stt("dve", Yb.ap, Ub.ap[:, 0:512], cwf.ap[:, 0, jj:jj + 1], Yb.ap, ALU.mult, ALU.add,
                             [Ub.r, cwf.r, Yb.r], [Yb.r])
                    self.stt("dve", Yb.ap, Ub.ap[:, 2:514], cwf.ap[:, 2, jj:jj + 1], Yb.ap, ALU.mult, ALU.add,
                             [Ub.r, cwf.r, Yb.r], [Yb.r])
                Yg = Y[par][0]; Yv = Y[par][1]
                self.actf(Yg.ap, Yg.ap, AF.Silu, [Yg.r], [Yg.r])
                self.tt("dve", G.ap[:, j, :], Yg.ap, Yv.ap, ALU.mult, [Yg.r, Yv.r], [G_r[j]])

        def down(c):
            for k in range(NJ):
                di = cnt["di"]; cnt["di"] += 1
                wd = wdn[di % 3]
                self.dma(wd.ap, self.wdn_d[l][k], f"wdnld{di % 3}", [self.r_wdn[l][k]], [wd.r])
                for m in range(4):
                    for half in range(2):
                        bank = (4 + 2 * m + half) % 8
                        self.mm(self.psb(bank), G.ap[:, k, m * 128:(m + 1) * 128], wd.ap[:, half * 512:(half + 1) * 512],
                                k == 0, k == NJ - 1, [G_r[k], wd.r], [PSr[bank]])

        def epi(c, tiles):
            ln_items = []
            for m in tiles:
                i = c * 4 + m
                Hi = self.H[:, i, :]
                for half in range(2):
                    bank = (4 + 2 * m + half) % 8
                    hs = slice(half * 512, (half + 1) * 512)
                    self.stt("dve", Hi[:, hs], Hi[:, hs], ALPHA, self.psb(bank), ALU.mult, ALU.add,
                             [self.Hr[i], PSr[bank]], [self.Hr[i]])
                ln_items.append((Hi, self.Hr[i], st[m]))
            self.layernorm_tiles(ln_items, g2, b2)

        htgen(0)
        for c in range(NQB):
            up(c)
            down(c)
            epi(c, [2, 3])
            if c + 1 < NQB:
                htgen(c + 1)
            epi(c, [0, 1])

    def final_store(self):
        for i in range(self.NT):
            self.dma(self.out[i * 128:(i + 1) * 128, :], self.H[:, i, :], f"ost{i % 3}", [self.Hr[i]], ())


@contextlib.contextmanager
def nc_allow(nc):
    with nc.allow_non_contiguous_dma(reason="tiny strided parameter load"):
        yield


def rope_table(SEQ):
    t = np.arange(SEQ)
    row = (t // GRID_W).astype(np.float64)
    col = (t % GRID_W).astype(np.float64)
    invB = ROPE_THETA ** (-np.arange(16, dtype=np.float64) / 16)
    invD = ROPE_THETA ** (-np.arange(8, dtype=np.float64) / 8)
    f32 = np.float32
    angB_r = (row.astype(f32)[:, None] * invB.astype(f32)[None, :]).astype(f32)
    angB_c = (col.astype(f32)[:, None] * invB.astype(f32)[None, :]).astype(f32)
    angD_r = (row.astype(f32)[:, None] * invD.astype(f32)[None, :]).astype(f32)
    angD_c = (col.astype(f32)[:, None] * invD.astype(f32)[None, :]).astype(f32)
    tbl = np.concatenate([
        np.cos(angB_r.astype(np.float64)), np.cos(angB_c.astype(np.float64)),
        np.sin(angB_r.astype(np.float64)), np.sin(angB_c.astype(np.float64)),
        np.cos(angD_r.astype(np.float64)), np.cos(angD_c.astype(np.float64)),
        np.sin(angD_r.astype(np.float64)), np.sin(angD_c.astype(np.float64)),
    ], axis=1).astype(np.float32)
    return np.ascontiguousarray(tbl)


_CACHE = {}


def get_program(SEQ, debug=False, stop_after=None):
    key = (SEQ, debug, stop_after)
    if key not in _CACHE:
        b = Builder(SEQ, debug=debug, stop_after=stop_after)
        nc, info = b.build()
        _CACHE[key] = (nc, info)
    return _CACHE[key]


def kernel(**inputs):
    x = np.asarray(inputs["x"], dtype=np.float32)
    B, SEQ, _ = x.shape
    nc, info = get_program(SEQ)
    tbl = rope_table(SEQ)
    shared = {k: np.ascontiguousarray(np.asarray(v, dtype=np.float32)) for k, v in inputs.items() if k != "x"}
    shared["rope_tbl"] = tbl
    in_maps = []
    for b in range(B):
        m = dict(shared)
        m["x"] = np.ascontiguousarray(x[b])
        in_maps.append(m)
    res = run_bass_kernel_spmd(nc, in_maps, core_ids=list(range(B)))
    out = np.stack([np.asarray(r["out"], dtype=np.float32) for r in res.results], axis=0)
    return out
```

```python
import contextlib
import numpy as np
import concourse.bass as bass
import concourse.mybir as mybir
from concourse.bass_utils import run_bass_kernel_spmd

F32 = mybir.dt.float32
BF16 = mybir.dt.bfloat16
AF = mybir.ActivationFunctionType
ALU = mybir.AluOpType
AX = mybir.AxisListType

D = 1024
DEPTH = 2
GW = 256
HD = 64
DIN = 1888
DFF = 2816
NJ = DFF // 128
ALPHA = float((2 * DEPTH) ** 0.25)
LN_EPS = 1e-5
RMS_EPS = 1e-6
ROPE_THETA = 10000.0
GRID_W = 64

ENGS = ("pe", "act", "dve", "pool", "sp")
EPOCH = 4000


class Res:
    __slots__ = ("name", "last_w", "readers")

    def __init__(self, name="r"):
        self.name = name
        self.last_w = None
        self.readers = []


class Op:
    __slots__ = ("eng", "fn", "deps", "is_dma", "lane", "sig", "cnt", "barrier", "idx")

    def __init__(self, eng, fn, is_dma=False, lane=None):
        self.eng = eng
        self.fn = fn
        self.deps = []
        self.is_dma = is_dma
        self.lane = lane
        self.sig = is_dma
        self.cnt = None
        self.barrier = False
        self.idx = -1


class Sched:
    def __init__(self, nc):
        self.nc = nc
        self.ops = []
        self.last_op = {e: None for e in ENGS}
        self.last_lane = {}

    def _dep(self, op, reads, writes):
        deps = set()
        for r in reads:
            if r.last_w is not None:
                deps.add(r.last_w)
        for w in writes:
            if w.last_w is not None:
                deps.add(w.last_w)
            latest = {}
            for rd in w.readers:
                if rd.is_dma:
                    deps.add(rd)
                else:
                    cur = latest.get(rd.eng)
                    if cur is None or rd.idx > cur.idx:
                        latest[rd.eng] = rd
            for rd in latest.values():
                deps.add(rd)
        deps.discard(op)
        for d in deps:
            if d.eng == "pe" and op.eng == "pe" and not d.is_dma and not op.is_dma:
                continue
            op.deps.append(d)
            d.sig = True
        for r in reads:
            r.readers.append(op)
        for w in writes:
            w.last_w = op
            w.readers = []

    def op(self, eng, fn, reads=(), writes=()):
        o = Op(eng, fn)
        o.idx = len(self.ops)
        self.ops.append(o)
        self._dep(o, reads, writes)
        self.last_op[eng] = o
        return o

    def dma(self, queue, fn, lane, reads=(), writes=()):
        o = Op(queue, fn, is_dma=True, lane=lane)
        o.idx = len(self.ops)
        self.ops.append(o)
        self._dep(o, reads, writes)
        prev = self.last_lane.get(lane)
        if prev is not None and prev not in o.deps:
            o.deps.append(prev)
        self.last_lane[lane] = o
        return o

    def barrier(self):
        pend = list(self.last_lane.values())
        for e in ENGS:
            if self.last_op[e] is not None:
                self.last_op[e].sig = True
                pend.append(self.last_op[e])
        for e in ENGS:
            o = Op(e, None)
            o.barrier = True
            o.deps = list(pend)
            self.ops.append(o)

    def emit(self):
        nc = self.nc
        eng_cnt = {e: 0 for e in ENGS}
        lane_cnt = {}
        for o in self.ops:
            if o.barrier:
                continue
            if o.is_dma:
                lane_cnt[o.lane] = lane_cnt.get(o.lane, 0) + 1
                o.cnt = lane_cnt[o.lane]
            elif o.sig:
                eng_cnt[o.eng] += 1
                o.cnt = eng_cnt[o.eng]
        sems = {}
        with contextlib.ExitStack() as stack:
            for e in ENGS:
                n_ep = max((eng_cnt[e] + EPOCH - 1) // EPOCH, 1)
                for k in range(n_ep):
                    sems[(e, k)] = stack.enter_context(nc.semaphore(f"s_{e}_{k}"))
            for i, ln in enumerate(lane_cnt):
                sems[("lane", ln)] = stack.enter_context(nc.semaphore(f"l_{i}"))

            def semof(o):
                if o.is_dma:
                    return sems[("lane", o.lane)], 16 * o.cnt, ("lane", o.lane)
                k = (o.cnt - 1) // EPOCH
                return sems[(o.eng, k)], o.cnt - k * EPOCH, (o.eng, k)

            per_eng = {e: [] for e in ENGS}
            for o in self.ops:
                per_eng[o.eng].append(o)
            final_waits = [(sems[("lane", ln)], 16 * c) for ln, c in lane_cnt.items()]

            def run(e, eng):
                seen = {}
                for o in per_eng[e]:
                    for d in o.deps:
                        s, v, key = semof(d)
                        if seen.get(key, 0) >= v:
                            continue
                        seen[key] = v
                        eng.wait_ge(s, v)
                    if o.barrier:
                        continue
                    ins = o.fn(eng)
                    if o.is_dma:
                        s, v, key = semof(o)
                        ins.then_inc(s, 16)
                    elif o.sig:
                        s, v, key = semof(o)
                        ins.then_inc(s, 1)
                if e == "sp":
                    for s, v in final_waits:
                        eng.wait_ge(s, v)

            with nc.Block() as block:
                @block.tensor
                def _(eng):
                    run("pe", eng)

                @block.scalar
                def _(eng):
                    run("act", eng)

                @block.vector
                def _(eng):
                    run("dve", eng)

                @block.gpsimd
                def _(eng):
                    run("pool", eng)

                @block.sync
                def _(eng):
                    run("sp", eng)
        return eng_cnt, len(lane_cnt)


class Buf:
    __slots__ = ("ap", "r")

    def __init__(self, ap, r):
        self.ap = ap
        self.r = r


DT_SIZE = {F32: 4, BF16: 2}


class Builder:
    def __init__(self, SEQ, debug=False, stop_after=None):
        self.SEQ = SEQ
        self.NT = SEQ // 128
        self.NQB = SEQ // 512
        self.debug = debug
        self.stop_after = stop_after
        nc = bass.Bass("TRN2", target_bir_lowering=False)
        self.nc = nc
        self.S = Sched(nc)
        self._uid = 0

    def mm(self, out, lhsT, rhs, start, stop, reads, writes):
        self.S.op("pe", lambda e: e.matmul(out, lhsT=lhsT, rhs=rhs, start=start, stop=stop), reads, writes)

    def tr(self, out, in_, ident, reads, writes):
        self.S.op("pe", lambda e: e.transpose(out=out, in_=in_, identity=ident), reads, writes)

    def actf(self, out, in_, func, reads, writes, bias=None, scale=None, accum_out=None):
        kw = {}
        if bias is not None:
            kw["bias"] = bias
        if scale is not None:
            kw["scale"] = scale
        if accum_out is not None:
            kw["accum_out"] = accum_out
        self.S.op("act", lambda e: e.activation(out=out, in_=in_, func=func, **kw), reads, writes)

    def tt(self, eng, out, in0, in1, op, reads, writes):
        self.S.op(eng, lambda e: e.tensor_tensor(out=out, in0=in0, in1=in1, op=op), reads, writes)

    def ts(self, eng, out, in0, s1, s2, op0, op1, reads, writes):
        if s2 is None:
            self.S.op(eng, lambda e: e.tensor_scalar(out=out, in0=in0, scalar1=s1, scalar2=None, op0=op0), reads, writes)
        else:
            self.S.op(eng, lambda e: e.tensor_scalar(out=out, in0=in0, scalar1=s1, scalar2=s2, op0=op0, op1=op1), reads, writes)

    def stt(self, eng, out, in0, scalar, in1, op0, op1, reads, writes):
        self.S.op(eng, lambda e: e.scalar_tensor_tensor(out=out, in0=in0, scalar=scalar, in1=in1, op0=op0, op1=op1), reads, writes)

    def cp(self, eng, out, in_, reads, writes):
        if eng == "act":
            self.S.op("act", lambda e: e.copy(out=out, in_=in_), reads, writes)
        else:
            self.S.op(eng, lambda e: e.tensor_copy(out=out, in_=in_), reads, writes)

    def memset(self, eng, ap, val, writes):
        self.S.op(eng, lambda e: e.memset(ap, val), (), writes)

    def dma(self, out, in_, lane, reads, writes, q="sp", nc_ok=False):
        if nc_ok:
            nc = self.nc

            def f(e):
                with nc.allow_non_contiguous_dma(reason="tiny strided parameter load"):
                    return e.dma_start(out=out, in_=in_)
            self.S.dma(q, f, lane, reads, writes)
        else:
            self.S.dma(q, lambda e: e.dma_start(out=out, in_=in_), lane, reads, writes)

    def arena_reset(self):
        self.S.barrier()
        self.aoff = 0

    def alloc(self, shape, dt, name="b"):
        n = 1
        for s in shape[1:]:
            n *= s
        nbytes = n * DT_SIZE[dt]
        nbytes = (nbytes + 63) // 64 * 64
        off = self.aoff
        self.aoff += nbytes
        assert self.aoff <= self.ARENA_BYTES, (name, self.aoff, self.ARENA_BYTES)
        if dt == F32:
            v = self.arena[:, off // 4: off // 4 + n]
        else:
            v = self.arena_bf[:, off // 2: off // 2 + n]
        if len(shape) == 3:
            v = v.rearrange("p (a b) -> p a b", a=shape[1])
        elif len(shape) == 4:
            v = v.rearrange("p (a b c) -> p a b c", a=shape[1], b=shape[2])
        if shape[0] < 128:
            v = v[0:shape[0]]
        self._uid += 1
        return Buf(v, Res(f"{name}{self._uid}"))

    def build(self):
        nc, SEQ, NT = self.nc, self.SEQ, self.NT
        L = DEPTH
        ext = lambda name, shape, dt=F32: nc.dram_tensor(name, shape, dt, kind="ExternalInput").ap()
        I = {}
        I["x"] = ext("x", [SEQ, D])
        I["ln_in_g"] = ext("ln_in_g", [D]); I["ln_in_b"] = ext("ln_in_b", [D])
        I["w_in"] = ext("w_in", [L, D, DIN])
        I["conv_a_w"] = ext("conv_a_w", [L, 31, GW]); I["conv_a_b"] = ext("conv_a_b", [L, GW])
        I["ln_a_g"] = ext("ln_a_g", [L, GW]); I["ln_a_b"] = ext("ln_a_b", [L, GW])
        I["qk_norm_q"] = ext("qk_norm_q", [L, HD]); I["qk_norm_k"] = ext("qk_norm_k", [L, HD])
        I["sgu_ln_g"] = ext("sgu_ln_g", [L, GW]); I["sgu_ln_b"] = ext("sgu_ln_b", [L, GW])
        I["sgu_w"] = ext("sgu_w", [L, 4, 128, 128]); I["sgu_b"] = ext("sgu_b", [L, 4, 128])
        I["mla_q_norm"] = ext("mla_q_norm", [L, 192]); I["mla_w_uq"] = ext("mla_w_uq", [L, 192, 384])
        I["mla_kv_norm"] = ext("mla_kv_norm", [L, 128]); I["mla_w_ukv"] = ext("mla_w_ukv", [L, 128, 512])
        I["w_out"] = ext("w_out", [L, D, D])
        I["ln_mix_g"] = ext("ln_mix_g", [L, D]); I["ln_mix_b"] = ext("ln_mix_b", [L, D])
        I["ffn_w_up"] = ext("ffn_w_up", [L, D, 2 * DFF])
        I["ffn_conv_w"] = ext("ffn_conv_w", [L, 3, 2 * DFF]); I["ffn_conv_b"] = ext("ffn_conv_b", [L, 2 * DFF])
        I["ffn_w_down"] = ext("ffn_w_down", [L, DFF, D])
        I["ln_ffn_g"] = ext("ln_ffn_g", [L, D]); I["ln_ffn_b"] = ext("ln_ffn_b", [L, D])
        I["rope_tbl"] = ext("rope_tbl", [SEQ, 96])
        self.I = I
        self.out = nc.dram_tensor("out", [SEQ, D], F32, kind="ExternalOutput").ap()

        skind = "ExternalOutput" if self.debug else "Internal"
        scr = lambda name, shape, dt=BF16: nc.dram_tensor(name, shape, dt, kind=skind).ap()
        self.aT_d = scr("aT_d", [2, 128, SEQ + 32]); self.r_aT = Res("aT_d")
        self.QT_d = scr("QT_d", [4, 64, SEQ]); self.r_QT = Res("QT_d")
        self.KT_d = scr("KT_d", [2, 64, SEQ]); self.r_KT = Res("KT_d")
        self.V_d = scr("V_d", [SEQ, 2, 64]); self.r_V = Res("V_d")
        self.QfT_d = scr("QfT_d", [4, 96, SEQ]); self.r_QfT = Res("QfT_d")
        self.KfT_d = scr("KfT_d", [4, 96, SEQ]); self.r_KfT = Res("KfT_d")
        self.Vd_d = scr("Vd_d", [SEQ, 4, 64]); self.r_Vd = Res("Vd_d")
        self.OT_d = scr("OT_d", [8, 128, SEQ]); self.r_OT = [Res(f"OT{c}") for c in range(8)]
        wk = "Internal"
        self.wup_d = [nc.dram_tensor(f"wup_d{l}", [NJ, 128, 8, 256], BF16, kind=wk).ap() for l in range(L)]
        self.wdn_d = [nc.dram_tensor(f"wdn_d{l}", [NJ, 128, D], BF16, kind=wk).ap() for l in range(L)]
        self.r_wup = [[Res(f"wup{l}_{j}") for j in range(NJ)] for l in range(L)]
        self.r_wdn = [[Res(f"wdn{l}_{j}") for j in range(NJ)] for l in range(L)]

        self.H = nc.alloc_sbuf_tensor("H", [128, NT, D], F32)
        self.Hr = [Res(f"H{i}") for i in range(NT)]
        self.ident = nc.alloc_sbuf_tensor("ident", [128, 128], BF16)
        self.identf = nc.alloc_sbuf_tensor("identf", [128, 128], F32)
        self.r_ident = Res("ident")
        self.cst = nc.alloc_sbuf_tensor("cst", [128, 16], F32)
        self.r_cst = Res("cst")
        rem = nc.sbuf_bytes_remaining
        self.ARENA_BYTES = (rem - 512) // 256 * 256
        self.arena = nc.alloc_sbuf_tensor("arena", [128, self.ARENA_BYTES // 4], F32)
        self.arena_bf = self.arena.bitcast(BF16)
        self.aoff = 0
        self.PS = nc.alloc_psum_tensor("PS", [128, 8, 512], F32)
        self.PSr = [Res(f"ps{b}") for b in range(8)]
        self.PSbf = self.PS.bitcast(BF16)
        self.wcast_queue = []

        self.consts()
        self.phase0()
        if self.stop_after != "p0":
            for l in range(L):
                self.wcast_prepare(l)
                self.phase1(l)
                if self.stop_after == f"p1_{l}":
                    break
                self.phase2_attn(l)
                if self.stop_after == f"p2a_{l}":
                    break
                self.phase2c(l)
                if self.stop_after == f"p2_{l}":
                    break
                self.phase3(l)
                if self.stop_after == f"p3_{l}":
                    break
        self.final_store()
        info = self.S.emit()
        return nc, info

    def psb(self, b):
        return self.PS[:, b, :]

    def psb_bf(self, b):
        return self.PSbf[:, b, :] if len(self.PSbf.shape) == 3 else self.PSbf[:, b * 1024:(b + 1) * 1024]

    def consts(self):
        S = self.S
        identf, ident = self.identf, self.ident
        S.op("pool", lambda e: e.memset(identf[:], 0.0), (), [self.r_ident])
        S.op("pool", lambda e: e.affine_select(out=identf[:], in_=identf[:], pattern=[[-1, 128]],
                                                compare_op=ALU.not_equal, fill=1.0, base=0, channel_multiplier=1),
             [self.r_ident], [self.r_ident])
        S.op("dve", lambda e: e.tensor_copy(out=ident[:], in_=identf[:]), [self.r_ident], [self.r_ident])
        cst = self.cst
        S.op("pool", lambda e: e.memset(cst[:, 0:8], -0.5), (), [self.r_cst])
        S.op("pool", lambda e: e.memset(cst[:, 8:9], 1.0 / 192.0), (), [self.r_cst])
        S.op("pool", lambda e: e.memset(cst[:, 9:10], 1.0 / 128.0), (), [self.r_cst])

    def rstd_pool(self, out, in_, n, eps, reads, writes, scale=None):
        if scale is None:
            self.ts("pool", out, in_, eps, None, ALU.add, None, reads, writes)
        else:
            self.ts("pool", out, in_, scale, eps, ALU.mult, ALU.add, reads, writes)
        self.tt("pool", out, out, self.cst[:, 0:n], ALU.pow, list(writes) + [self.r_cst], writes)

    def bcast_load(self, buf, src_1d, n, lane):
        self.dma(buf.ap, src_1d.partition_broadcast(128), lane, (), [buf.r])

    def layernorm_tile(self, x_ap, x_res, out_ap, out_res, g, b, scr, eng_b="pool"):
        st = scr.ap
        r = scr.r
        S = self.S
        S.op("dve", lambda e: e.bn_stats(out=st[:, 0:6], in_=x_ap[:, 0:512]), [x_res], [r])
        S.op("dve", lambda e: e.bn_stats(out=st[:, 6:12], in_=x_ap[:, 512:1024]), [x_res], [r])
        S.op("dve", lambda e: e.bn_aggr(out=st[:, 12:14], in_=st[:, 0:12]), [r], [r])
        self.rstd_pool(st[:, 14:15], st[:, 13:14], 1, LN_EPS, [r], [r])
        self.stt("dve", st[:, 15:16], st[:, 12:13], -1.0, st[:, 14:15], ALU.mult, ALU.mult, [r], [r])
        self.actf(out_ap, x_ap, AF.Identity, [x_res, r], [out_res], bias=st[:, 15:16], scale=st[:, 14:15])
        self.tt("dve", out_ap, out_ap, g.ap, ALU.mult, [out_res, g.r], [out_res])
        self.tt(eng_b, out_ap, out_ap, b.ap, ALU.add, [out_res, b.r], [out_res])

    def layernorm_tiles(self, items, g, b, eng_b="dve"):
        S = self.S
        for x_ap, xr, scr in items:
            st = scr.ap; r = scr.r
            S.op("dve", (lambda o, i_: (lambda e: e.bn_stats(out=o, in_=i_)))(st[:, 0:6], x_ap[:, 0:512]), [xr], [r])
            S.op("dve", (lambda o, i_: (lambda e: e.bn_stats(out=o, in_=i_)))(st[:, 6:12], x_ap[:, 512:1024]), [xr], [r])
            S.op("dve", (lambda o, i_: (lambda e: e.bn_aggr(out=o, in_=i_)))(st[:, 12:14], st[:, 0:12]), [r], [r])
        for x_ap, xr, scr in items:
            st = scr.ap; r = scr.r
            self.rstd_pool(st[:, 14:15], st[:, 13:14], 1, LN_EPS, [r], [r])
        for x_ap, xr, scr in items:
            st = scr.ap; r = scr.r
            self.stt("dve", st[:, 15:16], st[:, 12:13], -1.0, st[:, 14:15], ALU.mult, ALU.mult, [r], [r])
        for x_ap, xr, scr in items:
            st = scr.ap; r = scr.r
            self.actf(x_ap, x_ap, AF.Identity, [xr, r], [xr], bias=st[:, 15:16], scale=st[:, 14:15])
        for x_ap, xr, scr in items:
            self.tt("dve", x_ap, x_ap, g.ap, ALU.mult, [xr, g.r], [xr])
        for k, (x_ap, xr, scr) in enumerate(items):
            self.tt(eng_b if k % 2 == 0 else "pool", x_ap, x_ap, b.ap, ALU.add, [xr, b.r], [xr])

    def phase0(self):
        self.arena_reset()
        g = self.alloc([128, D], F32, "g0"); b = self.alloc([128, D], F32, "b0")
        self.bcast_load(g, self.I["ln_in_g"], D, "p0g")
        self.bcast_load(b, self.I["ln_in_b"], D, "p0b")
        scrs = [self.alloc([128, 32], F32, "st") for _ in range(8)]
        GP = 8
        for i0 in range(0, self.NT, GP):
            items = []
            for i in range(i0, i0 + GP):
                hi = self.H[:, i, :]
                self.dma(hi, self.I["x"][i * 128:(i + 1) * 128, :], f"Hld{i % 4}", (), [self.Hr[i]])
                items.append((hi, self.Hr[i], scrs[i % 8]))
            self.layernorm_tiles(items, g, b)

    def wcast_prepare(self, l):
        q = []
        wu = self.I["ffn_w_up"][l].rearrange("(k p) n -> p k n", p=128)
        for j in range(NJ):
            q.append((wu[:, :, j * 128:(j + 1) * 128], self.wup_d[l][j, :, :, 0:128], self.r_wup[l][j], True))
            q.append((wu[:, :, DFF + j * 128: DFF + (j + 1) * 128], self.wup_d[l][j, :, :, 128:256], self.r_wup[l][j], True))
        for j in range(NJ):
            q.append((self.I["ffn_w_down"][l, j * 128:(j + 1) * 128, :], self.wdn_d[l][j], self.r_wdn[l][j], False))
        self.wcast_queue = q

    def wcast_step(self, bufs, n=1):
        for _ in range(n):
            if not self.wcast_queue:
                return
            src, dst, res, is3 = self.wcast_queue.pop(0)
            k = self._wc_i = getattr(self, "_wc_i", 0) + 1
            fb, bb = bufs[k % len(bufs)]
            if is3:
                self.dma(fb.ap.rearrange("p (k n) -> p k n", k=8), src, f"wc_in{k % len(bufs)}", (), [fb.r])
            else:
                self.dma(fb.ap, src, f"wc_in{k % len(bufs)}", (), [fb.r])
            self.cp("pool", bb.ap, fb.ap, [fb.r], [bb.r])
            if is3:
                self.dma(dst, bb.ap.rearrange("p (k n) -> p k n", k=8), f"wc_out{k % len(bufs)}", [bb.r], [res])
            else:
                self.dma(dst, bb.ap, f"wc_out{k % len(bufs)}", [bb.r], [res])

    def phase1(self, l):
        self._p1(l, "AB")
        self._p1(l, "CD")

    def _p1(self, l, part):
        I, S, NT = self.I, self.S, self.NT
        AB = part == "AB"
        self.arena_reset()
        ident = self.ident[:]
        rid = self.r_ident
        PSr = self.PSr
        wc0, wc1 = (0, 1024) if AB else (1024, DIN)
        winb = self.alloc([128, 8, wc1 - wc0], BF16, "winb")
        for k in range(8):
            self.dma(winb.ap[:, k, :], I["w_in"][l, k * 128:(k + 1) * 128, wc0:wc1], f"winb{k % 2}", (), [winb.r], q="pool")
        if AB:
            gqk = self.alloc([128, 6, 64], F32, "gqk")
            for h in range(4):
                self.dma(gqk.ap[:, h, :], I["qk_norm_q"][l].partition_broadcast(128), "gq", (), [gqk.r])
            for h in range(2):
                self.dma(gqk.ap[:, 4 + h, :], I["qk_norm_k"][l].partition_broadcast(128), "gk", (), [gqk.r])
            zb = self.alloc([128, 2, 16], BF16, "zb")
            self.memset("dve", zb.ap, 0.0, [zb.r])
            self.dma(self.aT_d[:, :, 0:16].rearrange("c p t -> p c t"), zb.ap, "zpad0", [zb.r], [self.r_aT])
            self.dma(self.aT_d[:, :, 16 + self.SEQ:32 + self.SEQ].rearrange("c p t -> p c t"), zb.ap, "zpad1", [zb.r], [self.r_aT])
        else:
            wuq = self.alloc([128, 2, 384], BF16, "wuq")
            self.dma(wuq.ap[:, 0, :], I["mla_w_uq"][l, 0:128, :], "wuq0", (), [wuq.r], q="pool")
            self.dma(wuq.ap[0:64, 1, :], I["mla_w_uq"][l, 128:192, :], "wuq1", (), [wuq.r], q="pool")
            wukv = self.alloc([128, 512], BF16, "wukv")
            self.dma(wukv.ap, I["mla_w_ukv"][l], "wukv", (), [wukv.r], q="pool")
            wsn = self.alloc([128, 4, 128], BF16, "wsn")
            self.dma(wsn.ap, I["sgu_w"][l].rearrange("g p q -> p g q"), "wsn", (), [wsn.r], q="pool")
            wsT = self.alloc([128, 4, 128], BF16, "wsT")
            pb = self.psb_bf(7)
            for g in range(4):
                self.tr(pb[:, g * 128:(g + 1) * 128], wsn.ap[:, g, :], ident, [wsn.r, rid], [self.PSr[7]])
            self.cp("dve", wsT.ap, pb[:, 0:512].rearrange("p (g q) -> p g q", g=4), [self.PSr[7]], [wsT.r])
            sgub = self.alloc([128, 4], F32, "sgub")
            self.dma(sgub.ap, I["sgu_b"][l].rearrange("g p -> p g"), "sgub", (), [sgub.r], nc_ok=True)
            gsg = self.alloc([128, 256], F32, "gsg"); bsg = self.alloc([128, 256], F32, "bsg")
            self.bcast_load(gsg, I["sgu_ln_g"][l], 256, "gsg"); self.bcast_load(bsg, I["sgu_ln_b"][l], 256, "bsg")
            gcq = self.alloc([128, 192], F32, "gcq"); gckv = self.alloc([128, 128], F32, "gckv")
            self.bcast_load(gcq, I["mla_q_norm"][l], 192, "gcq"); self.bcast_load(gckv, I["mla_kv_norm"][l], 128, "gckv")

        NB = 2
        mk = lambda shape, dt, nm: [self.alloc(shape, dt, nm) for _ in range(NB)]
        hb = mk([128, D], BF16, "hb"); hT = mk([128, 8, 128], BF16, "hT")
        tbl = mk([128, 96], F32, "tbl"); st = mk([128, 32], F32, "st")
        if AB:
            sgm = mk([128, 256], F32, "sgm"); a_bf = mk([128, 256], BF16, "a_bf"); aT = mk([128, 2, 128], BF16, "aT")
            qk_sb = mk([128, 6, 64], F32, "qk_sb"); sq = mk([128, 384], F32, "sq")
            ra = mk([128, 192], F32, "ra"); rb = mk([128, 192], F32, "rb")
            qk_bf = mk([128, 6, 64], BF16, "qk_bf"); qkT = mk([64, 6, 128], BF16, "qkT"); v_bf = mk([128, 2, 64], BF16, "v_bf")
        else:
            sqd = mk([128, 384], F32, "sqd"); rad = mk([128, 64], F32, "rad"); rbd = mk([128, 64], F32, "rbd")
            c_sb = mk([128, 512], F32, "c_sb"); svn = mk([128, 256], F32, "svn"); svn_bf = mk([128, 256], BF16, "svn_bf")
            oc_bf = mk([128, 256], BF16, "oc_bf"); ocT = mk([128, 2, 128], BF16, "ocT")
            d_sb = mk([128, 352], F32, "d_sb"); cqn = mk([128, 320], BF16, "cqn"); krr = mk([128, 32], BF16, "krr")
            cT = mk([128, 3, 128], BF16, "cT")
            qf = mk([128, 4, 96], BF16, "qf"); kf = mk([128, 4, 96], BF16, "kf"); vd = mk([128, 4, 64], BF16, "vd")
            qkfT = mk([96, 8, 128], BF16, "qkfT")

        def rope(x1, x2, cos, sin, o1, o2, ta, tb, rin, rtmp_a, rtmp_b, rout):
            self.tt("dve", ta, x1, cos, ALU.mult, rin, [rtmp_a])
            self.tt("dve", tb, x2, sin, ALU.mult, rin, [rtmp_b])
            self.tt("dve", o1, ta, tb, ALU.subtract, [rtmp_a, rtmp_b], [rout])
            self.tt("dve", ta, x2, cos, ALU.mult, rin, [rtmp_a])
            self.tt("dve", tb, x1, sin, ALU.mult, rin, [rtmp_b])
            self.tt("dve", o2, ta, tb, ALU.add, [rtmp_a, rtmp_b], [rout])

        def make_tile(i):
            s = i % NB
            Hi = self.H[:, i, :]
            tsl = slice(i * 128, (i + 1) * 128)
            bT = 4 + s if AB else 4
            b0, b1 = 2 * s, 2 * s + 1
            cols = [(0, 512), (512, wc1 - wc0)]
            def pro():
                yield
                self.dma(tbl[s].ap, I["rope_tbl"][tsl, :], f"tbl{s}", (), [tbl[s].r])
                bT = 4 + s if AB else 4
                yield
                self.cp("act", hb[s].ap, Hi, [self.Hr[i]], [hb[s].r])
                p4 = self.psb_bf(bT)
                yield
                for k in range(8):
                    self.tr(p4[:, k * 128:(k + 1) * 128], hb[s].ap[:, k * 128:(k + 1) * 128], ident, [hb[s].r, rid], [PSr[bT]])
                yield
                self.cp("dve" if AB else "act", hT[s].ap, p4.rearrange("p (k t) -> p k t", k=8), [PSr[bT]], [hT[s].r])
                b0, b1 = 2 * s, 2 * s + 1
                cols = [(0, 512), (512, wc1 - wc0)]
                yield
                for bnk, (c0, c1) in zip((b0, b1), cols):
                    for k in range(8):
                        self.mm(self.PS[:, bnk, 0:c1 - c0], hT[s].ap[:, k, :], winb.ap[:, k, c0:c1], k == 0, k == 7,
                                [hT[s].r, winb.r], [PSr[bnk]])
                yield
            stq = st[s].ap
            if AB:
                bX = 6 + s
                pA = self.psb(b0); pB = self.psb(b1); pX = self.psb_bf(bX)
            else:
                pC = self.psb(b0); pD = self.psb(b1)
                p5f = self.psb(5); p5 = self.psb_bf(5); p6f = self.psb(6); p6 = self.psb_bf(6)
                p7f = self.psb(7); p7 = self.psb_bf(7)

            def chainA():
                p0 = pA
                yield
                self.actf(sgm[s].ap, p0[:, 256:512], AF.Tanh, [PSr[b0]], [sgm[s].r], scale=0.5)
                yield
                self.ts("dve", sgm[s].ap, sgm[s].ap, 0.5, 0.5, ALU.mult, ALU.add, [sgm[s].r], [sgm[s].r])
                yield
                self.tt("dve", a_bf[s].ap, p0[:, 0:256], sgm[s].ap, ALU.mult, [PSr[b0], sgm[s].r], [a_bf[s].r])
                p5 = pX
                yield
                for c in range(2):
                    self.tr(p5[:, c * 128:(c + 1) * 128], a_bf[s].ap[:, c * 128:(c + 1) * 128], ident, [a_bf[s].r, rid], [PSr[bX]])
                yield
                self.cp("act", aT[s].ap, p5[:, 0:256].rearrange("p (c t) -> p c t", c=2), [PSr[bX]], [aT[s].r])
                yield
                self.dma(self.aT_d[:, :, 16 + i * 128: 16 + (i + 1) * 128].rearrange("c p t -> p c t"), aT[s].ap,
                         f"aT{s}", [aT[s].r], [self.r_aT])
                yield
            def chainB():
                p1 = pB
                qkf = qk_sb[s].ap.rearrange("p h d -> p (h d)")
                yield
                self.cp("act", qkf, p1[:, 0:384], [PSr[b1]], [qk_sb[s].r])
                yield
                self.cp("act", v_bf[s].ap.rearrange("p h d -> p (h d)"), p1[:, 384:512], [PSr[b1]], [v_bf[s].r])
                yield
                self.dma(self.V_d[tsl, :, :], v_bf[s].ap, f"v{s}", [v_bf[s].r], [self.r_V])
                yield
                self.tt("pool", sq[s].ap, qkf, qkf, ALU.mult, [qk_sb[s].r], [sq[s].r])
                stq = st[s].ap
                yield
                S.op("dve", (lambda o, i_: (lambda e: e.tensor_reduce(out=o, in_=i_, axis=AX.X, op=ALU.add)))(
                    stq[:, 0:6], sq[s].ap.rearrange("p (h d) -> p h d", h=6)), [sq[s].r], [st[s].r])
                yield
                self.rstd_pool(stq[:, 0:6], stq[:, 0:6], 6, RMS_EPS, [st[s].r], [st[s].r], scale=1.0 / 64.0)
                yield
                self.tt("dve", qk_sb[s].ap, qk_sb[s].ap, stq[:, 0:6].unsqueeze(2).broadcast_to([128, 6, 64]), ALU.mult,
                        [qk_sb[s].r, st[s].r], [qk_sb[s].r])
                yield
                self.tt("dve", qk_sb[s].ap, qk_sb[s].ap, gqk.ap, ALU.mult, [qk_sb[s].r, gqk.r], [qk_sb[s].r])
                xv = qk_sb[s].ap.rearrange("p h (r f e) -> p h r f e", r=2, f=2)
                ov = qk_bf[s].ap.rearrange("p h (r f e) -> p h r f e", r=2, f=2)
                cosB = tbl[s].ap[:, 0:32].rearrange("p (r e) -> p r e", r=2).unsqueeze(1).broadcast_to([128, 6, 2, 16])
                sinB = tbl[s].ap[:, 32:64].rearrange("p (r e) -> p r e", r=2).unsqueeze(1).broadcast_to([128, 6, 2, 16])
                ta = ra[s].ap.rearrange("p (h r e) -> p h r e", h=6, r=2)
                tb = rb[s].ap.rearrange("p (h r e) -> p h r e", h=6, r=2)
                yield
                rope(xv[:, :, :, 0, :], xv[:, :, :, 1, :], cosB, sinB, ov[:, :, :, 0, :], ov[:, :, :, 1, :], ta, tb,
                     [qk_sb[s].r, tbl[s].r], ra[s].r, rb[s].r, qk_bf[s].r)
                yield
                for h in range(6):
                    self.tr(pX[0:64, 256 + h * 128:256 + (h + 1) * 128], qk_bf[s].ap[:, h, :], ident, [qk_bf[s].r, rid], [PSr[bX]])
                yield
                self.cp("act", qkT[s].ap, pX[0:64, 256:1024].rearrange("p (h t) -> p h t", h=6), [PSr[bX]], [qkT[s].r])
                yield
                self.dma(self.QT_d[:, :, tsl].rearrange("h d t -> d h t"), qkT[s].ap[:, 0:4, :], f"qT{s}", [qkT[s].r], [self.r_QT])
                yield
                self.dma(self.KT_d[:, :, tsl].rearrange("h d t -> d h t"), qkT[s].ap[:, 4:6, :], f"kT{s}", [qkT[s].r], [self.r_KT])
                yield
            def chainC():
                yield
                self.actf(c_sb[s].ap, pC, AF.Gelu, [PSr[b0]], [c_sb[s].r])
                sv = c_sb[s].ap[:, 256:512]
                yield
                S.op("dve", (lambda o, i_: (lambda e: e.bn_stats(out=o, in_=i_)))(stq[:, 8:14], sv), [c_sb[s].r], [st[s].r])
                yield
                S.op("dve", (lambda o, i_: (lambda e: e.bn_aggr(out=o, in_=i_)))(stq[:, 14:16], stq[:, 8:14]), [st[s].r], [st[s].r])
                yield
                self.rstd_pool(stq[:, 16:17], stq[:, 15:16], 1, LN_EPS, [st[s].r], [st[s].r])
                yield
                self.stt("dve", stq[:, 17:18], stq[:, 14:15], -1.0, stq[:, 16:17], ALU.mult, ALU.mult, [st[s].r], [st[s].r])
                yield
                self.actf(svn[s].ap, sv, AF.Identity, [c_sb[s].r, st[s].r], [svn[s].r], bias=stq[:, 17:18], scale=stq[:, 16:17])
                yield
                self.tt("dve", svn[s].ap, svn[s].ap, gsg.ap, ALU.mult, [svn[s].r, gsg.r], [svn[s].r])
                yield
                self.tt("pool", svn_bf[s].ap, svn[s].ap, bsg.ap, ALU.add, [svn[s].r, bsg.r], [svn_bf[s].r])
                yield
                for g in range(4):
                    self.mm(p5f[:, g * 64:(g + 1) * 64], wsT.ap[:, g, :], svn_bf[s].ap[:, g * 64:(g + 1) * 64], True, True,
                            [wsT.r, svn_bf[s].r], [PSr[5]])
                yield
                for g in range(4):
                    self.stt("dve", oc_bf[s].ap[:, g * 64:(g + 1) * 64], p5f[:, g * 64:(g + 1) * 64], sgub.ap[:, g:g + 1],
                             c_sb[s].ap[:, g * 64:(g + 1) * 64], ALU.add, ALU.mult, [PSr[5], sgub.r, c_sb[s].r], [oc_bf[s].r])
                yield
                for c in range(2):
                    self.tr(p5[:, 512 + c * 128: 512 + (c + 1) * 128], oc_bf[s].ap[:, c * 128:(c + 1) * 128], ident,
                            [oc_bf[s].r, rid], [PSr[5]])
                yield
                self.cp("act", ocT[s].ap, p5[:, 512:768].rearrange("p (c t) -> p c t", c=2), [PSr[5]], [ocT[s].r])
                yield
                self.dma(self.OT_d[4:6, :, tsl].rearrange("c p t -> p c t"), ocT[s].ap, f"ocT{s}", [ocT[s].r],
                         [self.r_OT[4], self.r_OT[5]])
                yield
            def chainD():
                yield
                self.cp("act", d_sb[s].ap, pD[:, 0:352], [PSr[b1]], [d_sb[s].r])
                yield
                self.tt("pool", sqd[s].ap[:, 0:320], d_sb[s].ap[:, 0:320], d_sb[s].ap[:, 0:320], ALU.mult, [d_sb[s].r], [sqd[s].r])
                yield
                S.op("dve", (lambda o, i_: (lambda e: e.tensor_reduce(out=o, in_=i_, axis=AX.X, op=ALU.add)))(
                    stq[:, 20:21], sqd[s].ap[:, 0:192]), [sqd[s].r], [st[s].r])
                yield
                S.op("dve", (lambda o, i_: (lambda e: e.tensor_reduce(out=o, in_=i_, axis=AX.X, op=ALU.add)))(
                    stq[:, 21:22], sqd[s].ap[:, 192:320]), [sqd[s].r], [st[s].r])
                yield
                self.tt("dve", stq[:, 20:22], stq[:, 20:22], self.cst[:, 8:10], ALU.mult, [st[s].r, self.r_cst], [st[s].r])
                yield
                self.rstd_pool(stq[:, 20:22], stq[:, 20:22], 2, RMS_EPS, [st[s].r], [st[s].r])
                yield
                self.stt("dve", cqn[s].ap[:, 0:192], d_sb[s].ap[:, 0:192], stq[:, 20:21], gcq.ap, ALU.mult, ALU.mult,
                         [d_sb[s].r, st[s].r, gcq.r], [cqn[s].r])
                yield
                self.stt("dve", cqn[s].ap[:, 192:320], d_sb[s].ap[:, 192:320], stq[:, 21:22], gckv.ap, ALU.mult, ALU.mult,
                         [d_sb[s].r, st[s].r, gckv.r], [cqn[s].r])
                kx = d_sb[s].ap[:, 320:352].rearrange("p (r f e) -> p r f e", r=2, f=2)
                ko = krr[s].ap.rearrange("p (r f e) -> p r f e", r=2, f=2)
                cosD = tbl[s].ap[:, 64:80].rearrange("p (r e) -> p r e", r=2)
                sinD = tbl[s].ap[:, 80:96].rearrange("p (r e) -> p r e", r=2)
                ta2 = rad[s].ap[:, 0:16].rearrange("p (r e) -> p r e", r=2)
                tb2 = rbd[s].ap[:, 0:16].rearrange("p (r e) -> p r e", r=2)
                yield
                rope(kx[:, :, 0, :], kx[:, :, 1, :], cosD, sinD, ko[:, :, 0, :], ko[:, :, 1, :], ta2, tb2,
                     [d_sb[s].r, tbl[s].r], rad[s].r, rbd[s].r, krr[s].r)
                yield
                self.tr(p7[:, 768:896], cqn[s].ap[:, 0:128], ident, [cqn[s].r, rid], [PSr[7]])
                yield
                self.tr(p7[0:64, 896:1024], cqn[s].ap[:, 128:192], ident, [cqn[s].r, rid], [PSr[7]])
                yield
                self.tr(p5[:, 768:896], cqn[s].ap[:, 192:320], ident, [cqn[s].r, rid], [PSr[5]])
                yield
                self.cp("act", cT[s].ap[:, 0, :], p7[:, 768:896], [PSr[7]], [cT[s].r])
                yield
                self.cp("act", cT[s].ap[0:64, 1, :], p7[0:64, 896:1024], [PSr[7]], [cT[s].r])
                yield
                self.cp("act", cT[s].ap[:, 2, :], p5[:, 768:896], [PSr[5]], [cT[s].r])
                yield
                self.mm(p7f[:, 0:384], cT[s].ap[:, 0, :], wuq.ap[:, 0, :], True, False, [cT[s].r, wuq.r], [PSr[7]])
                yield
                self.mm(p7f[:, 0:384], cT[s].ap[0:64, 1, :], wuq.ap[0:64, 1, :], False, True, [cT[s].r, wuq.r], [PSr[7]])
                yield
                self.mm(p6f, cT[s].ap[:, 2, :], wukv.ap, True, True, [cT[s].r, wukv.r], [PSr[6]])
                qd = p7f[:, 0:384].rearrange("p (h d) -> p h d", h=4)
                kvd = p6f.rearrange("p (h d) -> p h d", h=4)
                yield
                self.cp("act", qf[s].ap[:, :, 0:64], qd[:, :, 0:64], [PSr[7]], [qf[s].r])
                qd_sb = sqd[s].ap[:, 0:384].rearrange("p (h d) -> p h d", h=4)
                yield
                self.cp("act", qd_sb[:, :, 64:96], qd[:, :, 64:96], [PSr[7]], [sqd[s].r])
                qx = qd_sb[:, :, 64:96].rearrange("p h (r f e) -> p h r f e", r=2, f=2)
                qo = qf[s].ap[:, :, 64:96].rearrange("p h (r f e) -> p h r f e", r=2, f=2)
                cosD4 = cosD.unsqueeze(1).broadcast_to([128, 4, 2, 8])
                sinD4 = sinD.unsqueeze(1).broadcast_to([128, 4, 2, 8])
                ta3 = rad[s].ap[:, 0:64].rearrange("p (h r e) -> p h r e", h=4, r=2)
                tb3 = rbd[s].ap[:, 0:64].rearrange("p (h r e) -> p h r e", h=4, r=2)
                yield
                rope(qx[:, :, :, 0, :], qx[:, :, :, 1, :], cosD4, sinD4, qo[:, :, :, 0, :], qo[:, :, :, 1, :], ta3, tb3,
                     [sqd[s].r, tbl[s].r], rad[s].r, rbd[s].r, qf[s].r)
                yield
                self.cp("act", kf[s].ap[:, :, 0:64], kvd[:, :, 0:64], [PSr[6]], [kf[s].r])
                yield
                self.cp("pool", kf[s].ap[:, :, 64:96], krr[s].ap.unsqueeze(1).broadcast_to([128, 4, 32]), [krr[s].r], [kf[s].r])
                yield
                self.cp("act", vd[s].ap, kvd[:, :, 64:128], [PSr[6]], [vd[s].r])
                yield
                self.dma(self.Vd_d[tsl, :, :], vd[s].ap, f"vd{s}", [vd[s].r], [self.r_Vd])
                yield
                for h in range(4):
                    self.tr(p6[0:96, h * 128:(h + 1) * 128], qf[s].ap[:, h, :], ident, [qf[s].r, rid], [PSr[6]])
                yield
                for h in range(4):
                    self.tr(p6[0:96, (4 + h) * 128:(5 + h) * 128], kf[s].ap[:, h, :], ident, [kf[s].r, rid], [PSr[6]])
                yield
                self.cp("act", qkfT[s].ap, p6[0:96, :].rearrange("p (h t) -> p h t", h=8), [PSr[6]], [qkfT[s].r])
                yield
                self.dma(self.QfT_d[:, :, tsl].rearrange("h d t -> d h t"), qkfT[s].ap[:, 0:4, :], f"qfT{s}", [qkfT[s].r], [self.r_QfT])
                yield
                self.dma(self.KfT_d[:, :, tsl].rearrange("h d t -> d h t"), qkfT[s].ap[:, 4:8, :], f"kfT{s}", [qkfT[s].r], [self.r_KfT])
                yield
            return pro(), ([chainA(), chainB()] if AB else [chainC(), chainD()])

        tiles = [make_tile(i) for i in range(NT)]
        for _ in tiles[0][0]:
            pass
        for i in range(NT):
            gens = list(tiles[i][1])
            if i + 1 < NT:
                gens.append(tiles[i + 1][0])
            while gens:
                for gch in list(gens):
                    try:
                        next(gch)
                    except StopIteration:
                        gens.remove(gch)

    def phase2_attn(self, l):
        SEQ, NT, NQB = self.SEQ, self.NT, self.NQB
        self.arena_reset()
        PSr = self.PSr
        wbufs = [(self.alloc([128, 1024], F32, "wcf"), self.alloc([128, 1024], BF16, "wcb")) for _ in range(2)]
        kt_sb = [self.alloc([128, SEQ], BF16, "kt") for _ in range(2)]
        for kb_ in kt_sb:
            self.memset("pool", kb_.ap[64:128, :], 0.0, [kb_.r])
        v_sb = [self.alloc([128, NT, 128], BF16, "v") for _ in range(2)]
        for vb in v_sb:
            self.memset("pool", vb.ap[:, :, 64:128], 1.0, [vb.r])
        q_sb = [self.alloc([128, 512], BF16, "q") for _ in range(2)]
        for qb_ in q_sb:
            self.memset("pool", qb_.ap[64:128, :], 0.0, [qb_.r])
        pT = [self.alloc([128, 2, 512], BF16, "pT") for _ in range(3)]
        rc = [self.alloc([128, 512], F32, "rc") for _ in range(2)]
        o_sb = [self.alloc([128, 512], BF16, "o") for _ in range(2)]
        passes = []
        for kvh in range(2):
            heads = []
            for g in range(2):
                hq = kvh * 2 + g
                heads.append((self.QT_d[hq], self.r_QT, 2 + hq // 2, (hq % 2) * 64))
            passes.append((self.KT_d[kvh], self.r_KT, self.V_d[:, kvh, :], self.r_V, 64, 64 ** -0.5, heads))
        for h in range(4):
            passes.append((self.KfT_d[h], self.r_KfT, self.Vd_d[:, h, :], self.r_Vd, 96, 96 ** -0.5,
                           [(self.QfT_d[h], self.r_QfT, 6 + h // 2, (h % 2) * 64)]))
        n_iter = sum(len(p[6]) for p in passes) * NQB
        per_it = (len(self.wcast_queue) + n_iter - 1) // n_iter
        groups = []
        it = 0
        for pi, (Ksrc, rK, Vsrc, rV, dk, scale, heads) in enumerate(passes):
            for qb in range(NQB):
                for hi, (Qsrc, rQ, chunk, base) in enumerate(heads):
                    for kg in range(NT // 2):
                        groups.append(dict(pi=pi, qb=qb, hi=hi, kg=kg, it=it, first=(kg == 0), last=(kg == NT // 2 - 1),
                                           pass_first=(qb == 0 and hi == 0 and kg == 0),
                                           prefetch=(qb == 0 and hi == 0 and kg == min(6, NT // 2 - 1))))
                    it += 1
        LOOK = 2
        NG = len(groups)

        def load_kv(pi):
            Ksrc, rK, Vsrc, rV, dk, scale, heads = passes[pi]
            kb = kt_sb[pi % 2]; vb = v_sb[pi % 2]
            self.dma(kb.ap[0:dk, :], Ksrc, f"ktld{pi % 2}", [rK], [kb.r])
            Vv = Vsrc.rearrange("(t p) d -> p t d", p=128)
            for t0 in range(0, NT, 8):
                self.dma(vb.ap[:, t0:t0 + 8, 0:64], Vv[:, t0:t0 + 8, :], f"vld{pi % 2}", [rV], [vb.r])

        def emit_qk(gi):
            g = groups[gi]
            Ksrc, rK, Vsrc, rV, dk, scale, heads = passes[g["pi"]]
            Qsrc, rQ, chunk, base = heads[g["hi"]]
            kb = kt_sb[g["pi"] % 2]; vb = v_sb[g["pi"] % 2]
            if g["pass_first"] and g["pi"] == 0:
                load_kv(0)
            if g["prefetch"] and g["pi"] + 1 < len(passes):
                load_kv(g["pi"] + 1)
            qs = q_sb[g["it"] % 2]
            if g["first"]:
                qsl = slice(g["qb"] * 512, (g["qb"] + 1) * 512)
                self.dma(qs.ap[0:dk, :], Qsrc[:, qsl], f"qld{g['it'] % 2}", [rQ], [qs.r])
                self.wcast_step(wbufs, per_it)
            r3 = gi % 3
            for u in range(2):
                ktile = g["kg"] * 2 + u
                self.mm(self.PS[:, 2 * r3 + u, :], kb.ap[:, ktile * 128:(ktile + 1) * 128], qs.ap[:, :],
                        True, True, [kb.r, qs.r], [PSr[2 * r3 + u]])
            self.actf(pT[r3].ap, self.PS[:, 2 * r3:2 * r3 + 2, :], AF.Exp, [PSr[2 * r3], PSr[2 * r3 + 1]],
                      [pT[r3].r], scale=float(scale))

        def emit_pv(gi):
            g = groups[gi]
            Ksrc, rK, Vsrc, rV, dk, scale, heads = passes[g["pi"]]
            Qsrc, rQ, chunk, base = heads[g["hi"]]
            vb = v_sb[g["pi"] % 2]
            r3 = gi % 3
            acc_b = 6 + g["it"] % 2
            acc = self.psb(acc_b)
            for u in range(2):
                ktile = g["kg"] * 2 + u
                self.mm(acc, vb.ap[:, ktile, :], pT[r3].ap[:, u, :], ktile == 0, ktile == NT - 1,
                        [vb.r, pT[r3].r], [PSr[acc_b]])
            if g["last"]:
                qsl = slice(g["qb"] * 512, (g["qb"] + 1) * 512)
                rcb = rc[g["it"] % 2]; ob = o_sb[g["it"] % 2]
                self.S.op("dve", (lambda o, i_: (lambda e: e.reciprocal(out=o, in_=i_)))(rcb.ap[64:128, :], acc[64:128, :]),
                          [PSr[acc_b]], [rcb.r])
                self.tt("dve", ob.ap[base:base + 64, :], acc[0:64, :], rcb.ap[64:128, :], ALU.mult,
                        [PSr[acc_b], rcb.r], [ob.r])
                self.dma(self.OT_d[chunk, base:base + 64, qsl], ob.ap[base:base + 64, :], f"ost{g['it'] % 2}", [ob.r],
                         [self.r_OT[chunk]])

        for gi in range(NG + LOOK):
            if gi < NG:
                emit_qk(gi)
            if gi >= LOOK:
                emit_pv(gi - LOOK)
        while self.wcast_queue:
            self.wcast_step(wbufs, 1)

    def phase2c(self, l):
        I, S, NT, NQB = self.I, self.S, self.NT, self.NQB
        self.arena_reset()
        PSr = self.PSr
        ident = self.ident[:]; rid = self.r_ident
        woutb = self.alloc([128, 8, D], BF16, "woutb")
        for k in range(8):
            self.dma(woutb.ap[:, k, :], I["w_out"][l, k * 128:(k + 1) * 128, :], f"woutb{k % 2}", (), [woutb.r], q="pool")
        g1 = self.alloc([128, D], F32, "g1"); b1 = self.alloc([128, D], F32, "b1")
        self.bcast_load(g1, I["ln_mix_g"][l], D, "g1"); self.bcast_load(b1, I["ln_mix_b"][l], D, "b1")
        cab = self.alloc([128, 256], F32, "cab")
        self.bcast_load(cab, I["conv_a_b"][l], 256, "cab")
        cw31 = self.alloc([31, 256], F32, "cw31")
        self.dma(cw31.ap, I["conv_a_w"][l], "cw31", (), [cw31.r])
        cw = self.alloc([128, 2, 32], F32, "cw")
        lgb = self.alloc([4, 256], F32, "lgb")
        self.dma(lgb.ap[0:2, 0:128], I["ln_a_g"][l].rearrange("(c p) -> c p", p=128), "lga", (), [lgb.r])
        self.dma(lgb.ap[0:2, 128:256], I["ln_a_b"][l].rearrange("(c p) -> c p", p=128), "lgbb", (), [lgb.r])
        lgT = self.alloc([128, 4], F32, "lgT")
        pf = self.psb(7)
        identf = self.identf[:]
        for c in range(2):
            S.op("pe", (lambda o, i_: (lambda e: e.transpose(out=o, in_=i_, identity=identf[0:31, 0:31])))(
                pf[:, c * 32:c * 32 + 31], cw31.ap[:, c * 128:(c + 1) * 128]), [cw31.r, rid], [PSr[7]])
        S.op("pe", (lambda o, i_: (lambda e: e.transpose(out=o, in_=i_, identity=identf[0:2, 0:2])))(
            pf[:, 64:66], lgb.ap[0:2, 0:128]), [lgb.r, rid], [PSr[7]])
        S.op("pe", (lambda o, i_: (lambda e: e.transpose(out=o, in_=i_, identity=identf[0:2, 0:2])))(
            pf[:, 66:68], lgb.ap[0:2, 128:256]), [lgb.r, rid], [PSr[7]])
        self.cp("dve", cw.ap, pf[:, 0:64].rearrange("p (c j) -> p c j", c=2), [PSr[7]], [cw.r])
        self.cp("dve", lgT.ap, pf[:, 64:68], [PSr[7]], [lgT.r])
        dg = self.alloc([128, 2 * 31 * 128], BF16, "dg")
        dgv = dg.ap.rearrange("p (c j q) -> p c j q", c=2, j=31)
        for c in range(2):
            self.tt("dve", dgv[:, c, :, :], self.identf[:].unsqueeze(1).broadcast_to([128, 31, 128]),
                    cw.ap[:, c, 0:31].unsqueeze(2).broadcast_to([128, 31, 128]), ALU.mult, [cw.r, rid], [dg.r])
        NB = 2
        aw = [self.alloc([128, 2, 542], BF16, "aw") for _ in range(NB)]
        ob = [self.alloc([128, 6, 512], BF16, "ob") for _ in range(NB)]
        xa = [self.alloc([128, 256], F32, "xa") for _ in range(NB)]
        xab = [self.alloc([128, 256], BF16, "xab") for _ in range(NB)]
        oaT = [self.alloc([128, 2, 128], BF16, "oaT") for _ in range(NB)]
        st = [self.alloc([128, 32], F32, "st") for _ in range(NB)]
        def make_tile(i):
            qb, m = divmod(i, 4)
            s = qb % NB
            u = i % NB
            qsl = slice(qb * 512, (qb + 1) * 512)
            cb = 4 + 2 * (i % 2)
            tbk = 5 + 2 * (i % 2)
            pb0 = 2 * (i % 2)
            Hi = self.H[:, i, :]
            stq = st[u].ap

            def front():
                if m == 0:
                    self.dma(aw[s].ap, self.aT_d[:, :, 1 + qb * 512: 1 + qb * 512 + 542].rearrange("c p t -> p c t"), f"aw{s}",
                             [self.r_aT], [aw[s].r])
                    self.dma(ob[s].ap, self.OT_d[2:8, :, qsl].rearrange("c p t -> p c t"), f"ob{s}",
                             self.r_OT[2:8], [ob[s].r])
                    yield
                p4 = self.psb(cb)
                for c in range(2):
                    for j in range(31):
                        self.mm(p4[:, c * 128:(c + 1) * 128], aw[s].ap[:, c, m * 128 + j: m * 128 + j + 128], dgv[:, c, j, :],
                                j == 0, j == 30, [aw[s].r, dg.r], [PSr[cb]])
                        if j % 8 == 7:
                            yield
                yield
                self.tt("dve", xa[u].ap, p4[:, 0:256], cab.ap, ALU.add, [PSr[cb], cab.r], [xa[u].r])
                yield
                S.op("dve", (lambda o, i_: (lambda e: e.bn_stats(out=o, in_=i_)))(stq[:, 16:22], xa[u].ap), [xa[u].r], [st[u].r])
                S.op("dve", (lambda o, i_: (lambda e: e.bn_aggr(out=o, in_=i_)))(stq[:, 22:24], stq[:, 16:22]), [st[u].r], [st[u].r])
                yield
                self.rstd_pool(stq[:, 24:25], stq[:, 23:24], 1, LN_EPS, [st[u].r], [st[u].r])
                yield
                self.stt("dve", stq[:, 25:26], stq[:, 22:23], -1.0, stq[:, 24:25], ALU.mult, ALU.mult, [st[u].r], [st[u].r])
                yield
                self.actf(xab[u].ap, xa[u].ap, AF.Identity, [xa[u].r, st[u].r], [xab[u].r], bias=stq[:, 25:26], scale=stq[:, 24:25])
                yield
                p5 = self.psb_bf(tbk)
                for c in range(2):
                    self.tr(p5[:, c * 128:(c + 1) * 128], xab[u].ap[:, c * 128:(c + 1) * 128], ident, [xab[u].r, rid], [PSr[tbk]])
                yield
                for c in range(2):
                    self.actf(oaT[u].ap[:, c, :], p5[:, c * 128:(c + 1) * 128], AF.Silu, [PSr[tbk], lgT.r], [oaT[u].r],
                              bias=lgT.ap[:, 2 + c:3 + c], scale=lgT.ap[:, c:c + 1])
                yield

            def back():
                for half in range(2):
                    for k in range(8):
                        lhsT = oaT[u].ap[:, k, :] if k < 2 else ob[s].ap[:, k - 2, m * 128:(m + 1) * 128]
                        rr = [oaT[u].r] if k < 2 else [ob[s].r]
                        self.mm(self.PS[:, pb0 + half, :], lhsT, woutb.ap[:, k, half * 512:(half + 1) * 512], k == 0, k == 7,
                                rr + [woutb.r], [PSr[pb0 + half]])
                    yield
                for half in range(2):
                    hs = slice(half * 512, (half + 1) * 512)
                    self.stt("dve", Hi[:, hs], Hi[:, hs], ALPHA, self.PS[:, pb0 + half, :], ALU.mult, ALU.add,
                             [self.Hr[i], PSr[pb0 + half]], [self.Hr[i]])
                    yield
                r = st[u].r
                S.op("dve", (lambda o, i_: (lambda e: e.bn_stats(out=o, in_=i_)))(stq[:, 0:6], Hi[:, 0:512]), [self.Hr[i]], [r])
                S.op("dve", (lambda o, i_: (lambda e: e.bn_stats(out=o, in_=i_)))(stq[:, 6:12], Hi[:, 512:1024]), [self.Hr[i]], [r])
                S.op("dve", (lambda o, i_: (lambda e: e.bn_aggr(out=o, in_=i_)))(stq[:, 12:14], stq[:, 0:12]), [r], [r])
                yield
                self.rstd_pool(stq[:, 14:15], stq[:, 13:14], 1, LN_EPS, [r], [r])
                yield
                self.stt("dve", stq[:, 15:16], stq[:, 12:13], -1.0, stq[:, 14:15], ALU.mult, ALU.mult, [r], [r])
                yield
                self.actf(Hi, Hi, AF.Identity, [self.Hr[i], r], [self.Hr[i]], bias=stq[:, 15:16], scale=stq[:, 14:15])
                yield
                self.tt("dve", Hi, Hi, g1.ap, ALU.mult, [self.Hr[i], g1.r], [self.Hr[i]])
                yield
                self.tt("pool" if i % 2 else "dve", Hi, Hi, b1.ap, ALU.add, [self.Hr[i], b1.r], [self.Hr[i]])
                yield
            return front(), back()

        tiles = [make_tile(i) for i in range(NT)]
        for _ in tiles[0][0]:
            pass
        for i in range(NT):
            gens = [tiles[i][1]]
            if i + 1 < NT:
                gens.append(tiles[i + 1][0])
            while gens:
                for gch in list(gens):
                    try:
                        next(gch)
                    except StopIteration:
                        gens.remove(gch)

    def phase3(self, l):
        I, S, NT, NQB, SEQ = self.I, self.S, self.NT, self.NQB, self.SEQ
        self.arena_reset()
        PSr = self.PSr
        ident = self.ident[:]; rid = self.r_ident
        identf = self.identf[:]
        g2 = self.alloc([128, D], F32, "g2"); b2 = self.alloc([128, D], F32, "b2")
        self.bcast_load(g2, I["ln_ffn_g"][l], D, "g2"); self.bcast_load(b2, I["ln_ffn_b"][l], D, "b2")
        cin = self.alloc([44, 4, 128], F32, "cin")
        self.dma(cin.ap[:, 0:3, :], I["ffn_conv_w"][l].rearrange("j (c p) -> c j p", p=128), "cin_w", (), [cin.r])
        self.dma(cin.ap[:, 3, :], I["ffn_conv_b"][l].rearrange("(c p) -> c p", p=128), "cin_b", (), [cin.r])
        cwf = self.alloc([128, 4, 44], F32, "cwf")
        pf = self.psb(7)
        for t in range(4):
            S.op("pe", (lambda o, i_: (lambda e: e.transpose(out=o, in_=i_, identity=identf[0:44, 0:44])))(
                pf[:, t * 44:(t + 1) * 44], cin.ap[:, t, :]), [cin.r, rid], [PSr[7]])
        self.cp("dve", cwf.ap, pf[:, 0:176].rearrange("p (t c) -> p t c", t=4), [PSr[7]], [cwf.r])
        hT = self.alloc([128, 8, 513], BF16, "hT")
        one1 = self.alloc([1, 1], BF16, "one1")
        self.memset("dve", one1.ap, 1.0, [one1.r])
        G = self.alloc([128, NJ, 512], BF16, "G")
        G_r = [Res(f"G{j}") for j in range(NJ)]
        wup = [self.alloc([128, 8, 256], BF16, "wup") for _ in range(3)]
        wdn = [self.alloc([128, D], BF16, "wdn") for _ in range(3)]
        U = [[self.alloc([128, 514], F32, "U") for _ in range(2)] for _ in range(2)]
        Y = [[self.alloc([128, 512], F32, "Y") for _ in range(2)] for _ in range(2)]
        hb = Buf(Y[1][1].ap.bitcast(BF16), Y[1][1].r)
        hrow = Buf(Y[1][0].ap.bitcast(BF16)[0:1, 0:D], Y[1][0].r)
        carry = self.alloc([128, 44, 2], F32, "carry")
        self.memset("dve", carry.ap, 0.0, [carry.r])
        st = [self.alloc([128, 32], F32, "st") for _ in range(4)]
        cnt = {"wi": 0, "di": 0}

        def htgen(c):
            for m in range(4):
                i = c * 4 + m
                self.cp("act", hb.ap, self.H[:, i, :], [self.Hr[i]], [hb.r])
                bk = 2 + (m % 2)
                pb = self.psb_bf(bk)
                for k in range(8):
                    self.tr(pb[:, k * 128:(k + 1) * 128], hb.ap[:, k * 128:(k + 1) * 128], ident, [hb.r, rid], [PSr[bk]])
                self.cp("dve", hT.ap[:, :, m * 128:(m + 1) * 128], pb.rearrange("p (k t) -> p k t", k=8), [PSr[bk]], [hT.r])
            if c < NQB - 1:
                i = c * 4 + 4
                self.cp("act", hrow.ap, self.H[0:1, i, :], [self.Hr[i]], [hrow.r])
                p2 = self.psb(2)
                for k in range(8):
                    self.mm(p2[:, k:k + 1], hrow.ap[0:1, k * 128:(k + 1) * 128], one1.ap, True, True, [hrow.r, one1.r], [PSr[2]])
                self.cp("act", hT.ap[:, :, 512:513], p2[:, 0:8].unsqueeze(2), [PSr[2]], [hT.r])
            else:
                self.memset("dve", hT.ap[:, :, 512:513], 0.0, [hT.r])

        def up(c):
            for j in range(NJ):
                wi = cnt["wi"]; cnt["wi"] += 1
                wb = wup[wi % 3]
                self.dma(wb.ap, self.wup_d[l][j], f"wupld{wi % 3}", [self.r_wup[l][j]], [wb.r])
                par = j % 2
                for gv in range(2):
                    jj = gv * NJ + j
                    bank = 2 * par + gv
                    pbk = self.psb(bank)
                    for k in range(8):
                        self.mm(pbk, wb.ap[:, k, gv * 128:(gv + 1) * 128], hT.ap[:, k, 1:513], k == 0, k == 7,
                                [wb.r, hT.r], [PSr[bank]])
                    Ub = U[par][gv]; Yb = Y[par][gv]
                    if c == 0:
                        p7 = self.psb(7)
                        for k in range(8):
                            self.mm(p7[:, jj:jj + 1], wb.ap[:, k, gv * 128:(gv + 1) * 128], hT.ap[:, k, 0:1], k == 0, k == 7,
                                    [wb.r, hT.r], [PSr[7]])
                        self.memset("pool", Ub.ap[:, 0:1], 0.0, [Ub.r])
                        self.cp("act", Ub.ap[:, 1:2], p7[:, jj:jj + 1], [PSr[7]], [Ub.r])
                    else:
                        self.cp("pool", Ub.ap[:, 0:2], carry.ap[:, jj, :], [carry.r], [Ub.r])
                    self.cp("act", Ub.ap[:, 2:514], pbk, [PSr[bank]], [Ub.r])
                    self.cp("act", carry.ap[:, jj, :], pbk[:, 510:512], [PSr[bank]], [carry.r])
                    self.ts("pool", Yb.ap, Ub.ap[:, 1:513], cwf.ap[:, 1, jj:jj + 1], cwf.ap[:, 3, jj:jj + 1], ALU.mult, ALU.add,
                            [Ub.r, cwf.r], [Yb.r])
                    self.stt("dve", Yb.ap, Ub.ap[:, 0:512], cwf.ap[:, 0, jj:jj + 1], Yb.ap, ALU.mult, ALU.add,
                             [Ub.r, cwf.r, Yb.r], [Yb.r])
                    self.stt("dve", Yb.ap, Ub.ap[:, 2:514], cwf.ap[:, 2, jj:jj + 1], Yb.ap, ALU.mult, ALU.add,
                             [Ub.r, cwf.r, Yb.r], [Yb.r])
                Yg = Y[par][0]; Yv = Y[par][1]
                self.actf(Yg.ap, Yg.ap, AF.Silu, [Yg.r], [Yg.r])
                self.tt("dve", G.ap[:, j, :], Yg.ap, Yv.ap, ALU.mult, [Yg.r, Yv.r], [G_r[j]])

        def down(c):
            for k in range(NJ):
                di = cnt["di"]; cnt["di"] += 1
                wd = wdn[di % 3]
                self.dma(wd.ap, self.wdn_d[l][k], f"wdnld{di % 3}", [self.r_wdn[l][k]], [wd.r])
                for m in range(4):
                    for half in range(2):
                        bank = (4 + 2 * m + half) % 8
                        self.mm(self.psb(bank), G.ap[:, k, m * 128:(m + 1) * 128], wd.ap[:, half * 512:(half + 1) * 512],
                                k == 0, k == NJ - 1, [G_r[k], wd.r], [PSr[bank]])

        def epi(c, tiles):
            ln_items = []
            for m in tiles:
                i = c * 4 + m
                Hi = self.H[:, i, :]
                for half in range(2):
                    bank = (4 + 2 * m + half) % 8
                    hs = slice(half * 512, (half + 1) * 512)
                    self.stt("dve", Hi[:, hs], Hi[:, hs], ALPHA, self.psb(bank), ALU.mult, ALU.add,
                             [self.Hr[i], PSr[bank]], [self.Hr[i]])
                ln_items.append((Hi, self.Hr[i], st[m]))
            self.layernorm_tiles(ln_items, g2, b2)

        htgen(0)
        for c in range(NQB):
            up(c)
            down(c)
            epi(c, [2, 3])
            if c + 1 < NQB:
                htgen(c + 1)
            epi(c, [0, 1])

    def final_store(self):
        for i in range(self.NT):
            self.dma(self.out[i * 128:(i + 1) * 128, :], self.H[:, i, :], f"ost{i % 3}", [self.Hr[i]], ())


@contextlib.contextmanager
def nc_allow(nc):
    with nc.allow_non_contiguous_dma(reason="tiny strided parameter load"):
        yield


def rope_table(SEQ):
    t = np.arange(SEQ)
    row = (t // GRID_W).astype(np.float64)
    col = (t % GRID_W).astype(np.float64)
    invB = ROPE_THETA ** (-np.arange(16, dtype=np.float64) / 16)
    invD = ROPE_THETA ** (-np.arange(8, dtype=np.float64) / 8)
    f32 = np.float32
    angB_r = (row.astype(f32)[:, None] * invB.astype(f32)[None, :]).astype(f32)
    angB_c = (col.astype(f32)[:, None] * invB.astype(f32)[None, :]).astype(f32)
    angD_r = (row.astype(f32)[:, None] * invD.astype(f32)[None, :]).astype(f32)
    angD_c = (col.astype(f32)[:, None] * invD.astype(f32)[None, :]).astype(f32)
    tbl = np.concatenate([
        np.cos(angB_r.astype(np.float64)), np.cos(angB_c.astype(np.float64)),
        np.sin(angB_r.astype(np.float64)), np.sin(angB_c.astype(np.float64)),
        np.cos(angD_r.astype(np.float64)), np.cos(angD_c.astype(np.float64)),
        np.sin(angD_r.astype(np.float64)), np.sin(angD_c.astype(np.float64)),
    ], axis=1).astype(np.float32)
    return np.ascontiguousarray(tbl)


_CACHE = {}


def get_program(SEQ, debug=False, stop_after=None):
    key = (SEQ, debug, stop_after)
    if key not in _CACHE:
        b = Builder(SEQ, debug=debug, stop_after=stop_after)
        nc, info = b.build()
        _CACHE[key] = (nc, info)
    return _CACHE[key]


def kernel(**inputs):
    x = np.asarray(inputs["x"], dtype=np.float32)
    B, SEQ, _ = x.shape
    nc, info = get_program(SEQ)
    tbl = rope_table(SEQ)
    shared = {k: np.ascontiguousarray(np.asarray(v, dtype=np.float32)) for k, v in inputs.items() if k != "x"}
    shared["rope_tbl"] = tbl
    in_maps = []
    for b in range(B):
        m = dict(shared)
        m["x"] = np.ascontiguousarray(x[b])
        in_maps.append(m)
    res = run_bass_kernel_spmd(nc, in_maps, core_ids=list(range(B)))
    out = np.stack([np.asarray(r["out"], dtype=np.float32) for r in res.results], axis=0)
    return out
```
